# Optimizing a Trainium2 kernel written in Bass

```python
import jax, jax.numpy as jnp
from jax import lax
import numpy as np

D_MODEL = 1024
BATCH = 32
SEQ = 2048
DEPTH = 1
DEC_BATCH = 128
DEC_SEQ = 4
PAST_LEN = 16384
PAGE_SIZE = 128

N_META = 16
N_HEADS = 8
QK_NOPE_DIM = 64
QK_ROPE_DIM = 32
V_HEAD_DIM = 64
Q_RANK = 256
KV_RANK = 256
CONV_CH = 512
CONV_WIDTH = 31
D_FF = 2816
ROPE_THETA = 10000.0
EPS = 1e-6
Q_BLOCK = 128
ATTN_WIDTH = N_HEADS * V_HEAD_DIM
MIX_WIDTH = ATTN_WIDTH + CONV_CH
Q_HEAD_DIM = QK_NOPE_DIM + QK_ROPE_DIM
IN_COLS = Q_RANK + KV_RANK + QK_ROPE_DIM + 2 * CONV_CH
SOFTMAX_SCALE = Q_HEAD_DIM ** -0.5

kernel_name = "hybrid_mla_conformer_conv_step"


def rmsnorm(x, g):
    xf = x.astype(jnp.float32)
    y = xf * lax.rsqrt(jnp.mean(xf * xf, axis=-1, keepdims=True) + EPS)
    return (y * g.astype(jnp.float32)).astype(x.dtype)


def layernorm(x, g, b):
    xf = x.astype(jnp.float32)
    mu = jnp.mean(xf, axis=-1, keepdims=True)
    var = jnp.mean(jnp.square(xf - mu), axis=-1, keepdims=True)
    y = (xf - mu) * lax.rsqrt(var + EPS)
    return (y * g.astype(jnp.float32) + b.astype(jnp.float32)).astype(x.dtype)


def rope(x, pos):
    half = QK_ROPE_DIM // 2
    inv = ROPE_THETA ** (-jnp.arange(half, dtype=jnp.float32) / half)
    ang = pos.astype(jnp.float32)[:, None] * inv[None, :]
    cos = jnp.cos(ang)[None, :, None, :]
    sin = jnp.sin(ang)[None, :, None, :]
    xf = x.astype(jnp.float32)
    x1, x2 = xf[..., :half], xf[..., half:]
    return jnp.concatenate([x1 * cos - x2 * sin, x1 * sin + x2 * cos], axis=-1).astype(x.dtype)


def swiglu(x, w_gate, w_up, w_down):
    return (jax.nn.silu(x @ w_gate) * (x @ w_up)) @ w_down


def project_mixers(n, pos, w_in, q_norm, w_uq, kv_norm, w_uk):
    b, s, _ = n.shape
    z = n @ w_in
    q_c = z[..., :Q_RANK]
    kv_c = z[..., Q_RANK:Q_RANK + KV_RANK]
    k_r = z[..., Q_RANK + KV_RANK:Q_RANK + KV_RANK + QK_ROPE_DIM]
    conv_in = z[..., Q_RANK + KV_RANK + QK_ROPE_DIM:]
    q = (rmsnorm(q_c, q_norm) @ w_uq).reshape(b, s, N_HEADS, Q_HEAD_DIM)
    q_nope, q_rope = q[..., :QK_NOPE_DIM], rope(q[..., QK_NOPE_DIM:], pos)
    q_lat = jnp.einsum('bshd,rhd->bshr', q_nope, w_uk)
    c_kv = rmsnorm(kv_c, kv_norm)
    k_rope = rope(k_r[:, :, None, :], pos)[:, :, 0, :]
    glu = conv_in[..., :CONV_CH] * jax.nn.sigmoid(conv_in[..., CONV_CH:])
    return q_lat, q_rope, c_kv, k_rope, glu


def mla_scores(q_lat, q_rope, c, kr):
    s = jnp.einsum('bshr,btr->bhst', q_lat, c) + jnp.einsum('bshd,btd->bhst', q_rope, kr)
    return s.astype(jnp.float32) * SOFTMAX_SCALE


def prompt_attention(q_lat, q_rope, c, kr, w_uv):
    b, t = c.shape[0], c.shape[1]
    n_blk = -(-t // Q_BLOCK)
    t_pad = n_blk * Q_BLOCK
    pad = ((0, 0), (0, t_pad - t), (0, 0), (0, 0))
    qb = jnp.pad(q_lat, pad).reshape(b, n_blk, Q_BLOCK, N_HEADS, KV_RANK).swapaxes(0, 1)
    rb = jnp.pad(q_rope, pad).reshape(b, n_blk, Q_BLOCK, N_HEADS, QK_ROPE_DIM).swapaxes(0, 1)
    qpos = jnp.arange(t_pad, dtype=jnp.int32).reshape(n_blk, Q_BLOCK)
    kpos = jnp.arange(t, dtype=jnp.int32)

    def one_block(args):
        ql, qr, qp = args
        s = mla_scores(ql, qr, c, kr)
        s = jnp.where(kpos[None, :] <= qp[:, None], s, -jnp.inf)
        p = jax.nn.softmax(s, axis=-1).astype(c.dtype)
        o = jnp.einsum('bhst,btr->bshr', p, c)
        return jnp.einsum('bshr,rhv->bshv', o, w_uv).reshape(b, Q_BLOCK, ATTN_WIDTH)

    out = lax.map(one_block, (qb, rb, qpos))
    return out.swapaxes(0, 1).reshape(b, t_pad, ATTN_WIDTH)[:, :t]


def sample_attention(q_lat, q_rope, c_new, kr_new, c_past, kr_past, w_uv):
    b, s = c_new.shape[0], c_new.shape[1]
    past_len = c_past.shape[1]
    s_past = mla_scores(q_lat, q_rope, c_past, kr_past)
    s_new = mla_scores(q_lat, q_rope, c_new, kr_new)
    s_new = jnp.where(jnp.tril(jnp.ones((s, s), dtype=bool)), s_new, -jnp.inf)
    p = jax.nn.softmax(jnp.concatenate([s_past, s_new], axis=-1), axis=-1).astype(c_new.dtype)
    o = (jnp.einsum('bhst,btr->bshr', p[..., :past_len], c_past)
         + jnp.einsum('bhst,btr->bshr', p[..., past_len:], c_new))
    return jnp.einsum('bshr,rhv->bshv', o, w_uv).reshape(b, s, ATTN_WIDTH)


def conformer_conv(glu, prev, conv_w, conv_b, ln_g, ln_b):
    full = jnp.concatenate([prev, glu], axis=1)
    y = lax.conv_general_dilated(full, conv_w[:, None, :], window_strides=(1,), padding='VALID',
                                 dimension_numbers=('NWC', 'WIO', 'NWC'),
                                 feature_group_count=CONV_CH) + conv_b
    y = jax.nn.silu(layernorm(y, ln_g, ln_b))
    return y, full[:, -(CONV_WIDTH - 1):]


def layer_forward(x, pos, conv_prev, attn_fn,
                  ffn1_norm, ffn1_w_gate, ffn1_w_up, ffn1_w_down,
                  mix_norm, w_in, q_norm, w_uq, kv_norm, w_uk, w_uv,
                  conv_w, conv_b, conv_ln_g, conv_ln_b,
                  attn_grp_norm, conv_grp_norm, w_out,
                  ffn2_norm, ffn2_w_gate, ffn2_w_up, ffn2_w_down):
    h = x + 0.5 * swiglu(rmsnorm(x, ffn1_norm), ffn1_w_gate, ffn1_w_up, ffn1_w_down)
    n = rmsnorm(h, mix_norm)
    q_lat, q_rope, c_kv, k_rope, glu = project_mixers(n, pos, w_in, q_norm, w_uq, kv_norm, w_uk)
    a = attn_fn(q_lat, q_rope, c_kv, k_rope, w_uv)
    cv, conv_state = conformer_conv(glu, conv_prev, conv_w, conv_b, conv_ln_g, conv_ln_b)
    mix = jnp.concatenate([rmsnorm(a, attn_grp_norm), rmsnorm(cv, conv_grp_norm)], axis=-1) @ w_out
    h = h + mix
    h = h + 0.5 * swiglu(rmsnorm(h, ffn2_norm), ffn2_w_gate, ffn2_w_up, ffn2_w_down)
    return h, c_kv, k_rope, conv_state


def setup_inputs(seed: int = 0) -> dict:
    key = jax.random.key(seed)
    ks = iter(jax.random.split(key, 48))
    f32 = jnp.float32

    def nrm(shape, scale):
        return jax.random.normal(next(ks), shape, f32) * scale

    def gain(shape):
        return 1.0 + 0.02 * jax.random.normal(next(ks), shape, f32)

    n_pages = PAST_LEN // PAGE_SIZE
    n_used = DEC_BATCH * n_pages
    n_pool = (n_used * 5) // 4
    L = DEPTH
    d = {}
    d['x_prompt'] = nrm((BATCH, SEQ, D_MODEL), 1.0)
    d['x_sample'] = nrm((DEC_BATCH, DEC_SEQ, D_MODEL), 1.0)
    d['cache_kv_latent'] = nrm((L, n_pool, PAGE_SIZE, KV_RANK), 1.0)
    d['cache_k_rope'] = nrm((L, n_pool, PAGE_SIZE, QK_ROPE_DIM), 1.0)
    d['state_conv'] = nrm((L, DEC_BATCH, CONV_WIDTH - 1, CONV_CH), 0.5)
    perm = jax.random.permutation(next(ks), n_pool)
    d['page_table'] = perm[:n_used].reshape(DEC_BATCH, n_pages).astype(jnp.int32)
    d['meta_tokens'] = nrm((N_META, D_MODEL), 1.0)
    d['ffn1_norm'] = gain((L, D_MODEL))
    d['ffn1_w_gate'] = nrm((L, D_MODEL, D_FF), D_MODEL ** -0.5)
    d['ffn1_w_up'] = nrm((L, D_MODEL, D_FF), D_MODEL ** -0.5)
    d['ffn1_w_down'] = nrm((L, D_FF, D_MODEL), D_FF ** -0.5)
    d['mix_norm'] = gain((L, D_MODEL))
    d['w_in'] = nrm((L, D_MODEL, IN_COLS), D_MODEL ** -0.5)
    d['q_norm'] = gain((L, Q_RANK))
    d['w_uq'] = nrm((L, Q_RANK, N_HEADS * Q_HEAD_DIM), Q_RANK ** -0.5)
    d['kv_norm'] = gain((L, KV_RANK))
    d['w_uk'] = nrm((L, KV_RANK, N_HEADS, QK_NOPE_DIM), KV_RANK ** -0.5)
    d['w_uv'] = nrm((L, KV_RANK, N_HEADS, V_HEAD_DIM), KV_RANK ** -0.5)
    d['conv_w'] = nrm((L, CONV_WIDTH, CONV_CH), CONV_WIDTH ** -0.5)
    d['conv_b'] = nrm((L, CONV_CH), 0.02)
    d['conv_ln_g'] = gain((L, CONV_CH))
    d['conv_ln_b'] = nrm((L, CONV_CH), 0.02)
    d['attn_grp_norm'] = gain((L, ATTN_WIDTH))
    d['conv_grp_norm'] = gain((L, CONV_CH))
    d['w_out'] = nrm((L, MIX_WIDTH, D_MODEL), MIX_WIDTH ** -0.5)
    d['ffn2_norm'] = gain((L, D_MODEL))
    d['ffn2_w_gate'] = nrm((L, D_MODEL, D_FF), D_MODEL ** -0.5)
    d['ffn2_w_up'] = nrm((L, D_MODEL, D_FF), D_MODEL ** -0.5)
    d['ffn2_w_down'] = nrm((L, D_FF, D_MODEL), D_FF ** -0.5)
    d['final_norm'] = gain((D_MODEL,))
    return d


def reference(x_prompt, x_sample, cache_kv_latent, cache_k_rope, state_conv, page_table,
              meta_tokens, ffn1_norm, ffn1_w_gate, ffn1_w_up, ffn1_w_down,
              mix_norm, w_in, q_norm, w_uq, kv_norm, w_uk, w_uv,
              conv_w, conv_b, conv_ln_g, conv_ln_b, attn_grp_norm, conv_grp_norm, w_out,
              ffn2_norm, ffn2_w_gate, ffn2_w_up, ffn2_w_down, final_norm):
    b_p, s_p = x_prompt.shape[0], x_prompt.shape[1]
    b_s, s_s = x_sample.shape[0], x_sample.shape[1]
    past_len = page_table.shape[1] * PAGE_SIZE
    layer_params = (ffn1_norm, ffn1_w_gate, ffn1_w_up, ffn1_w_down,
                    mix_norm, w_in, q_norm, w_uq, kv_norm, w_uk, w_uv,
                    conv_w, conv_b, conv_ln_g, conv_ln_b,
                    attn_grp_norm, conv_grp_norm, w_out,
                    ffn2_norm, ffn2_w_gate, ffn2_w_up, ffn2_w_down)

    hp = jnp.concatenate([jnp.broadcast_to(meta_tokens[None].astype(x_prompt.dtype),
                                           (b_p, N_META, D_MODEL)), x_prompt], axis=1)
    pos_p = jnp.arange(s_p + N_META, dtype=jnp.int32)
    pos_s = past_len + jnp.arange(s_s, dtype=jnp.int32)
    hs = x_sample

    ckv_p, kr_p, conv_p, ckv_s, kr_s, conv_s = [], [], [], [], [], []
    for l in range(DEPTH):
        lp = [w[l] for w in layer_params]
        zeros_prev = jnp.zeros((b_p, CONV_WIDTH - 1, CONV_CH), hp.dtype)
        hp, c1, k1, st1 = layer_forward(hp, pos_p, zeros_prev, prompt_attention, *lp)

        c_past = cache_kv_latent[l, page_table].reshape(b_s, past_len, KV_RANK)
        kr_past = cache_k_rope[l, page_table].reshape(b_s, past_len, QK_ROPE_DIM)
        attn_s = lambda ql, qr, c, kr, wv: sample_attention(ql, qr, c, kr, c_past, kr_past, wv)
        hs, c2, k2, st2 = layer_forward(hs, pos_s, state_conv[l], attn_s, *lp)

        ckv_p.append(c1); kr_p.append(k1); conv_p.append(st1)
        ckv_s.append(c2); kr_s.append(k2); conv_s.append(st2)

    y_prompt = rmsnorm(hp, final_norm)[:, N_META:]
    y_sample = rmsnorm(hs, final_norm)
    return (y_prompt, y_sample,
            jnp.stack(ckv_p), jnp.stack(kr_p), jnp.stack(conv_p),
            jnp.stack(ckv_s), jnp.stack(kr_s), jnp.stack(conv_s))
```

```python
import numpy as np
from contextlib import ExitStack
import concourse.bass as bass
import concourse.mybir as mybir
from concourse.bass_utils import run_bass_kernel_spmd

F32 = mybir.dt.float32
BF16 = mybir.dt.bfloat16
I32 = mybir.dt.int32
AF = mybir.ActivationFunctionType
ALU = mybir.AluOpType
AX = mybir.AxisListType
P = 128
EPS = 1e-6


class Cfg:
    def __init__(self, ncores=8, nseq=4, seq=2048, nsamp=16, npg=128, npool_sh=2560, dff=2816, gp=16):
        self.gp = gp; self.pt = 128 // gp; self.nseg = 4
        self.ncores = ncores; self.nseq = nseq; self.seq = seq; self.S = seq + 16
        self.nsamp = nsamp; self.npg = npg; self.npool_sh = npool_sh; self.dff = dff
        self.D = 1024; self.past = npg * 128
        self.NB = -(-self.S // P)
        self.nstok = nsamp * 4
        self.nsall = ncores * nsamp
        self.T = nseq * self.S + self.nstok + self.nsall * 4
        self.tpd = 128 // gp
        self.nj = dff // P


class Reg:
    __slots__ = ("w", "r", "dsem", "const")

    def __init__(self, dsem=None, const=False):
        self.w = None; self.r = {}; self.dsem = dsem; self.const = const


class Sync:
    def __init__(self, nc, es):
        self.nc = nc; self.es = es
        self.E = {"pe": nc.tensor, "act": nc.scalar, "dve": nc.vector, "pool": nc.gpsimd, "sp": nc.sync}
        self.sem = {}; self.cnt = {}; self.known = {e: {} for e in self.E}
        for e in ("pe", "act", "dve", "pool"):
            self.sem[e] = es.enter_context(nc.semaphore("sem_" + e)); self.cnt[e] = 0
        self.dcnt = {}; self.dsems = []; self.nd = 0

    def dreg(self, const=False):
        s = self.es.enter_context(self.nc.semaphore("dsem%d" % self.nd)); self.nd += 1
        self.dcnt[id(s)] = 0; self.dsems.append(s)
        return Reg(dsem=s, const=const)

    def _wait(self, eng, evs):
        need = {}
        for (s, v) in evs:
            k = id(s)
            if k not in need or need[k][1] < v:
                need[k] = (s, v)
        kn = self.known[eng]
        for k, (s, v) in need.items():
            if kn.get(k, 0) < v:
                self.E[eng].wait_ge(s, v); kn[k] = v

    def op(self, eng, fn, reads=(), writes=(), signal=True, dma=None):
        evs = []
        for r in reads:
            if r.w is not None: evs.append(r.w)
        for w in writes:
            if w.w is not None: evs.append(w.w)
            evs.extend(w.r.values())
        if eng == "pe":
            evs = [e for e in evs if e[0] is not self.sem["pe"]]
        self._wait(eng, evs)
        inst = fn()
        if not signal:
            return inst
        if dma is not None:
            s = dma.dsem; self.dcnt[id(s)] += 16; v = self.dcnt[id(s)]; inst.then_inc(s, 16)
        else:
            s = self.sem[eng]; self.cnt[eng] += 1; v = self.cnt[eng]; inst.then_inc(s, 1)
        ev = (s, v)
        for w in writes:
            w.w = ev; w.r = {}
        for r in reads:
            if not r.const:
                r.r[id(s)] = ev
        return inst

    def barrier(self):
        evs = [(self.sem[e], self.cnt[e]) for e in self.sem if self.cnt[e] > 0]
        evs += [(s, self.dcnt[id(s)]) for s in self.dsems if self.dcnt[id(s)] > 0]
        for e in self.E:
            self._wait(e, evs)


def build(cfg, stage=1):
    nc = bass.Bass("TRN2", target_bir_lowering=False)
    es = ExitStack()
    sy = Sync(nc, es)
    D, S, NB, T, DFF, NJ = cfg.D, cfg.S, cfg.NB, cfg.T, cfg.dff, cfg.nj
    NSQ, NSM, NPG = cfg.nseq, cfg.nsamp, cfg.npg
    SCALE = 96.0 ** -0.5

    def din(name, shape, dt=F32):
        return nc.dram_tensor(name, list(shape), dt, kind="ExternalInput")

    def dout(name, shape, dt=F32):
        return nc.dram_tensor(name, list(shape), dt, kind="ExternalOutput")

    def dscr(name, shape, dt):
        return nc.dram_tensor(name, list(shape), dt, kind="Internal")

    NSALL = cfg.nsall; GP = cfg.gp; TPD = cfg.tpd; TOUT = NSQ * S + cfg.nstok
    NCR = cfg.ncores
    if stage == 2:
        wsrc = {}
        for k_, shp in (("wg2", [D, DFF]), ("wu2", [D, DFF]), ("wd2", [DFF, D]), ("wuv", [256, 512]), ("wout", [D, D])):
            wsrc[k_] = din(k_, shp)
        vecs = din("vecs", [P, 64]); bvec = din("bvec", [1, 1536]); identf = din("identf", [P, P])
        pin_o = din("pin_o", [NSM * NCR * 32, 256]); pin_ml = din("pin_ml", [NSM * NCR * 32, 2])
        hs_in = din("hs_in", [cfg.nstok, D]); cm_in = din("cm_in", [NSM * P, 16])
        ys = dout("ys", [cfg.nstok, D])
        xp = xs = xsa = poolx = stconv = ptabT = ownb = meta = convw = maskc_d = masks_d = ropet = None
        yp = ckv_o = kr_o = convp = convs = part_o = part_ml = hs_o = cm_o = None
    else:
        xp = din("xp", [NSQ * cfg.seq, D]); xs = din("xs", [cfg.nstok, D]); xsa = din("xsa", [NSALL * 4, D])
        poolx = din("poolx", [cfg.npool_sh * GP, TPD * 288])
        stconv = din("stconv", [NSM * 30, 512]); ptabT = din("ptabT", [NPG, NSALL], I32)
        ownb = din("ownb", [NSALL, 4])
        meta = din("meta", [16, D])
        wsrc = {}
        for f in ("1", "2"):
            wsrc["wg" + f] = din("wg" + f, [D, DFF]); wsrc["wu" + f] = din("wu" + f, [D, DFF])
            wsrc["wd" + f] = din("wd" + f, [DFF, D])
        wsrc["win"] = din("win", [D, 1568]); wsrc["wuq"] = din("wuq", [256, 768])
        wsrc["wuk"] = din("wuk", [256, 512]); wsrc["wuv"] = din("wuv", [256, 512]); wsrc["wout"] = din("wout", [D, D])
        vecs = din("vecs", [P, 64])
        convw = din("convw", [P, 4 * 31])
        bvec = din("bvec", [1, 1536])
        identf = din("identf", [P, P]); maskc_d = din("maskc", [P, P]); masks_d = din("masks", [32, 4])
        ropet = din("ropet", [P, (NB + 1) * 32])

        yp = dout("yp", [NSQ * cfg.seq, D]); ys = None
        ckv_o = dout("ckv_o", [TOUT, 256]); kr_o = dout("kr_o", [TOUT, 32])
        convp = dout("convp", [NSQ * 30, 512]); convs = dout("convs", [NSM * 30, 512])
        part_o = dout("part_o", [NSALL * 32, 256]); part_ml = dout("part_ml", [NSALL * 32, 2])
        hs_o = dout("hs_o", [cfg.nstok, D]); cm_o = dout("cm_o", [NSM * P, 16])

    wb = {k: dscr(k + "_b", v.shape, BF16) for k, v in wsrc.items()}
    h_d = dscr("h_d", [T, D], F32)
    qcT_d = dscr("qcT_d", [256, T], BF16); cT_d = dscr("cT_d", [256, T], BF16)
    ckvb_d = dscr("ckvb_d", [T, 256], BF16); krT_d = dscr("krT_d", [32, T], BF16)
    GW = NSQ * (S + 30) + NSM * 34
    glu_d = dscr("glu_d", [512, GW], F32)
    NCOLI = GP * NSALL

    uid = [0]

    def sb(st, name, shape, dt=F32):
        uid[0] += 1
        return st.enter_context(nc.sbuf_tensor("s%d_%s" % (uid[0], name), list(shape), dt))

    def ps(st, name, shape, dt=F32):
        uid[0] += 1
        return st.enter_context(nc.psum_tensor("p%d_%s" % (uid[0], name), list(shape), dt))

    op = sy.op
    dram_regs = {}

    def dr(key):
        if key not in dram_regs: dram_regs[key] = Reg()
        return dram_regs[key]

    out_events = []

    def load(dst_ap, src_ap, reg, src_regs=(), eng="sp"):
        return op(eng, lambda: sy.E[eng].dma_start(out=dst_ap, in_=src_ap), reads=src_regs, writes=[reg], dma=reg)

    def store(dst_ap, src_ap, reg, dst_regs=(), final=False, eng="sp"):
        i = op(eng, lambda: sy.E[eng].dma_start(out=dst_ap, in_=src_ap), reads=[reg], writes=dst_regs, dma=reg)
        if final: out_events.append((reg.dsem, sy.dcnt[id(reg.dsem)]))
        return i

    cs = es
    ident_f = sb(cs, "ident_f", [P, P]); ident_b = sb(cs, "ident_b", [P, P], BF16)
    ones_b = sb(cs, "ones_b", [P, P], BF16); maskc = sb(cs, "maskc_sb", [P, P], BF16); masks = sb(cs, "masks_sb", [32, 4], BF16)
    mtmp = sb(cs, "mtmp", [P, P]); mtmp2 = sb(cs, "mtmp2", [32, 4])
    vec_sb = sb(cs, "vec_sb", [P, 64]); convw_sb = sb(cs, "convw_sb", [P, 4 * 31]); bvec_sb = sb(cs, "bvec_sb", [P, 1536])
    rope_sb = sb(cs, "rope_sb", [P, (NB + 1) * 32])
    idxk = sb(cs, "idxk", [P, NCOLI], I32); ptT = sb(cs, "ptT", [P, NSALL], I32)
    fA = sb(cs, "fA", [P, NSALL]); fV = sb(cs, "fV", [P, NSALL]); fN = sb(cs, "fN", [P, NSALL]); fT = sb(cs, "fT", [P, NSALL])
    biasb = sb(cs, "biasb", [P, NSALL], BF16); R_idxf = Reg()
    zero_sb = sb(cs, "zero_sb", [P, 32])
    R_c = sy.dreg(const=True); R_c2 = Reg(const=True); R_idx = sy.dreg(); R_m = sy.dreg(); R_z = Reg()
    load(ident_f[:], identf.ap(), R_c); load(vec_sb[:], vecs.ap(), R_c)
    load(bvec_sb[:], bvec.ap().partition_broadcast(P), R_c)
    if stage == 1:
        load(convw_sb[:], convw.ap(), R_c); load(rope_sb[:], ropet.ap(), R_c)
        load(mtmp[:], maskc_d.ap(), R_m); load(mtmp2[:], masks_d.ap(), R_m)
        load(ptT[:NPG, :], ptabT.ap(), R_idx)
    op("dve", lambda: nc.vector.tensor_copy(out=ident_b[:], in_=ident_f[:]), reads=[R_c], writes=[R_c2])
    op("dve", lambda: nc.vector.memset(ones_b[:], 1.0), writes=[R_c2])
    op("dve", lambda: nc.vector.memset(zero_sb[:], 0.0), writes=[R_z])
    if stage == 1:
        op("dve", lambda: nc.vector.tensor_copy(out=maskc[:], in_=mtmp[:]), reads=[R_m], writes=[R_c2])
        op("dve", lambda: nc.vector.tensor_copy(out=masks[:], in_=mtmp2[:]), reads=[R_m], writes=[R_c2])
    if stage == 1:
        NPL = float(cfg.npool_sh); BIG = float(2 ** 30)
        def dv(fn, rd=(), wr=()):
            op("dve", fn, reads=[R_idxf] + list(rd), writes=[R_idxf] + list(wr))
        op("dve", lambda: nc.vector.tensor_copy(out=fA[:NPG], in_=ptT[:NPG]), reads=[R_idx], writes=[R_idxf])
        dv(lambda: nc.vector.tensor_scalar(out=fA[:NPG], in0=fA[:NPG], scalar1=vec_sb[:NPG, 45:46], scalar2=None, op0=ALU.subtract), rd=[R_c])
        dv(lambda: nc.vector.tensor_scalar(out=fV[:NPG], in0=fA[:NPG], scalar1=1.0, scalar2=0.0, op0=ALU.add, op1=ALU.max))
        dv(lambda: nc.vector.tensor_scalar(out=fV[:NPG], in0=fV[:NPG], scalar1=1.0, scalar2=None, op0=ALU.min))
        dv(lambda: nc.vector.tensor_scalar(out=fT[:NPG], in0=fA[:NPG], scalar1=-1.0, scalar2=NPL, op0=ALU.mult, op1=ALU.add))
        dv(lambda: nc.vector.tensor_scalar(out=fT[:NPG], in0=fT[:NPG], scalar1=0.0, scalar2=1.0, op0=ALU.max, op1=ALU.min))
        dv(lambda: nc.vector.tensor_tensor(out=fV[:NPG], in0=fV[:NPG], in1=fT[:NPG], op=ALU.mult))
        dv(lambda: nc.vector.tensor_scalar(out=fN[:NPG], in0=fV[:NPG], scalar1=-BIG, scalar2=BIG, op0=ALU.mult, op1=ALU.add))
        dv(lambda: nc.vector.tensor_scalar(out=fT[:NPG], in0=fV[:NPG], scalar1=-1.0, scalar2=30000.0, op0=ALU.add, op1=ALU.mult))
        dv(lambda: nc.vector.tensor_copy(out=biasb[:NPG], in_=fT[:NPG]), wr=[R_idx])
        dv(lambda: nc.vector.tensor_scalar(out=fA[:NPG], in0=fA[:NPG], scalar1=float(GP), scalar2=None, op0=ALU.mult))
        for k in range(GP):
            dv(lambda: nc.vector.tensor_scalar(out=fT[:NPG], in0=fA[:NPG], scalar1=float(k), scalar2=None, op0=ALU.add))
            dv(lambda: nc.vector.tensor_tensor(out=fT[:NPG], in0=fT[:NPG], in1=fV[:NPG], op=ALU.mult))
            dv(lambda: nc.vector.tensor_tensor(out=fT[:NPG], in0=fT[:NPG], in1=fN[:NPG], op=ALU.add))
            dv(lambda: nc.vector.tensor_copy(out=idxk[:NPG, k * NSALL:(k + 1) * NSALL], in_=fT[:NPG]), wr=[R_idx])
    CONST = [R_c, R_c2]
    V_F1, V_MIX, V_F2, V_AG, V_CG, V_CB, V_LG, V_LB = 0, 8, 16, 24, 28, 32, 36, 40
    alt = [0]

    def evac(out_ap, in_ap, reads, writes, scale=None):
        alt[0] ^= 1
        if alt[0]:
            if scale is None:
                return op("act", lambda: nc.scalar.activation(out=out_ap, in_=in_ap, func=AF.Copy), reads=reads, writes=writes)
            return op("act", lambda: nc.scalar.activation(out=out_ap, in_=in_ap, func=AF.Copy, scale=scale), reads=reads, writes=writes)
        if scale is None:
            return op("dve", lambda: nc.vector.tensor_copy(out=out_ap, in_=in_ap), reads=reads, writes=writes)
        return op("dve", lambda: nc.vector.tensor_scalar(out=out_ap, in0=in_ap, scalar1=scale, scalar2=None, op0=ALU.mult),
                  reads=reads, writes=writes)

    def rstd_of(ss_ap, n, out_ap, reg, nfeat):
        op("dve", lambda: nc.vector.tensor_scalar(out=out_ap[:n], in0=ss_ap[:n], scalar1=1.0 / nfeat, scalar2=EPS,
                                                   op0=ALU.mult, op1=ALU.add), reads=[reg], writes=[reg])
        op("act", lambda: nc.scalar.activation(out=out_ap[:n], in_=out_ap[:n], func=AF.Sqrt), reads=[reg], writes=[reg])
        op("dve", lambda: nc.vector.reciprocal(out=out_ap[:n], in_=out_ap[:n]), reads=[reg], writes=[reg])

    R_wb = {k: Reg() for k in wsrc}
    with ExitStack() as st:
        MAXC = max(DFF, 1568)
        stf = [sb(st, "stf%d" % i, [P, MAXC]) for i in range(2)]; stb = [sb(st, "stb%d" % i, [P, MAXC], BF16) for i in range(2)]
        Rf = [sy.dreg() for _ in range(2)]; Rb = [sy.dreg() for _ in range(2)]
        k = 0
        for name, w in wsrc.items():
            R, Cn = w.shape
            for rc in range(R // P):
                s = k % 2; k += 1
                load(stf[s][:, :Cn], w[rc * P:(rc + 1) * P, :], Rf[s])
                e = ("dve", "act", "pool")[k % 3]
                if e == "act":
                    op("act", lambda: nc.scalar.activation(out=stb[s][:, :Cn], in_=stf[s][:, :Cn], func=AF.Copy), reads=[Rf[s]], writes=[Rb[s]])
                elif e == "dve":
                    op("dve", lambda: nc.vector.tensor_copy(out=stb[s][:, :Cn], in_=stf[s][:, :Cn]), reads=[Rf[s]], writes=[Rb[s]])
                else:
                    op("pool", lambda: nc.gpsimd.tensor_copy(out=stb[s][:, :Cn], in_=stf[s][:, :Cn]), reads=[Rf[s]], writes=[Rb[s]])
                store(wb[name][rc * P:(rc + 1) * P, :], stb[s][:, :Cn], Rb[s], dst_regs=[R_wb[name]])
        sy.barrier()

    def seq_tiles():
        tiles = []
        if stage == 2:
            return [("s", 0, 0, cfg.nstok)]
        for s in range(NSQ):
            for t0 in range(0, S, 512):
                n = min(512, S - t0)
                tiles.append(("p", s, t0, n))
        tiles.append(("s", 0, 0, cfg.nstok))
        for t0 in range(0, NSALL * 4, 512):
            tiles.append(("a", 0, t0, min(512, NSALL * 4 - t0)))
        return tiles

    def tile_gt0(kind, s, t0):
        if kind == "p": return s * S + t0
        if kind == "s": return NSQ * S
        return NSQ * S + cfg.nstok + t0

    def gcol(kind, s, t0):
        return s * (S + 30) + 30 + t0

    def ffn_phase(which):
        first = which == "1"
        with ExitStack() as st:
            xin = [sb(st, "xin%d" % i, [P, D]) for i in range(4)]; Rx = [sy.dreg() for _ in range(4)]
            junk = sb(st, "junk", [P, D]); Rj = Reg()
            ssq = sb(st, "ssq", [P, 8]); Rss = Reg()
            xn = sb(st, "xn", [P, D], BF16); Rxn = Reg()
            nT = sb(st, "nT", [P, 8, 512], BF16); RnT = Reg()
            actT = sb(st, "actT", [P, NJ, 512], BF16); Ract = Reg()
            sg = [sb(st, "sg%d" % i, [P, 512]) for i in range(2)]; Rsg = [Reg(), Reg()]
            wg = [sb(st, "wg%d" % i, [P, 8, 256], BF16) for i in range(2)]; Rwg = [sy.dreg() for _ in range(2)]
            wu = [sb(st, "wu%d" % i, [P, 8, 256], BF16) for i in range(2)]; Rwu = [sy.dreg() for _ in range(2)]
            wd = [sb(st, "wd%d" % i, [P, NJ, 256], BF16) for i in range(2)]; Rwd = [sy.dreg() for _ in range(2)]
            psG = [ps(st, "psG%d" % i, [P, 512]) for i in range(2)]; RpG = [Reg(), Reg()]
            psU = [ps(st, "psU%d" % i, [P, 512]) for i in range(2)]; RpU = [Reg(), Reg()]
            psD = [ps(st, "psD%d" % i, [P, 512]) for i in range(2)]; RpD = [Reg(), Reg()]
            psT = ps(st, "psT", [P, 8, P], BF16); RpT = Reg()
            psM = ps(st, "psM", [P, 512]); RpM = Reg()
            if first:
                win_sb = sb(st, "win_sb", [P, 8, 1568], BF16); Rwin = sy.dreg(const=True)
                load(win_sb[:], wb["win"].ap().rearrange("(kc p) n -> p kc n", p=P), Rwin, src_regs=[R_wb["win"]])
                yo = sb(st, "yo", [P, 512]); Ryo = sy.dreg()
                yb = sb(st, "yb", [P, 256], BF16); Ryb = sy.dreg()
                tT = sb(st, "tT", [P, 2, 512], BF16); RtT = sy.dreg()
                cTt = sb(st, "cTt", [P, 2, 512], BF16); RcTt = sy.dreg()
                krf = sb(st, "krf", [P, 32]); Rkrf = sy.dreg()
                krb = sb(st, "krb", [P, 32], BF16); Rkrb = Reg()
                krt = sb(st, "krt", [32, 512], BF16); Rkrt = sy.dreg()
                rtmp = sb(st, "rtmp", [P, 4, 16]); Rrt = Reg()
                glu = sb(st, "glu", [P, 4, 512]); Rglu = sy.dreg()
                sgm = sb(st, "sgm", [P, 512]); Rsgm = Reg()
            else:
                yo = sb(st, "yo", [P, D]); Ryo = sy.dreg()
            gcol0 = V_F1 if first else V_F2
            wgd, wud, wdd = wb["wg" + which], wb["wu" + which], wb["wd" + which]
            Rwsrc = [R_wb["wg" + which], R_wb["wu" + which], R_wb["wd" + which]]
            cnt = {"g": 0, "d": 0, "p": 0}

            def norm_T(src, n, boff, gc, Rsrc):
                op("act", lambda: nc.scalar.activation(out=junk[:n], in_=src[:n], func=AF.Square, accum_out=ssq[:n, 0:1]),
                   reads=[Rsrc], writes=[Rj, Rss])
                rstd_of(ssq[:, 0:1], n, ssq[:, 1:2], Rss, D)
                op("dve", lambda: nc.vector.tensor_scalar(out=xn[:n], in0=src[:n], scalar1=ssq[:n, 1:2], scalar2=None, op0=ALU.mult),
                   reads=[Rsrc, Rss], writes=[Rxn])
                for kc in range(8):
                    op("pe", lambda: nc.tensor.transpose(out=psT[:, kc, :n], in_=xn[:n, kc * P:(kc + 1) * P], identity=ident_b[:n, :n]),
                       reads=[Rxn] + CONST, writes=[RpT], signal=(kc == 7))
                for kc in range(8):
                    evac(nT[:, kc, boff:boff + n], psT[:, kc, :n], [RpT] + CONST, [RnT], scale=vec_sb[:, gc + kc:gc + kc + 1])

            for (kind, s, t0, n) in seq_tiles():
                if stage == 1 and (not first) and kind != "p":
                    continue
                gt0 = tile_gt0(kind, s, t0)
                blocks = [(b0, min(P, n - b0)) for b0 in range(0, n, P)]
                for bi, (b0, nb) in enumerate(blocks):
                    if first:
                        if kind == "p":
                            tok = t0 + b0
                            if tok == 0:
                                load(xin[bi][0:16, :], meta.ap(), Rx[bi])
                                load(xin[bi][16:nb, :], xp[s * cfg.seq: s * cfg.seq + nb - 16, :], Rx[bi])
                            else:
                                load(xin[bi][:nb, :], xp[s * cfg.seq + tok - 16: s * cfg.seq + tok - 16 + nb, :], Rx[bi])
                        elif kind == "s":
                            load(xin[bi][:nb, :], xs[b0:b0 + nb, :], Rx[bi])
                        else:
                            load(xin[bi][:nb, :], xsa[t0 + b0:t0 + b0 + nb, :], Rx[bi])
                    else:
                        load(xin[bi][:nb, :], h_d[gt0 + b0: gt0 + b0 + nb, :], Rx[bi], src_regs=[dr(("h", gt0 + b0))])
                    norm_T(xin[bi], nb, b0, gcol0, Rx[bi])
                for g in range(0, NJ, 2):
                    gs = cnt["g"] % 2; cnt["g"] += 1
                    ncol = min(2, NJ - g) * P
                    load(wg[gs][:, :, :ncol], wgd.ap().rearrange("(kc p) n -> p kc n", p=P)[:, :, g * P: g * P + ncol], Rwg[gs], src_regs=Rwsrc)
                    load(wu[gs][:, :, :ncol], wud.ap().rearrange("(kc p) n -> p kc n", p=P)[:, :, g * P: g * P + ncol], Rwu[gs], src_regs=Rwsrc)
                    for jj in range(ncol // P):
                        j = g + jj
                        pp = cnt["p"] % 2; cnt["p"] += 1
                        for kc in range(8):
                            op("pe", lambda: nc.tensor.matmul(psG[pp][:, :n], lhsT=wg[gs][:, kc, jj * P:(jj + 1) * P], rhs=nT[:, kc, :n],
                                                               start=(kc == 0), stop=(kc == 7)),
                               reads=[Rwg[gs], RnT], writes=[RpG[pp]], signal=(kc == 7))
                        for kc in range(8):
                            op("pe", lambda: nc.tensor.matmul(psU[pp][:, :n], lhsT=wu[gs][:, kc, jj * P:(jj + 1) * P], rhs=nT[:, kc, :n],
                                                               start=(kc == 0), stop=(kc == 7)),
                               reads=[Rwu[gs], RnT], writes=[RpU[pp]], signal=(kc == 7))
                        op("act", lambda: nc.scalar.activation(out=sg[pp][:, :n], in_=psG[pp][:, :n], func=AF.Silu),
                           reads=[RpG[pp]], writes=[Rsg[pp]])
                        op("dve", lambda: nc.vector.tensor_tensor(out=actT[:, j, :n], in0=sg[pp][:, :n], in1=psU[pp][:, :n], op=ALU.mult),
                           reads=[Rsg[pp], RpU[pp]], writes=[Ract])
                for q4 in range(4):
                    ds_ = cnt["d"] % 2; cnt["d"] += 1
                    load(wd[ds_][:], wdd.ap().rearrange("(j p) n -> p j n", p=P)[:, :, q4 * 256:(q4 + 1) * 256], Rwd[ds_], src_regs=Rwsrc)
                    for bi, (b0, nb) in enumerate(blocks):
                        pp = cnt["p"] % 2; cnt["p"] += 1
                        for j in range(NJ):
                            op("pe", lambda: nc.tensor.matmul(psD[pp][:nb, :256], lhsT=actT[:, j, b0:b0 + nb], rhs=wd[ds_][:, j, :],
                                                               start=(j == 0), stop=(j == NJ - 1)),
                               reads=[Rwd[ds_], Ract], writes=[RpD[pp]], signal=(j == NJ - 1))
                        op("dve", lambda: nc.vector.scalar_tensor_tensor(out=xin[bi][:nb, q4 * 256:(q4 + 1) * 256], in0=psD[pp][:nb, :256],
                                                                          scalar=0.5, in1=xin[bi][:nb, q4 * 256:(q4 + 1) * 256],
                                                                          op0=ALU.mult, op1=ALU.add),
                           reads=[RpD[pp], Rx[bi]], writes=[Rx[bi]])
                if not first:
                    for bi, (b0, nb) in enumerate(blocks):
                        op("act", lambda: nc.scalar.activation(out=junk[:nb], in_=xin[bi][:nb], func=AF.Square, accum_out=ssq[:nb, 0:1]),
                           reads=[Rx[bi]], writes=[Rj, Rss])
                        rstd_of(ssq[:, 0:1], nb, ssq[:, 1:2], Rss, D)
                        op("dve", lambda: nc.vector.scalar_tensor_tensor(out=yo[:nb], in0=xin[bi][:nb], scalar=ssq[:nb, 1:2],
                                                                          in1=bvec_sb[:nb, 512:1536], op0=ALU.mult, op1=ALU.mult),
                           reads=[Rx[bi], Rss] + CONST, writes=[Ryo])
                        if kind == "p":
                            tok = t0 + b0
                            if tok == 0:
                                store(yp[s * cfg.seq: s * cfg.seq + nb - 16, :], yo[16:nb, :], Ryo, final=True)
                            else:
                                store(yp[s * cfg.seq + tok - 16: s * cfg.seq + tok - 16 + nb, :], yo[:nb, :], Ryo, final=True)
                        else:
                            store(ys[b0:b0 + nb, :], yo[:nb, :], Ryo, final=True)
                    continue
                for bi, (b0, nb) in enumerate(blocks):
                    store(h_d[gt0 + b0: gt0 + b0 + nb, :], xin[bi][:nb, :], Rx[bi], dst_regs=[dr(("h", gt0 + b0))])
                    norm_T(xin[bi], nb, b0, V_MIX, Rx[bi])
                rblk0 = (t0 // P) if kind == "p" else NB
                rstep = 1 if kind == "p" else 0
                for bi, (b0, nb) in enumerate(blocks):
                    pp = cnt["p"] % 2; cnt["p"] += 1
                    for kc in range(8):
                        op("pe", lambda: nc.tensor.matmul(psG[pp][:nb, :512], lhsT=nT[:, kc, b0:b0 + nb], rhs=win_sb[:, kc, 0:512],
                                                           start=(kc == 0), stop=(kc == 7)), reads=[RnT, Rwin], writes=[RpG[pp]], signal=(kc == 7))
                    for kc in range(8):
                        op("pe", lambda: nc.tensor.matmul(psU[pp][:nb, :32], lhsT=nT[:, kc, b0:b0 + nb], rhs=win_sb[:, kc, 512:544],
                                                           start=(kc == 0), stop=(kc == 7)), reads=[RnT, Rwin], writes=[RpU[pp]], signal=(kc == 7))
                    for part in range(2):
                        src = psG[pp][:nb, part * 256:(part + 1) * 256]
                        op("act", lambda: nc.scalar.activation(out=junk[:nb, :256], in_=src, func=AF.Square, accum_out=ssq[:nb, 2:3]),
                           reads=[RpG[pp]], writes=[Rj, Rss])
                        rstd_of(ssq[:, 2:3], nb, ssq[:, 3:4], Rss, 256)
                        if part == 0:
                            op("dve", lambda: nc.vector.scalar_tensor_tensor(out=yb[:nb, :], in0=src, scalar=ssq[:nb, 3:4], in1=bvec_sb[:nb, 0:256],
                                                                              op0=ALU.mult, op1=ALU.mult), reads=[RpG[pp], Rss] + CONST, writes=[Ryb])
                            dstT, RdT = tT, RtT
                        else:
                            op("dve", lambda: nc.vector.scalar_tensor_tensor(out=yo[:nb, :256], in0=src, scalar=ssq[:nb, 3:4], in1=bvec_sb[:nb, 256:512],
                                                                              op0=ALU.mult, op1=ALU.mult), reads=[RpG[pp], Rss] + CONST, writes=[Ryo])
                            if kind != "a":
                                store(ckv_o[gt0 + b0: gt0 + b0 + nb, :], yo[:nb, :256], Ryo, final=True)
                            op("act", lambda: nc.scalar.activation(out=yb[:nb, :], in_=yo[:nb, :256], func=AF.Copy), reads=[Ryo], writes=[Ryb])
                            store(ckvb_d[gt0 + b0: gt0 + b0 + nb, :], yb[:nb, :], Ryb, dst_regs=[dr(("kv", gt0 + b0))])
                            dstT, RdT = cTt, RcTt
                        for rc in range(2):
                            op("pe", lambda: nc.tensor.transpose(out=psT[:, rc, :nb], in_=yb[:nb, rc * P:(rc + 1) * P], identity=ident_b[:nb, :nb]),
                               reads=[Ryb] + CONST, writes=[RpT], signal=(rc == 1))
                        evac(dstT[:, :, b0:b0 + nb], psT[:, 0:2, :nb], [RpT], [RdT])
                    cosb = rope_sb[:nb, (rblk0 + bi * rstep) * 32:(rblk0 + bi * rstep) * 32 + 16]
                    sinb = rope_sb[:nb, (rblk0 + bi * rstep) * 32 + 16:(rblk0 + bi * rstep) * 32 + 32]
                    x1 = psU[pp][:nb, 0:16]; x2 = psU[pp][:nb, 16:32]
                    rd = [RpU[pp]] + CONST
                    op("dve", lambda: nc.vector.tensor_tensor(out=rtmp[:nb, 0, :], in0=x1, in1=cosb, op=ALU.mult), reads=rd, writes=[Rrt])
                    op("dve", lambda: nc.vector.tensor_tensor(out=rtmp[:nb, 1, :], in0=x2, in1=sinb, op=ALU.mult), reads=rd, writes=[Rrt])
                    op("dve", lambda: nc.vector.tensor_tensor(out=rtmp[:nb, 2, :], in0=x1, in1=sinb, op=ALU.mult), reads=rd, writes=[Rrt])
                    op("dve", lambda: nc.vector.tensor_tensor(out=rtmp[:nb, 3, :], in0=x2, in1=cosb, op=ALU.mult), reads=rd, writes=[Rrt])
                    op("dve", lambda: nc.vector.tensor_tensor(out=krf[:nb, 0:16], in0=rtmp[:nb, 0, :], in1=rtmp[:nb, 1, :], op=ALU.subtract),
                       reads=[Rrt], writes=[Rkrf])
                    op("dve", lambda: nc.vector.tensor_tensor(out=krf[:nb, 16:32], in0=rtmp[:nb, 2, :], in1=rtmp[:nb, 3, :], op=ALU.add),
                       reads=[Rrt], writes=[Rkrf])
                    if kind != "a":
                        store(kr_o[gt0 + b0: gt0 + b0 + nb, :], krf[:nb, :], Rkrf, final=True)
                    op("act", lambda: nc.scalar.activation(out=krb[:nb, :], in_=krf[:nb, :], func=AF.Copy), reads=[Rkrf], writes=[Rkrb])
                    op("pe", lambda: nc.tensor.transpose(out=psT[:32, 2, :nb], in_=krb[:nb, :], identity=ident_b[:nb, :nb]),
                       reads=[Rkrb] + CONST, writes=[RpT])
                    evac(krt[:, b0:b0 + nb], psT[:32, 2, :nb], [RpT], [Rkrt])
                store(qcT_d.ap().rearrange("(rc p) t -> p rc t", p=P)[:, :, gt0:gt0 + n], tT[:, :, :n], RtT, dst_regs=[dr(("qc", gt0))])
                store(cT_d.ap().rearrange("(rc p) t -> p rc t", p=P)[:, :, gt0:gt0 + n], cTt[:, :, :n], RcTt, dst_regs=[dr(("cT", gt0))])
                store(krT_d[:, gt0:gt0 + n], krt[:, :n], Rkrt, dst_regs=[dr(("krT", gt0))])
                for c in range(4 if kind != "a" else 0):
                    pp = cnt["p"] % 2; cnt["p"] += 1
                    for kc in range(8):
                        op("pe", lambda: nc.tensor.matmul(psG[pp][:, :n], lhsT=win_sb[:, kc, 544 + c * P: 544 + (c + 1) * P], rhs=nT[:, kc, :n],
                                                           start=(kc == 0), stop=(kc == 7)), reads=[RnT, Rwin], writes=[RpG[pp]], signal=(kc == 7))
                    for kc in range(8):
                        op("pe", lambda: nc.tensor.matmul(psU[pp][:, :n], lhsT=win_sb[:, kc, 1056 + c * P: 1056 + (c + 1) * P], rhs=nT[:, kc, :n],
                                                           start=(kc == 0), stop=(kc == 7)), reads=[RnT, Rwin], writes=[RpU[pp]], signal=(kc == 7))
                    op("act", lambda: nc.scalar.activation(out=sgm[:, :n], in_=psU[pp][:, :n], func=AF.Sigmoid), reads=[RpU[pp]], writes=[Rsgm])
                    op("dve", lambda: nc.vector.tensor_tensor(out=glu[:, c, :n], in0=sgm[:, :n], in1=psG[pp][:, :n], op=ALU.mult),
                       reads=[Rsgm, RpG[pp]], writes=[Rglu])
                if kind == "p":
                    g0 = gcol(kind, s, t0)
                    store(glu_d.ap().rearrange("(c p) t -> p c t", p=P)[:, :, g0:g0 + n], glu[:, :, :n], Rglu, dst_regs=[dr(("glu", s))])
                elif kind == "s":
                    store(hs_o[:, :], xin[0][:n, :], Rx[0], final=True)
                    for b in range(NSM):
                        g0 = NSQ * (S + 30) + b * 34 + 30
                        store(glu_d.ap().rearrange("(c p) t -> p c t", p=P)[:, :, g0:g0 + 4], glu[:, :, b * 4:b * 4 + 4], Rglu,
                              dst_regs=[dr(("glus", b))])
            sy.barrier()

    if stage == 1:
        ffn_phase("1")

    def mixer_phase():
        with ExitStack() as st:
            wqn = sb(st, "wqn", [P, 2, 8, 64], BF16); wqr = sb(st, "wqr", [P, 2, 8, 32], BF16)
            wuk_sb = sb(st, "wuk_sb", [P, 2, 512], BF16); wukT = sb(st, "wukT", [64, 8, 256], BF16)
            wuv_sb = sb(st, "wuv_sb", [P, 2, 512], BF16); wout_sb = sb(st, "wout_sb", [P, 8, D], BF16)
            Rw = sy.dreg(const=True); RwT = Reg(const=True)
            if stage == 1:
                wq_v = wb["wuq"].ap().rearrange("(kc p) (h d) -> p kc h d", p=P, d=96)
                for kc in range(2):
                    load(wqn[:, kc], wq_v[:, kc, :, 0:64], Rw, src_regs=[R_wb["wuq"]])
                    load(wqr[:, kc], wq_v[:, kc, :, 64:96], Rw, src_regs=[R_wb["wuq"]])
                load(wuk_sb[:], wb["wuk"].ap().rearrange("(kc p) n -> p kc n", p=P), Rw, src_regs=[R_wb["wuk"]])
            load(wuv_sb[:], wb["wuv"].ap().rearrange("(kc p) n -> p kc n", p=P), Rw, src_regs=[R_wb["wuv"]])
            load(wout_sb[:], wb["wout"].ap().rearrange("(kc p) n -> p kc n", p=P), Rw, src_regs=[R_wb["wout"]])
            psA_ = ps(st, "psA_", [P, 1024]); RpA = Reg()
            psS = [ps(st, "psS%d" % i, [P, 512]) for i in range(2)]; RpS = [Reg(), Reg()]
            psB = ps(st, "psB", [P, 8, P], BF16); RpB = Reg()
            psO = ps(st, "psO", [P, 512]); RpO = Reg()
            psW = ps(st, "psW", [P, 512]); RpW = Reg()
            psX = ps(st, "psX", [P, 512]); RpX = Reg()
            for rc in range(2 if stage == 1 else 0):
                for h in range(8):
                    op("pe", lambda: nc.tensor.transpose(out=psB[:64, h, :], in_=wuk_sb[:, rc, h * 64:(h + 1) * 64], identity=ident_b[:]),
                       reads=[Rw] + CONST, writes=[RpB], signal=(h == 7))
                evac(wukT[:, :, rc * P:(rc + 1) * P], psB[:64, :, :], [RpB], [RwT])
            MW = [Rw, RwT] + CONST
            NSEG = cfg.nseg
            SMAX = max(S, (GP // NSEG) * TPD * NPG + 4)
            cT = sb(st, "cT", [P, 2, S], BF16); RcT = sy.dreg()
            kvb = sb(st, "kvb", [P, max(NB, (GP // NSEG) * TPD + 1), 256], BF16); Rkvb = sy.dreg()
            krT = sb(st, "krT", [32, S], BF16); RkrT = sy.dreg()
            qcT = sb(st, "qcT", [P, 2, P], BF16); RqcT = sy.dreg()
            qn = sb(st, "qn", [64, 8, P], BF16); Rqn = Reg()
            qlat = sb(st, "qlat", [P, 8, 2, P], BF16); Rql = Reg()
            qrt = sb(st, "qrt", [P, 8, 32], BF16); Rqrt = Reg()
            rt4 = sb(st, "rt4", [P, 4, 8, 16]); Rrt4 = Reg()
            qr = sb(st, "qr", [32, 8, P], BF16); Rqr = Reg()
            s_all = sb(st, "s_all", [P, SMAX]); Rs = Reg()
            p_all = sb(st, "p_all", [P, SMAX], BF16); Rp = Reg()
            sm = sb(st, "sm", [P, 8]); Rsm = Reg()
            pT = [sb(st, "pT%d" % i, [P, 4, P], BF16) for i in range(2)]; RpT_ = [Reg(), Reg()]
            on = sb(st, "on", [P, 256], BF16); Ron = Reg()
            oT = sb(st, "oT", [P, 2, P], BF16); RoT = Reg()
            an = sb(st, "an", [P, 512], BF16); Ran = Reg()
            mixT = sb(st, "mixT", [P, 8, P], BF16); Rmix = Reg()
            gpad = sb(st, "gpad", [P, 4, 30 + P]); Rgp = sy.dreg()
            yc = sb(st, "yc", [P, 4, P]); Ryc = Reg()
            ybf = sb(st, "ybf", [P, 4, P], BF16); Rybf = Reg()
            st3 = sb(st, "st3", [P, 3, P]); Rst3 = Reg()
            hin = sb(st, "hin", [P, D]); Rhin = sy.dreg()
            kvf = [sb(st, "kvf%d" % i, [P, TPD * 288]) for i in range(2)]; Rkvf = [sy.dreg() for _ in range(2)]
            krb8 = sb(st, "krb8", [P, TPD, 33], BF16); Rkrb4 = Reg()
            krTn = sb(st, "krTn", [33, 8], BF16)
            ownf = sb(st, "ownf", [1, 4]); ownr = sb(st, "ownr", [1, 4], BF16); Rown = sy.dreg(); Rownr = sy.dreg()
            po = sb(st, "po", [32, 256]); Rpo = sy.dreg(); pml = sb(st, "pml", [32, 2]); Rpml = sy.dreg()
            cTc = [sb(st, "cTc%d" % i, [P, 2, 512], BF16) for i in range(2)]; RcTc = [Reg(), Reg()]
            krTc = [sb(st, "krTc%d" % i, [33, 512], BF16) for i in range(2)]; RkrTc = [Reg(), Reg()]
            prev = sb(st, "prev", [32, 512]); Rprev = sy.dreg()
            qls = sb(st, "qls", [P, 2, 32], BF16); qrs = sb(st, "qrs", [33, 32], BF16); Rqls = Reg()
            segm = sb(st, "segm", [32, 8]); segl = sb(st, "segl", [32, 8]); segw = sb(st, "segw", [32, 8])
            sego = sb(st, "sego", [32, NSEG, 256]); Rseg = Reg()
            cnt = {"s": 0, "t": 0, "c": 0, "k": 0}

            def q_proj(nq, ropeblk, sample):
                for h in range(8):
                    for kc in range(2):
                        op("pe", lambda: nc.tensor.matmul(psA_[:64, h * P: h * P + nq], lhsT=wqn[:, kc, h, :], rhs=qcT[:, kc, :nq],
                                                           start=(kc == 0), stop=(kc == 1)), reads=[RqcT] + MW, writes=[RpA],
                           signal=(h == 7 and kc == 1))
                evac(qn[:, :, :nq], psA_[:64, :].rearrange("p (h t) -> p h t", h=8)[:, :, :nq], [RpA], [Rqn])
                for half in range(2):
                    for hh in range(4):
                        h = half * 4 + hh
                        for rc in range(2):
                            c0 = (hh * 2 + rc) * P
                            op("pe", lambda: nc.tensor.matmul(psA_[:, c0:c0 + nq], lhsT=wukT[:, h, rc * P:(rc + 1) * P], rhs=qn[:, h, :nq],
                                                               start=True, stop=True), reads=[Rqn] + MW, writes=[RpA],
                               signal=(hh == 3 and rc == 1))
                    evac(qlat[:, half * 4:half * 4 + 4, :, :nq], psA_[:, :].rearrange("p (h r t) -> p h r t", h=4, r=2)[:, :, :, :nq], [RpA], [Rql])
                for kc in range(2):
                    op("pe", lambda: nc.tensor.matmul(psX[:nq, :256], lhsT=qcT[:, kc, :nq], rhs=wqr[:, kc].rearrange("p h d -> p (h d)"),
                                                       start=(kc == 0), stop=(kc == 1)), reads=[RqcT] + MW, writes=[RpX], signal=(kc == 1))
                xv = psX[:nq, :256].rearrange("p (h d) -> p h d", h=8)
                x1 = xv[:, :, 0:16]; x2 = xv[:, :, 16:32]
                cosb = rope_sb[:nq, ropeblk * 32: ropeblk * 32 + 16].unsqueeze(1).to_broadcast([nq, 8, 16])
                sinb = rope_sb[:nq, ropeblk * 32 + 16: ropeblk * 32 + 32].unsqueeze(1).to_broadcast([nq, 8, 16])
                rd = [RpX] + CONST
                op("dve", lambda: nc.vector.tensor_tensor(out=rt4[:nq, 0], in0=x1, in1=cosb, op=ALU.mult), reads=rd, writes=[Rrt4])
                op("dve", lambda: nc.vector.tensor_tensor(out=rt4[:nq, 1], in0=x2, in1=sinb, op=ALU.mult), reads=rd, writes=[Rrt4])
                op("dve", lambda: nc.vector.tensor_tensor(out=rt4[:nq, 2], in0=x1, in1=sinb, op=ALU.mult), reads=rd, writes=[Rrt4])
                op("dve", lambda: nc.vector.tensor_tensor(out=rt4[:nq, 3], in0=x2, in1=cosb, op=ALU.mult), reads=rd, writes=[Rrt4])
                op("dve", lambda: nc.vector.tensor_tensor(out=qrt[:nq, :, 0:16], in0=rt4[:nq, 0], in1=rt4[:nq, 1], op=ALU.subtract),
                   reads=[Rrt4], writes=[Rqrt])
                op("dve", lambda: nc.vector.tensor_tensor(out=qrt[:nq, :, 16:32], in0=rt4[:nq, 2], in1=rt4[:nq, 3], op=ALU.add),
                   reads=[Rrt4], writes=[Rqrt])
                for h in range(8):
                    op("pe", lambda: nc.tensor.transpose(out=psB[:32, h, :nq], in_=qrt[:nq, h, :], identity=ident_b[:nq, :nq]),
                       reads=[Rqrt] + CONST, writes=[RpB], signal=(h == 7))
                evac(qr[:, :, :nq], psB[:32, :, :nq], [RpB], [Rqr])

            def attn_rows(nr, ql, qrr, chunks, vblocks, rdq):
                for ch in chunks:
                    score_chunk(nr, ql, qrr, ch, rdq)
                softmax_pv(nr, chunks[-1][0] + chunks[-1][1], vblocks)

            def score_chunk(nr, ql, qrr, ch, rdq):
                if True:
                    (k0, nk, cfn, kap, regs, mask) = ch
                    si = cnt["s"] % 2; cnt["s"] += 1
                    rdd = rdq + regs + CONST
                    op("pe", lambda: nc.tensor.matmul(psS[si][:nr, :nk], lhsT=ql(0), rhs=cfn(0), start=True, stop=False), reads=rdd, writes=[RpS[si]], signal=False)
                    op("pe", lambda: nc.tensor.matmul(psS[si][:nr, :nk], lhsT=ql(1), rhs=cfn(1), start=False, stop=False), reads=rdd, writes=[RpS[si]], signal=False)
                    if mask is not None:
                        map_, c0, ncol = mask
                        op("pe", lambda: nc.tensor.matmul(psS[si][:nr, c0:c0 + ncol], lhsT=ident_b[:nr, :nr], rhs=map_, start=False, stop=False),
                           reads=rdd, writes=[RpS[si]], signal=False)
                    op("pe", lambda: nc.tensor.matmul(psS[si][:nr, :nk], lhsT=qrr, rhs=kap, start=False, stop=True), reads=rdd, writes=[RpS[si]])
                    evac(s_all[:nr, k0:k0 + nk], psS[si][:nr, :nk], [RpS[si]], [Rs])
            def softmax_pv(nr, nkeys, vblocks):
                op("dve", lambda: nc.vector.tensor_reduce(out=sm[:nr, 0:1], in_=s_all[:nr, :nkeys], axis=AX.X, op=ALU.max), reads=[Rs], writes=[Rsm])
                op("dve", lambda: nc.vector.tensor_scalar(out=sm[:nr, 1:2], in0=sm[:nr, 0:1], scalar1=-SCALE, scalar2=None, op0=ALU.mult),
                   reads=[Rsm], writes=[Rsm])
                op("act", lambda: nc.scalar.activation(out=p_all[:nr, :nkeys], in_=s_all[:nr, :nkeys], func=AF.Exp, bias=sm[:nr, 1:2], scale=SCALE,
                                                        accum_out=sm[:nr, 3:4]), reads=[Rs, Rsm], writes=[Rp, Rsm])
                op("dve", lambda: nc.vector.reciprocal(out=sm[:nr, 2:3], in_=sm[:nr, 3:4]), reads=[Rsm], writes=[Rsm])
                nvb = len(vblocks)
                for g0 in range(0, nvb, 4):
                    grp = vblocks[g0:g0 + 4]
                    ti = cnt["t"] % 2; cnt["t"] += 1
                    for j, (k0, nkb, kvap, regs) in enumerate(grp):
                        op("pe", lambda: nc.tensor.transpose(out=psB[:nkb, ti * 4 + j, :nr], in_=p_all[:nr, k0:k0 + nkb], identity=ident_b[:nr, :nr]),
                           reads=[Rp] + CONST, writes=[RpB], signal=(j == len(grp) - 1))
                    if len(grp) == 4 and all(g[1] == P for g in grp):
                        evac(pT[ti][:, :, :nr], psB[:, ti * 4:ti * 4 + 4, :nr], [RpB], [RpT_[ti]])
                    else:
                        for j, (k0, nkb, kvap, regs) in enumerate(grp):
                            evac(pT[ti][:nkb, j, :nr], psB[:nkb, ti * 4 + j, :nr], [RpB], [RpT_[ti]])
                    for j, (k0, nkb, kvap, regs) in enumerate(grp):
                        last = (g0 + j == nvb - 1)
                        op("pe", lambda: nc.tensor.matmul(psO[:nr, :256], lhsT=pT[ti][:nkb, j, :nr], rhs=kvap, start=(g0 + j == 0), stop=last),
                           reads=[RpT_[ti]] + regs, writes=[RpO], signal=(last or j == len(grp) - 1))

            def o_to_T(nr, src=None, Rsrc=None):
                if src is None:
                    src = psO[:nr, :256]; Rsrc = RpO
                op("act", lambda: nc.scalar.activation(out=on[:nr, :], in_=src, func=AF.Copy, scale=sm[:nr, 2:3]),
                   reads=[Rsrc, Rsm], writes=[Ron])
                for rc in range(2):
                    op("pe", lambda: nc.tensor.transpose(out=psB[:, rc, :nr], in_=on[:nr, rc * P:(rc + 1) * P], identity=ident_b[:nr, :nr]),
                       reads=[Ron] + CONST, writes=[RpB], signal=(rc == 1))
                evac(oT[:, :, :nr], psB[:, 0:2, :nr], [RpB], [RoT])

            def mix_attn(nq):
                op("act", lambda: nc.scalar.activation(out=hin[:nq, :512], in_=psW[:nq, :512], func=AF.Square, accum_out=sm[:nq, 4:5]),
                   reads=[RpW], writes=[Rhin, Rsm])
                rstd_of(sm[:, 4:5], nq, sm[:, 5:6], Rsm, 512)
                op("dve", lambda: nc.vector.tensor_scalar(out=an[:nq, :], in0=psW[:nq, :512], scalar1=sm[:nq, 5:6], scalar2=None, op0=ALU.mult),
                   reads=[RpW, Rsm], writes=[Ran])
                for c in range(4):
                    op("pe", lambda: nc.tensor.transpose(out=psB[:, c, :nq], in_=an[:nq, c * P:(c + 1) * P], identity=ident_b[:nq, :nq]),
                       reads=[Ran] + CONST, writes=[RpB], signal=(c == 3))
                for c in range(4):
                    evac(mixT[:, c, :nq], psB[:, c, :nq], [RpB] + CONST, [Rmix], scale=vec_sb[:, V_AG + c:V_AG + c + 1])

            def mix_conv(nq, gp_c0):
                for c in range(4):
                    eng = "dve"; E = nc.vector
                    op(eng, lambda: E.tensor_scalar(out=yc[:, c, :nq], in0=gpad[:, c, gp_c0:gp_c0 + nq], scalar1=convw_sb[:, c * 31:c * 31 + 1],
                                                    scalar2=vec_sb[:, V_CB + c:V_CB + c + 1], op0=ALU.mult, op1=ALU.add),
                       reads=[Rgp] + CONST, writes=[Ryc])
                    for k in range(1, 31):
                        op(eng, lambda: E.scalar_tensor_tensor(out=yc[:, c, :nq], in0=gpad[:, c, gp_c0 + k:gp_c0 + k + nq],
                                                               scalar=convw_sb[:, c * 31 + k:c * 31 + k + 1], in1=yc[:, c, :nq],
                                                               op0=ALU.mult, op1=ALU.add), reads=[Rgp, Ryc] + CONST, writes=[Ryc])
                op("act", lambda: nc.scalar.activation(out=ybf[:, :, :nq], in_=yc[:, :, :nq], func=AF.Copy), reads=[Ryc], writes=[Rybf])
                for c in range(4):
                    op("pe", lambda: nc.tensor.matmul(psX[:, :nq], lhsT=ones_b[:], rhs=ybf[:, c, :nq], start=(c == 0), stop=(c == 3)),
                       reads=[Rybf] + CONST, writes=[RpX], signal=(c == 3))
                op("dve", lambda: nc.vector.tensor_scalar(out=st3[:, 0, :nq], in0=psX[:, :nq], scalar1=1.0 / 512, scalar2=None, op0=ALU.mult),
                   reads=[RpX], writes=[Rst3])
                op("act", lambda: nc.scalar.activation(out=ybf[:, :, :nq], in_=yc[:, :, :nq], func=AF.Square), reads=[Ryc, RpX], writes=[Rybf])
                for c in range(4):
                    op("pe", lambda: nc.tensor.matmul(psX[:, :nq], lhsT=ones_b[:], rhs=ybf[:, c, :nq], start=(c == 0), stop=(c == 3)),
                       reads=[Rybf] + CONST, writes=[RpX], signal=(c == 3))
                op("dve", lambda: nc.vector.tensor_tensor(out=st3[:, 1, :nq], in0=st3[:, 0, :nq], in1=st3[:, 0, :nq], op=ALU.mult), reads=[Rst3], writes=[Rst3])
                op("dve", lambda: nc.vector.scalar_tensor_tensor(out=st3[:, 1, :nq], in0=psX[:, :nq], scalar=1.0 / 512, in1=st3[:, 1, :nq],
                                                                  op0=ALU.mult, op1=ALU.subtract), reads=[RpX, Rst3], writes=[Rst3])
                op("dve", lambda: nc.vector.tensor_scalar(out=st3[:, 1, :nq], in0=st3[:, 1, :nq], scalar1=EPS, scalar2=None, op0=ALU.add),
                   reads=[Rst3], writes=[Rst3])
                op("act", lambda: nc.scalar.activation(out=st3[:, 1, :nq], in_=st3[:, 1, :nq], func=AF.Sqrt), reads=[Rst3], writes=[Rst3])
                op("dve", lambda: nc.vector.reciprocal(out=st3[:, 1, :nq], in_=st3[:, 1, :nq]), reads=[Rst3], writes=[Rst3])
                for c in range(4):
                    op("dve", lambda: nc.vector.tensor_tensor(out=yc[:, c, :nq], in0=yc[:, c, :nq], in1=st3[:, 0, :nq], op=ALU.subtract),
                       reads=[Ryc, Rst3], writes=[Ryc])
                    op("dve", lambda: nc.vector.tensor_tensor(out=yc[:, c, :nq], in0=yc[:, c, :nq], in1=st3[:, 1, :nq], op=ALU.mult),
                       reads=[Ryc, Rst3], writes=[Ryc])
                    op("act", lambda: nc.scalar.activation(out=yc[:, c, :nq], in_=yc[:, c, :nq], func=AF.Silu,
                                                            bias=vec_sb[:, V_LB + c:V_LB + c + 1], scale=vec_sb[:, V_LG + c:V_LG + c + 1]),
                       reads=[Ryc] + CONST, writes=[Ryc])
                op("act", lambda: nc.scalar.activation(out=ybf[:, :, :nq], in_=yc[:, :, :nq], func=AF.Square), reads=[Ryc, RpX], writes=[Rybf])
                for c in range(4):
                    op("pe", lambda: nc.tensor.matmul(psX[:, :nq], lhsT=ones_b[:], rhs=ybf[:, c, :nq], start=(c == 0), stop=(c == 3)),
                       reads=[Rybf] + CONST, writes=[RpX], signal=(c == 3))
                op("dve", lambda: nc.vector.tensor_scalar(out=st3[:, 2, :nq], in0=psX[:, :nq], scalar1=1.0 / 512, scalar2=EPS, op0=ALU.mult, op1=ALU.add),
                   reads=[RpX], writes=[Rst3])
                op("act", lambda: nc.scalar.activation(out=st3[:, 2, :nq], in_=st3[:, 2, :nq], func=AF.Sqrt), reads=[Rst3], writes=[Rst3])
                op("dve", lambda: nc.vector.reciprocal(out=st3[:, 2, :nq], in_=st3[:, 2, :nq]), reads=[Rst3], writes=[Rst3])
                for c in range(4):
                    op("dve", lambda: nc.vector.scalar_tensor_tensor(out=mixT[:, 4 + c, :nq], in0=yc[:, c, :nq], scalar=vec_sb[:, V_CG + c:V_CG + c + 1],
                                                                      in1=st3[:, 2, :nq], op0=ALU.mult, op1=ALU.mult),
                       reads=[Ryc, Rst3] + CONST, writes=[Rmix])

            def mix_proj(nq, gt):
                load(hin[:nq, :], h_d[gt:gt + nq, :], Rhin, src_regs=[dr(("h", gt))])
                for half in range(2):
                    for kc in range(8):
                        op("pe", lambda: nc.tensor.matmul(psX[:nq, :512], lhsT=mixT[:, kc, :nq], rhs=wout_sb[:, kc, half * 512:(half + 1) * 512],
                                                           start=(kc == 0), stop=(kc == 7)), reads=[Rmix] + MW, writes=[RpX], signal=(kc == 7))
                    op("dve", lambda: nc.vector.tensor_tensor(out=hin[:nq, half * 512:(half + 1) * 512], in0=hin[:nq, half * 512:(half + 1) * 512],
                                                               in1=psX[:nq, :512], op=ALU.add), reads=[Rhin, RpX], writes=[Rhin])
                store(h_d[gt:gt + nq, :], hin[:nq, :], Rhin, dst_regs=[dr(("h", gt))])

            if stage == 2:
                Rh0 = sy.dreg()
                op("sp", lambda: nc.sync.dma_start(out=h_d[NSQ * S:NSQ * S + cfg.nstok, :], in_=hs_in[:, :]),
                   writes=[dr(("h", NSQ * S + t)) for t in range(0, cfg.nstok, 4)], dma=Rh0)
                sego2 = sb(st, "sego2", [32, NCR, 256]); segml = sb(st, "segml", [32, NCR, 2]); Rs2 = sy.dreg()
                segm2 = sb(st, "segm2", [32, NCR]); segl2 = sb(st, "segl2", [32, NCR]); segw2 = sb(st, "segw2", [32, NCR]); Rw2 = Reg()
                osum = sb(st, "osum", [32, 256]); Rosum = Reg()
                cmf2 = sb(st, "cmf2", [P, 16]); Rcm2 = sy.dreg()
                for b in range(NSM):
                    gt = NSQ * S + b * 4
                    r0 = b * NCR * 32
                    load(sego2[:, :, :], pin_o[r0:r0 + NCR * 32, :].rearrange("(c r) f -> r c f", r=32), Rs2)
                    load(segml[:, :, :], pin_ml[r0:r0 + NCR * 32, :].rearrange("(c r) t -> r c t", r=32), Rs2)
                    op("dve", lambda: nc.vector.tensor_copy(out=segm2[:, :], in_=segml[:, :, 0]), reads=[Rs2], writes=[Rw2])
                    op("dve", lambda: nc.vector.tensor_copy(out=segl2[:, :], in_=segml[:, :, 1]), reads=[Rs2], writes=[Rw2])
                    op("dve", lambda: nc.vector.tensor_reduce(out=sm[:32, 0:1], in_=segm2[:, :], axis=AX.X, op=ALU.max), reads=[Rw2], writes=[Rsm])
                    op("dve", lambda: nc.vector.tensor_scalar(out=sm[:32, 1:2], in0=sm[:32, 0:1], scalar1=-SCALE, scalar2=None, op0=ALU.mult),
                       reads=[Rsm], writes=[Rsm])
                    op("act", lambda: nc.scalar.activation(out=segw2[:, :], in_=segm2[:, :], func=AF.Exp, bias=sm[:32, 1:2], scale=SCALE),
                       reads=[Rw2, Rsm], writes=[Rw2])
                    op("dve", lambda: nc.vector.tensor_tensor(out=segl2[:, :], in0=segl2[:, :], in1=segw2[:, :], op=ALU.mult), reads=[Rw2], writes=[Rw2])
                    op("dve", lambda: nc.vector.tensor_reduce(out=sm[:32, 3:4], in_=segl2[:, :], axis=AX.X, op=ALU.add), reads=[Rw2], writes=[Rsm])
                    op("dve", lambda: nc.vector.reciprocal(out=sm[:32, 2:3], in_=sm[:32, 3:4]), reads=[Rsm], writes=[Rsm])
                    op("dve", lambda: nc.vector.tensor_scalar(out=osum[:, :], in0=sego2[:, 0, :], scalar1=segw2[:, 0:1], scalar2=None, op0=ALU.mult),
                       reads=[Rs2, Rw2], writes=[Rosum])
                    for c_ in range(1, NCR):
                        op("dve", lambda: nc.vector.scalar_tensor_tensor(out=osum[:, :], in0=sego2[:, c_, :], scalar=segw2[:, c_:c_ + 1],
                                                                          in1=osum[:, :], op0=ALU.mult, op1=ALU.add), reads=[Rs2, Rw2, Rosum], writes=[Rosum])
                    o_to_T(32, src=osum[:32, :], Rsrc=Rosum)
                    for h in range(8):
                        for rc in range(2):
                            op("pe", lambda: nc.tensor.matmul(psW[:4, h * 64:(h + 1) * 64], lhsT=oT[:, rc, h * 4:(h + 1) * 4], rhs=wuv_sb[:, rc, h * 64:(h + 1) * 64],
                                                               start=(rc == 0), stop=(rc == 1)), reads=[RoT] + MW, writes=[RpW], signal=(h == 7 and rc == 1))
                    mix_attn(4)
                    load(cmf2[:, :], cm_in[b * P:(b + 1) * P, :], Rcm2)
                    op("dve", lambda: nc.vector.tensor_copy(out=mixT[:, 4:8, :4], in_=cmf2[:, :].rearrange("p (c q) -> p c q", c=4)), reads=[Rcm2], writes=[Rmix])
                    mix_proj(4, gt)
                sy.barrier()
                return
            for s in range(NSQ):
                c0 = s * (S + 30)
                op("sp", lambda: nc.sync.dma_start(out=glu_d.ap().rearrange("(c p) t -> p c t", p=P)[:, :, c0:c0 + 30],
                                                   in_=zero_sb[:, 0:30].unsqueeze(1).to_broadcast([P, 4, 30])),
                   reads=[R_z], writes=[dr(("glu", s))], dma=R_m)
            gview = glu_d.ap().rearrange("(c p) t -> p c t", p=P)
            def mix_out(nq, gt, a_reads, gp_c0):
                mix_attn(nq); mix_conv(nq, gp_c0); mix_proj(nq, gt)

            for s in range(NSQ):
                g0s = s * S
                load(cT[:], cT_d.ap().rearrange("(rc p) t -> p rc t", p=P)[:, :, g0s:g0s + S], RcT,
                     src_regs=[dr(("cT", g0s + t)) for t in range(0, S, 512)])
                load(krT[:], krT_d[:, g0s:g0s + S], RkrT, src_regs=[dr(("krT", g0s + t)) for t in range(0, S, 512)])
                nfull = S // P
                kvregs = [dr(("kv", g0s + t)) for t in range(0, S, P)]
                if nfull:
                    load(kvb[:, :nfull, :], ckvb_d[g0s:g0s + nfull * P, :].rearrange("(b p) r -> p b r", p=P), Rkvb, src_regs=kvregs)
                if S % P:
                    load(kvb[:S % P, nfull, :], ckvb_d[g0s + nfull * P:g0s + S, :], Rkvb, src_regs=kvregs)
                with nc.allow_non_contiguous_dma(reason="conv state transpose"):
                    for c in range(4):
                        cc = s * (S + 30) + 30 + S - 30
                        op("sp", lambda: nc.sync.dma_start(out=convp[s * 30:(s + 1) * 30, c * P:(c + 1) * P],
                                                           in_=glu_d[c * P:(c + 1) * P, cc:cc + 30].rearrange("c t -> t c")),
                           reads=[dr(("glu", s))], writes=[], dma=R_m)
                        out_events.append((R_m.dsem, sy.dcnt[id(R_m.dsem)]))
                for i in range(NB):
                    nq = min(P, S - i * P); gt = g0s + i * P
                    load(qcT[:, :, :nq], qcT_d.ap().rearrange("(rc p) t -> p rc t", p=P)[:, :, gt:gt + nq], RqcT,
                         src_regs=[dr(("qc", g0s + (i * P // 512) * 512))])
                    q_proj(nq, i, False)
                    gc0 = s * (S + 30) + i * P
                    load(gpad[:, :, :30 + nq], gview[:, :, gc0:gc0 + 30 + nq], Rgp, src_regs=[dr(("glu", s))])
                    nkeys = i * P + nq
                    for h in range(8):
                        chunks = []
                        for k0 in range(0, nkeys, 512):
                            nk = min(512, nkeys - k0)
                            mask = None
                            if k0 + nk == nkeys:
                                mask = (maskc[:nq, :nq], i * P - k0, nq)
                            chunks.append((k0, nk, (lambda rc, k0=k0, nk=nk: cT[:, rc, k0:k0 + nk]), krT[:, k0:k0 + nk], [RcT, RkrT], mask))
                        vbl = [(b * P, min(P, nkeys - b * P), kvb[:min(P, nkeys - b * P), b, :], [Rkvb]) for b in range(i + 1)]
                        attn_rows(nq, (lambda rc, h=h: qlat[:, h, rc, :nq]), qr[:, h, :nq], chunks, vbl, [Rql, Rqr])
                        o_to_T(nq)
                        for rc in range(2):
                            op("pe", lambda: nc.tensor.matmul(psW[:nq, h * 64:(h + 1) * 64], lhsT=oT[:, rc, :nq], rhs=wuv_sb[:, rc, h * 64:(h + 1) * 64],
                                                               start=(rc == 0), stop=(rc == 1)), reads=[RoT] + MW, writes=[RpW], signal=(rc == 1))
                    mix_out(nq, gt, [], 0)
            cmf = sb(st, "cmf", [P, 16]); Rcmf = sy.dreg()
            for b in range(NSM):
                load(prev[:30, :], stconv[b * 30:(b + 1) * 30, :], Rprev)
                for c in range(4):
                    op("pe", lambda: nc.tensor.matmul(psX[:, c * 32:c * 32 + 30], lhsT=prev[:30, c * P:(c + 1) * P], rhs=ident_f[:30, :30],
                                                       start=True, stop=True), reads=[Rprev] + CONST, writes=[RpX], signal=(c == 3))
                gs0 = NSQ * (S + 30) + b * 34
                load(gpad[:, :, 30:34], gview[:, :, gs0 + 30:gs0 + 34], Rgp, src_regs=[dr(("glus", b))])
                op("dve", lambda: nc.vector.tensor_copy(out=gpad[:, :, 0:30], in_=psX[:, 0:128].rearrange("p (c t) -> p c t", c=4)[:, :, 0:30]),
                   reads=[RpX], writes=[Rgp])
                op("sp", lambda: nc.sync.dma_start(out=convs[b * 30:b * 30 + 26, :], in_=stconv[b * 30 + 4:(b + 1) * 30, :]), writes=[], dma=R_m)
                out_events.append((R_m.dsem, sy.dcnt[id(R_m.dsem)]))
                with nc.allow_non_contiguous_dma(reason="conv state transpose"):
                    for c in range(4):
                        op("sp", lambda: nc.sync.dma_start(out=convs[b * 30 + 26:(b + 1) * 30, c * P:(c + 1) * P],
                                                           in_=glu_d[c * P:(c + 1) * P, gs0 + 30:gs0 + 34].rearrange("c t -> t c")),
                           reads=[dr(("glus", b))], writes=[], dma=R_m)
                        out_events.append((R_m.dsem, sy.dcnt[id(R_m.dsem)]))
                mix_conv(4, 0)
                op("dve", lambda: nc.vector.tensor_copy(out=cmf[:, :].rearrange("p (c q) -> p c q", c=4), in_=mixT[:, 4:8, :4]), reads=[Rmix], writes=[Rcmf])
                store(cm_o[b * P:(b + 1) * P, :], cmf[:, :], Rcmf, final=True)
            op("dve", lambda: nc.vector.memset(qrs[32:33, :], 1.0), writes=[Rqls])
            op("dve", lambda: nc.vector.memset(kvf[0][:], 0.0), writes=[Rkvf[0]])
            op("dve", lambda: nc.vector.memset(kvf[1][:], 0.0), writes=[Rkvf[1]])
            DPS = GP // NSEG
            bc_reg = nc.gpsimd.to_reg(cfg.npool_sh * GP - 1)
            for b in range(NSALL):
                gt = NSQ * S + cfg.nstok + b * 4
                load(qcT[:, :, :4], qcT_d.ap().rearrange("(rc p) t -> p rc t", p=P)[:, :, gt:gt + 4], RqcT,
                     src_regs=[dr(("qc", NSQ * S + cfg.nstok + (b * 4 // 512) * 512))])
                q_proj(4, NB, True)
                for rc in range(2):
                    evac(qls[:, rc, :].rearrange("p (h q) -> p h q", h=8), qlat[:, :, rc, :4], [Rql], [Rqls])
                evac(qrs[0:32, :].rearrange("p (h q) -> p h q", h=8), qr[:, :, :4], [Rqr], [Rqls])
                ql_s = (lambda rc: qls[:, rc, :]); qr_s = qrs[:, :]
                for sg_ in range(NSEG):
                    vbl = []; k0 = 0
                    for dd in range(DPS):
                        kk = sg_ * DPS + dd
                        ci = cnt["c"] % 2; cnt["c"] += 1
                        col = kk * NSALL + b
                        op("pool", lambda: nc.gpsimd.indirect_dma_start(out=kvf[ci][:NPG, :], out_offset=None, in_=poolx[:, :],
                                                                        in_offset=bass.IndirectOffsetOnAxis(ap=idxk[:NPG, col:col + 1], axis=0),
                                                                        bounds_check=bc_reg, oob_is_err=False),
                           reads=[R_idx], writes=[Rkvf[ci]], dma=Rkvf[ci])
                        kvv = kvf[ci][:NPG, :].rearrange("p (t f) -> p t f", f=288)
                        op("dve", lambda: nc.vector.tensor_copy(out=kvb[:NPG, dd * TPD:(dd + 1) * TPD, :], in_=kvv[:, :, 0:256]),
                           reads=[Rkvf[ci]], writes=[Rkvb])
                        op("act", lambda: nc.scalar.activation(out=krb8[:NPG, :, 0:32], in_=kvv[:, :, 256:288], func=AF.Copy),
                           reads=[Rkvf[ci]], writes=[Rkrb4])
                        op("dve", lambda: nc.vector.tensor_copy(out=krb8[:NPG, :, 32:33], in_=biasb[:NPG, b:b + 1].unsqueeze(1).to_broadcast([NPG, TPD, 1])),
                           reads=[R_idx], writes=[Rkrb4])
                        for hf in range(TPD // 4):
                            c2 = cnt["k"] % 2; cnt["k"] += 1
                            for j in range(4):
                                blk = dd * TPD + hf * 4 + j
                                for rc in range(2):
                                    op("pe", lambda: nc.tensor.transpose(out=psB[:, rc * 4 + j, :NPG], in_=kvb[:NPG, blk, rc * P:(rc + 1) * P],
                                                                         identity=ident_b[:NPG, :NPG]),
                                       reads=[Rkvb] + CONST, writes=[RpB], signal=(j == 3 and rc == 1))
                            evac(cTc[c2][:, :, :4 * NPG].rearrange("p r (j t) -> p r j t", t=NPG),
                                 psB[:, :, :NPG].rearrange("p (r j) t -> p r j t", r=2), [RpB], [RcTc[c2]])
                            for j in range(4):
                                op("pe", lambda: nc.tensor.transpose(out=psB[:33, j, :NPG], in_=krb8[:NPG, hf * 4 + j, :], identity=ident_b[:NPG, :NPG]),
                                   reads=[Rkrb4] + CONST, writes=[RpB], signal=(j == 3))
                            evac(krTc[c2][:, :4 * NPG].rearrange("p (j t) -> p j t", t=NPG), psB[:33, 0:4, :NPG], [RpB], [RkrTc[c2]])
                            score_chunk(32, ql_s, qr_s, (k0, 4 * NPG, (lambda rc, c2=c2: cTc[c2][:, rc, :4 * NPG]), krTc[c2][:, :4 * NPG],
                                                         [RcTc[c2], RkrTc[c2]], None), [Rqls])
                            for j in range(4):
                                vbl.append((k0 + j * NPG, NPG, kvb[:NPG, dd * TPD + hf * 4 + j, :], [Rkvb]))
                            k0 += 4 * NPG
                    if sg_ == NSEG - 1:
                        load(cT[:, :, :4], cT_d.ap().rearrange("(rc p) t -> p rc t", p=P)[:, :, gt:gt + 4], RcT,
                             src_regs=[dr(("cT", NSQ * S + cfg.nstok + (b * 4 // 512) * 512))])
                        load(krTn[0:32, :4], krT_d[:, gt:gt + 4], RkrT, src_regs=[dr(("krT", NSQ * S + cfg.nstok + (b * 4 // 512) * 512))])
                        load(ownf[0:1, :], ownb[b:b + 1, :], Rown)
                        op("dve", lambda: nc.vector.tensor_copy(out=ownr[0:1, :], in_=ownf[0:1, :]), reads=[Rown], writes=[Rownr])
                        op("sp", lambda: nc.sync.dma_start(out=krTn[32:33, :4], in_=ownr[0:1, :]), reads=[Rownr], writes=[RkrT], dma=RkrT)
                        load(kvb[:4, DPS * TPD, :], ckvb_d[gt:gt + 4, :], Rkvb, src_regs=[dr(("kv", NSQ * S + cfg.nstok + (b * 4 // P) * P))])
                        score_chunk(32, ql_s, qr_s, (k0, 4, (lambda rc: cT[:, rc, :4]), krTn[:, :4], [RcT, RkrT], (masks[:32, :4], 0, 4)), [Rqls])
                        vbl.append((k0, 4, kvb[:4, DPS * TPD, :], [Rkvb]))
                        k0 += 4
                    softmax_pv(32, k0, vbl)
                    op("dve", lambda: nc.vector.tensor_copy(out=segm[:32, sg_:sg_ + 1], in_=sm[:32, 0:1]), reads=[Rsm], writes=[Rseg])
                    op("dve", lambda: nc.vector.tensor_copy(out=segl[:32, sg_:sg_ + 1], in_=sm[:32, 3:4]), reads=[Rsm], writes=[Rseg])
                    op("act", lambda: nc.scalar.activation(out=sego[:32, sg_, :], in_=psO[:32, :256], func=AF.Copy), reads=[RpO], writes=[Rseg])
                op("dve", lambda: nc.vector.tensor_reduce(out=pml[:32, 0:1], in_=segm[:32, :NSEG], axis=AX.X, op=ALU.max), reads=[Rseg], writes=[Rpml])
                op("dve", lambda: nc.vector.tensor_scalar(out=sm[:32, 1:2], in0=pml[:32, 0:1], scalar1=-SCALE, scalar2=None, op0=ALU.mult), reads=[Rpml], writes=[Rsm])
                op("act", lambda: nc.scalar.activation(out=segw[:32, :NSEG], in_=segm[:32, :NSEG], func=AF.Exp, bias=sm[:32, 1:2], scale=SCALE),
                   reads=[Rseg, Rsm], writes=[Rseg])
                op("dve", lambda: nc.vector.tensor_tensor(out=segl[:32, :NSEG], in0=segl[:32, :NSEG], in1=segw[:32, :NSEG], op=ALU.mult), reads=[Rseg], writes=[Rseg])
                op("dve", lambda: nc.vector.tensor_reduce(out=pml[:32, 1:2], in_=segl[:32, :NSEG], axis=AX.X, op=ALU.add), reads=[Rseg], writes=[Rpml])
                op("dve", lambda: nc.vector.tensor_scalar(out=po[:32, :], in0=sego[:32, 0, :], scalar1=segw[:32, 0:1], scalar2=None, op0=ALU.mult),
                   reads=[Rseg], writes=[Rpo])
                for sg_ in range(1, NSEG):
                    op("dve", lambda: nc.vector.scalar_tensor_tensor(out=po[:32, :], in0=sego[:32, sg_, :], scalar=segw[:32, sg_:sg_ + 1],
                                                                      in1=po[:32, :], op0=ALU.mult, op1=ALU.add), reads=[Rseg, Rpo], writes=[Rpo])
                store(part_o[b * 32:(b + 1) * 32, :], po[:32, :], Rpo, final=True)
                store(part_ml[b * 32:(b + 1) * 32, :], pml[:32, 0:2], Rpml, final=True)
            sy.barrier()

    mixer_phase()
    ffn_phase("2")
    sy._wait("sp", out_events)
    sy.barrier()
    return nc, es


_BUILD_CACHE = {}


def _common_consts(cfg, inp, c):
    f32 = np.float32
    m = {}
    vecs = np.zeros((P, 64), f32)

    def fm(v, n):
        return np.asarray(v, f32).reshape(n, P).T
    vecs[:, 0:8] = fm(inp["ffn1_norm"][0], 8); vecs[:, 8:16] = fm(inp["mix_norm"][0], 8); vecs[:, 16:24] = fm(inp["ffn2_norm"][0], 8)
    vecs[:, 24:28] = fm(inp["attn_grp_norm"][0], 4); vecs[:, 28:32] = fm(inp["conv_grp_norm"][0], 4)
    vecs[:, 32:36] = fm(inp["conv_b"][0], 4); vecs[:, 36:40] = fm(inp["conv_ln_g"][0], 4); vecs[:, 40:44] = fm(inp["conv_ln_b"][0], 4)
    vecs[:, 45] = np.float32(c * cfg.npool_sh)
    m["vecs"] = vecs
    m["bvec"] = np.concatenate([np.asarray(inp["q_norm"][0], f32), np.asarray(inp["kv_norm"][0], f32),
                                np.asarray(inp["final_norm"], f32)]).reshape(1, 1536)
    m["identf"] = np.eye(P, dtype=f32)
    return m


def _host_layout1(cfg, inp, core):
    D, S, NB = cfg.D, cfg.S, cfg.NB
    f32 = np.float32
    c = core
    m = _common_consts(cfg, inp, c)
    m["xp"] = np.ascontiguousarray(inp["x_prompt"][c * cfg.nseq:(c + 1) * cfg.nseq]).reshape(cfg.nseq * cfg.seq, D)
    m["xs"] = np.ascontiguousarray(inp["x_sample"][c * cfg.nsamp:(c + 1) * cfg.nsamp]).reshape(cfg.nstok, D)
    m["xsa"] = np.ascontiguousarray(inp["x_sample"]).reshape(cfg.nsall * 4, D)
    pk = inp["cache_kv_latent"][0][c * cfg.npool_sh:(c + 1) * cfg.npool_sh]
    pr = inp["cache_k_rope"][0][c * cfg.npool_sh:(c + 1) * cfg.npool_sh]
    m["poolx"] = np.concatenate([pk, pr], axis=-1).reshape(cfg.npool_sh * cfg.gp, cfg.tpd * 288)
    m["stconv"] = np.ascontiguousarray(inp["state_conv"][0, c * cfg.nsamp:(c + 1) * cfg.nsamp]).reshape(cfg.nsamp * 30, 512)
    m["ptabT"] = np.ascontiguousarray(inp["page_table"].T).astype(np.int32)
    own = np.full((cfg.nsall, 4), -30000.0, f32); own[c * cfg.nsamp:(c + 1) * cfg.nsamp] = 0.0
    m["ownb"] = own
    m["meta"] = np.ascontiguousarray(inp["meta_tokens"]).astype(f32)
    for f in ("1", "2"):
        m["wg" + f] = np.ascontiguousarray(inp["ffn%s_w_gate" % f][0]); m["wu" + f] = np.ascontiguousarray(inp["ffn%s_w_up" % f][0])
        m["wd" + f] = np.ascontiguousarray(inp["ffn%s_w_down" % f][0])
    m["win"] = np.ascontiguousarray(inp["w_in"][0]); m["wuq"] = np.ascontiguousarray(inp["w_uq"][0])
    m["wuk"] = np.ascontiguousarray(inp["w_uk"][0]).reshape(256, 512); m["wuv"] = np.ascontiguousarray(inp["w_uv"][0]).reshape(256, 512)
    m["wout"] = np.ascontiguousarray(inp["w_out"][0])
    cw = np.asarray(inp["conv_w"][0], f32)
    m["convw"] = np.ascontiguousarray(cw.T.reshape(4, P, 31).transpose(1, 0, 2)).reshape(P, 4 * 31)
    qi = np.arange(P)[:, None]; ki = np.arange(P)[None, :]
    m["maskc"] = np.where(ki <= qi, 0.0, -30000.0).astype(f32)
    r = np.arange(32)[:, None] % 4; k4 = np.arange(4)[None, :]
    m["masks"] = np.where(k4 <= r, 0.0, -30000.0).astype(f32)
    half = 16
    inv = (np.float32(10000.0) ** (-np.arange(half, dtype=f32) / np.float32(half))).astype(f32)
    tab = np.zeros((P, NB + 1, 32), f32)
    pos = (np.arange(NB * P).reshape(NB, P).T).astype(f32)
    ang = pos[:, :, None] * inv[None, None, :]
    tab[:, :NB, :16] = np.cos(ang); tab[:, :NB, 16:] = np.sin(ang)
    sp = (cfg.past + (np.arange(P) % 4)).astype(f32)
    angs = sp[:, None] * inv[None, :]
    tab[:, NB, :16] = np.cos(angs); tab[:, NB, 16:] = np.sin(angs)
    m["ropet"] = tab.reshape(P, (NB + 1) * 32)
    return m


def _host_layout2(cfg, inp, core, rs1):
    c = core
    m = _common_consts(cfg, inp, c)
    m["wg2"] = np.ascontiguousarray(inp["ffn2_w_gate"][0]); m["wu2"] = np.ascontiguousarray(inp["ffn2_w_up"][0])
    m["wd2"] = np.ascontiguousarray(inp["ffn2_w_down"][0])
    m["wuv"] = np.ascontiguousarray(inp["w_uv"][0]).reshape(256, 512); m["wout"] = np.ascontiguousarray(inp["w_out"][0])
    b0 = c * cfg.nsamp
    po = np.stack([np.asarray(r["part_o"]).reshape(cfg.nsall, 32, 256)[b0:b0 + cfg.nsamp] for r in rs1], axis=1)
    pm = np.stack([np.asarray(r["part_ml"]).reshape(cfg.nsall, 32, 2)[b0:b0 + cfg.nsamp] for r in rs1], axis=1)
    m["pin_o"] = np.ascontiguousarray(po).reshape(cfg.nsamp * cfg.ncores * 32, 256)
    m["pin_ml"] = np.ascontiguousarray(pm).reshape(cfg.nsamp * cfg.ncores * 32, 2)
    m["hs_in"] = np.ascontiguousarray(rs1[c]["hs_o"]); m["cm_in"] = np.ascontiguousarray(rs1[c]["cm_o"])
    return m


def kernel(**inp):
    cfg = Cfg()
    inp = {k: np.asarray(v) for k, v in inp.items()}
    if "nc1" not in _BUILD_CACHE:
        _BUILD_CACHE["nc1"] = build(cfg, 1)
        _BUILD_CACHE["nc2"] = build(cfg, 2)
    nc1, _e1 = _BUILD_CACHE["nc1"]; nc2, _e2 = _BUILD_CACHE["nc2"]
    cores = list(range(cfg.ncores))
    res1 = run_bass_kernel_spmd(nc1, [_host_layout1(cfg, inp, c) for c in cores], core_ids=cores).results
    res2 = run_bass_kernel_spmd(nc2, [_host_layout2(cfg, inp, c, res1) for c in cores], core_ids=cores).results
    return assemble(cfg, res1, res2)


def assemble(cfg, rs, rs2):
    S = cfg.S; NSQ = cfg.nseq
    yp = np.concatenate([r["yp"].reshape(NSQ, cfg.seq, cfg.D) for r in rs], 0)
    ys = np.concatenate([r["ys"].reshape(cfg.nsamp, 4, cfg.D) for r in rs2], 0)
    ckv = [np.asarray(r["ckv_o"]) for r in rs]; kr = [np.asarray(r["kr_o"]) for r in rs]
    ckp = np.concatenate([a[:NSQ * S].reshape(NSQ, S, 256) for a in ckv], 0)[None]
    krp = np.concatenate([a[:NSQ * S].reshape(NSQ, S, 32) for a in kr], 0)[None]
    cks = np.concatenate([a[NSQ * S:].reshape(cfg.nsamp, 4, 256) for a in ckv], 0)[None]
    krs = np.concatenate([a[NSQ * S:].reshape(cfg.nsamp, 4, 32) for a in kr], 0)[None]
    cvp = np.concatenate([r["convp"].reshape(NSQ, 30, 512) for r in rs], 0)[None]
    cvs = np.concatenate([r["convs"].reshape(cfg.nsamp, 30, 512) for r in rs], 0)[None]
    f = lambda a: np.ascontiguousarray(a, dtype=np.float32)
    return (f(yp), f(ys), f(ckp), f(krp), f(cvp), f(cks), f(krs), f(cvs))
```

```python
import numpy as np
from contextlib import ExitStack
import concourse.bass as bass
import concourse.mybir as mybir
from concourse.bass_utils import run_bass_kernel_spmd

F32 = mybir.dt.float32
BF16 = mybir.dt.bfloat16
I32 = mybir.dt.int32
AF = mybir.ActivationFunctionType
ALU = mybir.AluOpType
AX = mybir.AxisListType
P = 128
EPS = 1e-6


class Cfg:
    def __init__(self, ncores=8, nseq=4, seq=2048, nsamp=16, npg=128, npool_sh=2560, dff=2816, gp=16):
        self.gp = gp; self.pt = 128 // gp; self.nseg = 4
        self.ncores = ncores; self.nseq = nseq; self.seq = seq; self.S = seq + 16
        self.nsamp = nsamp; self.npg = npg; self.npool_sh = npool_sh; self.dff = dff
        self.D = 1024; self.past = npg * 128
        self.NB = -(-self.S // P)
        self.nstok = nsamp * 4
        self.nsall = ncores * nsamp
        self.T = nseq * self.S + self.nstok + self.nsall * 4
        self.tpd = 128 // gp
        self.nj = dff // P


class Reg:
    __slots__ = ("w", "r", "dsem", "const", "subs")

    def __init__(self, dsem=None, const=False, subs=None):
        self.w = None; self.r = {}; self.dsem = dsem; self.const = const; self.subs = subs


class Sync:
    def __init__(self, nc, es):
        self.nc = nc; self.es = es
        self.E = {"pe": nc.tensor, "act": nc.scalar, "dve": nc.vector, "pool": nc.gpsimd, "sp": nc.sync}
        self.sem = {}; self.cnt = {}; self.known = {e: {} for e in self.E}
        for e in ("pe", "act", "dve", "pool"):
            self.sem[e] = es.enter_context(nc.semaphore("sem_" + e)); self.cnt[e] = 0
        self.dcnt = {}; self.dsems = []; self.nd = 0

    def dreg(self, const=False):
        s = self.es.enter_context(self.nc.semaphore("dsem%d" % self.nd)); self.nd += 1
        self.dcnt[id(s)] = 0; self.dsems.append(s)
        return Reg(dsem=s, const=const)

    def _wait(self, eng, evs):
        need = {}
        for (s, v) in evs:
            k = id(s)
            if k not in need or need[k][1] < v:
                need[k] = (s, v)
        kn = self.known[eng]
        for k, (s, v) in need.items():
            if kn.get(k, 0) < v:
                self.E[eng].wait_ge(s, v); kn[k] = v

    @staticmethod
    def _flat(regs):
        out = []
        for r in regs:
            if r.subs is not None: out.extend(r.subs)
            else: out.append(r)
        return out

    def op(self, eng, fn, reads=(), writes=(), signal=True, dma=None):
        reads = self._flat(reads); writes = self._flat(writes)
        evs = []
        for r in reads:
            if r.w is not None: evs.append(r.w)
        for w in writes:
            if w.w is not None: evs.append(w.w)
            evs.extend(w.r.values())
        if eng == "pe":
            evs = [e for e in evs if e[0] is not self.sem["pe"]]
        self._wait(eng, evs)
        inst = fn()
        if not signal:
            return inst
        if dma is not None:
            s = dma.dsem; self.dcnt[id(s)] += 16; v = self.dcnt[id(s)]; inst.then_inc(s, 16)
        else:
            s = self.sem[eng]; self.cnt[eng] += 1; v = self.cnt[eng]; inst.then_inc(s, 1)
        ev = (s, v)
        for w in writes:
            w.w = ev; w.r = {}
        for r in reads:
            if not r.const:
                r.r[id(s)] = ev
        return inst

    def barrier(self):
        evs = [(self.sem[e], self.cnt[e]) for e in self.sem if self.cnt[e] > 0]
        evs += [(s, self.dcnt[id(s)]) for s in self.dsems if self.dcnt[id(s)] > 0]
        for e in self.E:
            self._wait(e, evs)


def build(cfg, stage=1):
    nc = bass.Bass("TRN2", target_bir_lowering=False)
    es = ExitStack()
    sy = Sync(nc, es)
    D, S, NB, T, DFF, NJ = cfg.D, cfg.S, cfg.NB, cfg.T, cfg.dff, cfg.nj
    NSQ, NSM, NPG = cfg.nseq, cfg.nsamp, cfg.npg
    SCALE = 96.0 ** -0.5

    def din(name, shape, dt=F32):
        return nc.dram_tensor(name, list(shape), dt, kind="ExternalInput")

    def dout(name, shape, dt=F32):
        return nc.dram_tensor(name, list(shape), dt, kind="ExternalOutput")

    def dscr(name, shape, dt):
        return nc.dram_tensor(name, list(shape), dt, kind="Internal")

    NSALL = cfg.nsall; GP = cfg.gp; TPD = cfg.tpd; TOUT = NSQ * S + cfg.nstok
    NCR = cfg.ncores
    if stage == 2:
        wsrc = {}
        for k_, shp in (("wg2", [D, DFF]), ("wu2", [D, DFF]), ("wd2", [DFF, D]), ("wuv", [256, 512]), ("wout", [D, D])):
            wsrc[k_] = din(k_, shp)
        vecs = din("vecs", [P, 64]); bvec = din("bvec", [1, 1536]); identf = din("identf", [P, P])
        pin_o = din("pin_o", [NSM * NCR * 32, 256]); pin_ml = din("pin_ml", [NSM * NCR * 32, 2])
        hs_in = din("hs_in", [cfg.nstok, D]); cm_in = din("cm_in", [NSM * P, 16])
        ys = dout("ys", [cfg.nstok, D])
        xp = xs = xsa = poolx = stconv = ptabT = ownb = meta = convw = maskc_d = masks_d = ropet = None
        yp = ckv_o = kr_o = convp = convs = part_o = part_ml = hs_o = cm_o = None
    else:
        xp = din("xp", [NSQ * cfg.seq, D]); xs = din("xs", [cfg.nstok, D]); xsa = din("xsa", [NSALL * 4, D])
        poolx = din("poolx", [cfg.npool_sh * GP, TPD * 288])
        stconv = din("stconv", [NSM * 30, 512]); ptabT = din("ptabT", [NPG, NSALL], I32)
        ownb = din("ownb", [NSALL, 4])
        meta = din("meta", [16, D])
        wsrc = {}
        for f in ("1", "2"):
            wsrc["wg" + f] = din("wg" + f, [D, DFF]); wsrc["wu" + f] = din("wu" + f, [D, DFF])
            wsrc["wd" + f] = din("wd" + f, [DFF, D])
        wsrc["win"] = din("win", [D, 1568]); wsrc["wuq"] = din("wuq", [256, 768])
        wsrc["wuk"] = din("wuk", [256, 512]); wsrc["wuv"] = din("wuv", [256, 512]); wsrc["wout"] = din("wout", [D, D])
        vecs = din("vecs", [P, 64])
        convw = din("convw", [P, 4 * 31])
        bvec = din("bvec", [1, 1536])
        identf = din("identf", [P, P]); maskc_d = din("maskc", [P, P]); masks_d = din("masks", [32, 4])
        ropet = din("ropet", [P, (NB + 1) * 32])

        yp = dout("yp", [NSQ * cfg.seq, D]); ys = None
        ckv_o = dout("ckv_o", [TOUT, 256]); kr_o = dout("kr_o", [TOUT, 32])
        convp = dout("convp", [NSQ * 30, 512]); convs = dout("convs", [NSM * 30, 512])
        part_o = dout("part_o", [NSALL * 32, 256]); part_ml = dout("part_ml", [NSALL * 32, 2])
        hs_o = dout("hs_o", [cfg.nstok, D]); cm_o = dout("cm_o", [NSM * P, 16])

    wb = {k: dscr(k + "_b", v.shape, BF16) for k, v in wsrc.items()}
    h_d = dscr("h_d", [T, D], F32)
    qcT_d = dscr("qcT_d", [256, T], BF16); cT_d = dscr("cT_d", [256, T], BF16)
    ckvb_d = dscr("ckvb_d", [T, 256], BF16); krT_d = dscr("krT_d", [32, T], BF16)
    GW = NSQ * (S + 30) + NSM * 34
    glu_d = dscr("glu_d", [512, GW], F32)
    NCOLI = GP * NSALL

    uid = [0]

    def sb(st, name, shape, dt=F32):
        uid[0] += 1
        return st.enter_context(nc.sbuf_tensor("s%d_%s" % (uid[0], name), list(shape), dt))

    def ps(st, name, shape, dt=F32):
        uid[0] += 1
        return st.enter_context(nc.psum_tensor("p%d_%s" % (uid[0], name), list(shape), dt))

    op = sy.op
    dram_regs = {}

    def dr(key):
        if key not in dram_regs: dram_regs[key] = Reg()
        return dram_regs[key]

    out_events = []

    def load(dst_ap, src_ap, reg, src_regs=(), eng="sp"):
        return op(eng, lambda: sy.E[eng].dma_start(out=dst_ap, in_=src_ap), reads=src_regs, writes=[reg], dma=reg)

    def store(dst_ap, src_ap, reg, dst_regs=(), final=False, eng="sp"):
        i = op(eng, lambda: sy.E[eng].dma_start(out=dst_ap, in_=src_ap), reads=[reg], writes=dst_regs, dma=reg)
        if final: out_events.append((reg.dsem, sy.dcnt[id(reg.dsem)]))
        return i

    cs = es
    ident_f = sb(cs, "ident_f", [P, P]); ident_b = sb(cs, "ident_b", [P, P], BF16)
    ones_b = sb(cs, "ones_b", [P, P], BF16); maskc = sb(cs, "maskc_sb", [P, P], BF16); masks = sb(cs, "masks_sb", [32, 4], BF16)
    mtmp = sb(cs, "mtmp", [P, P]); mtmp2 = sb(cs, "mtmp2", [32, 4])
    vec_sb = sb(cs, "vec_sb", [P, 64]); convw_sb = sb(cs, "convw_sb", [P, 4 * 31]); bvec_sb = sb(cs, "bvec_sb", [P, 1536])
    rope_sb = sb(cs, "rope_sb", [P, (NB + 1) * 32])
    idxk = sb(cs, "idxk", [P, NCOLI], I32); ptT = sb(cs, "ptT", [P, NSALL], I32)
    fA = sb(cs, "fA", [P, NSALL]); fV = sb(cs, "fV", [P, NSALL]); fN = sb(cs, "fN", [P, NSALL]); fT = sb(cs, "fT", [P, NSALL])
    biasb = sb(cs, "biasb", [P, NSALL], BF16); R_idxf = Reg()
    zero_sb = sb(cs, "zero_sb", [P, 32])
    R_c = sy.dreg(const=True); R_c2 = Reg(const=True); R_idx = sy.dreg(); R_m = sy.dreg(); R_z = Reg()
    load(ident_f[:], identf.ap(), R_c); load(vec_sb[:], vecs.ap(), R_c)
    load(bvec_sb[:], bvec.ap().partition_broadcast(P), R_c)
    if stage == 1:
        load(convw_sb[:], convw.ap(), R_c); load(rope_sb[:], ropet.ap(), R_c)
        load(mtmp[:], maskc_d.ap(), R_m); load(mtmp2[:], masks_d.ap(), R_m)
        load(ptT[:NPG, :], ptabT.ap(), R_idx)
    op("dve", lambda: nc.vector.tensor_copy(out=ident_b[:], in_=ident_f[:]), reads=[R_c], writes=[R_c2])
    op("dve", lambda: nc.vector.memset(ones_b[:], 1.0), writes=[R_c2])
    op("dve", lambda: nc.vector.memset(zero_sb[:], 0.0), writes=[R_z])
    if stage == 1:
        op("dve", lambda: nc.vector.tensor_copy(out=maskc[:], in_=mtmp[:]), reads=[R_m], writes=[R_c2])
        op("dve", lambda: nc.vector.tensor_copy(out=masks[:], in_=mtmp2[:]), reads=[R_m], writes=[R_c2])
    if stage == 1:
        NPL = float(cfg.npool_sh); BIG = float(2 ** 30)
        def dv(fn, rd=(), wr=()):
            op("dve", fn, reads=[R_idxf] + list(rd), writes=[R_idxf] + list(wr))
        op("dve", lambda: nc.vector.tensor_copy(out=fA[:NPG], in_=ptT[:NPG]), reads=[R_idx], writes=[R_idxf])
        dv(lambda: nc.vector.tensor_scalar(out=fA[:NPG], in0=fA[:NPG], scalar1=vec_sb[:NPG, 45:46], scalar2=None, op0=ALU.subtract), rd=[R_c])
        dv(lambda: nc.vector.tensor_scalar(out=fV[:NPG], in0=fA[:NPG], scalar1=1.0, scalar2=0.0, op0=ALU.add, op1=ALU.max))
        dv(lambda: nc.vector.tensor_scalar(out=fV[:NPG], in0=fV[:NPG], scalar1=1.0, scalar2=None, op0=ALU.min))
        dv(lambda: nc.vector.tensor_scalar(out=fT[:NPG], in0=fA[:NPG], scalar1=-1.0, scalar2=NPL, op0=ALU.mult, op1=ALU.add))
        dv(lambda: nc.vector.tensor_scalar(out=fT[:NPG], in0=fT[:NPG], scalar1=0.0, scalar2=1.0, op0=ALU.max, op1=ALU.min))
        dv(lambda: nc.vector.tensor_tensor(out=fV[:NPG], in0=fV[:NPG], in1=fT[:NPG], op=ALU.mult))
        dv(lambda: nc.vector.tensor_scalar(out=fN[:NPG], in0=fV[:NPG], scalar1=-BIG, scalar2=BIG, op0=ALU.mult, op1=ALU.add))
        dv(lambda: nc.vector.tensor_scalar(out=fT[:NPG], in0=fV[:NPG], scalar1=-1.0, scalar2=30000.0, op0=ALU.add, op1=ALU.mult))
        dv(lambda: nc.vector.tensor_copy(out=biasb[:NPG], in_=fT[:NPG]), wr=[R_idx])
        dv(lambda: nc.vector.tensor_scalar(out=fA[:NPG], in0=fA[:NPG], scalar1=float(GP), scalar2=None, op0=ALU.mult))
        for k in range(GP):
            dv(lambda: nc.vector.tensor_scalar(out=fT[:NPG], in0=fA[:NPG], scalar1=float(k), scalar2=None, op0=ALU.add))
            dv(lambda: nc.vector.tensor_tensor(out=fT[:NPG], in0=fT[:NPG], in1=fV[:NPG], op=ALU.mult))
            dv(lambda: nc.vector.tensor_tensor(out=fT[:NPG], in0=fT[:NPG], in1=fN[:NPG], op=ALU.add))
            dv(lambda: nc.vector.tensor_copy(out=idxk[:NPG, k * NSALL:(k + 1) * NSALL], in_=fT[:NPG]), wr=[R_idx])
    CONST = [R_c, R_c2]
    V_F1, V_MIX, V_F2, V_AG, V_CG, V_CB, V_LG, V_LB = 0, 8, 16, 24, 28, 32, 36, 40
    alt = [0]

    def evac(out_ap, in_ap, reads, writes, scale=None):
        alt[0] ^= 1
        if alt[0]:
            if scale is None:
                return op("act", lambda: nc.scalar.activation(out=out_ap, in_=in_ap, func=AF.Copy), reads=reads, writes=writes)
            return op("act", lambda: nc.scalar.activation(out=out_ap, in_=in_ap, func=AF.Copy, scale=scale), reads=reads, writes=writes)
        if scale is None:
            return op("dve", lambda: nc.vector.tensor_copy(out=out_ap, in_=in_ap), reads=reads, writes=writes)
        return op("dve", lambda: nc.vector.tensor_scalar(out=out_ap, in0=in_ap, scalar1=scale, scalar2=None, op0=ALU.mult),
                  reads=reads, writes=writes)

    def rstd_of(ss_ap, n, out_ap, reg, nfeat):
        op("dve", lambda: nc.vector.tensor_scalar(out=out_ap[:n], in0=ss_ap[:n], scalar1=1.0 / nfeat, scalar2=EPS,
                                                   op0=ALU.mult, op1=ALU.add), reads=[reg], writes=[reg])
        op("act", lambda: nc.scalar.activation(out=out_ap[:n], in_=out_ap[:n], func=AF.Sqrt), reads=[reg], writes=[reg])
        op("dve", lambda: nc.vector.reciprocal(out=out_ap[:n], in_=out_ap[:n]), reads=[reg], writes=[reg])

    R_wb = {k: Reg() for k in wsrc}
    with ExitStack() as st:
        MAXC = max(DFF, 1568)
        stf = [sb(st, "stf%d" % i, [P, MAXC]) for i in range(2)]; stb = [sb(st, "stb%d" % i, [P, MAXC], BF16) for i in range(2)]
        Rf = [sy.dreg() for _ in range(2)]; Rb = [sy.dreg() for _ in range(2)]
        k = 0
        for name, w in wsrc.items():
            R, Cn = w.shape
            for rc in range(R // P):
                s = k % 2; k += 1
                load(stf[s][:, :Cn], w[rc * P:(rc + 1) * P, :], Rf[s])
                e = ("dve", "act", "pool")[k % 3]
                if e == "act":
                    op("act", lambda: nc.scalar.activation(out=stb[s][:, :Cn], in_=stf[s][:, :Cn], func=AF.Copy), reads=[Rf[s]], writes=[Rb[s]])
                elif e == "dve":
                    op("dve", lambda: nc.vector.tensor_copy(out=stb[s][:, :Cn], in_=stf[s][:, :Cn]), reads=[Rf[s]], writes=[Rb[s]])
                else:
                    op("pool", lambda: nc.gpsimd.tensor_copy(out=stb[s][:, :Cn], in_=stf[s][:, :Cn]), reads=[Rf[s]], writes=[Rb[s]])
                store(wb[name][rc * P:(rc + 1) * P, :], stb[s][:, :Cn], Rb[s], dst_regs=[R_wb[name]])
        sy.barrier()

    def seq_tiles():
        tiles = []
        if stage == 2:
            return [("s", 0, 0, cfg.nstok)]
        for s in range(NSQ):
            for t0 in range(0, S, 512):
                n = min(512, S - t0)
                tiles.append(("p", s, t0, n))
        tiles.append(("s", 0, 0, cfg.nstok))
        for t0 in range(0, NSALL * 4, 512):
            tiles.append(("a", 0, t0, min(512, NSALL * 4 - t0)))
        return tiles

    def tile_gt0(kind, s, t0):
        if kind == "p": return s * S + t0
        if kind == "s": return NSQ * S
        return NSQ * S + cfg.nstok + t0

    def gcol(kind, s, t0):
        return s * (S + 30) + 30 + t0

    def ffn_phase(which):
        first = which == "1"
        with ExitStack() as st:
            xin = [sb(st, "xin%d" % i, [P, D]) for i in range(4)]; Rx = [sy.dreg() for _ in range(4)]
            junk = sb(st, "junk", [P, D]); Rj = Reg()
            ssq = sb(st, "ssq", [P, 8]); Rss = Reg()
            xn = sb(st, "xn", [P, D], BF16); Rxn = Reg()
            nT = sb(st, "nT", [P, 8, 512], BF16); RnT = Reg()
            actT = sb(st, "actT", [P, NJ, 512], BF16); Ract = Reg()
            sg = [sb(st, "sg%d" % i, [P, 512]) for i in range(2)]; Rsg = [Reg(), Reg()]
            wg = [sb(st, "wg%d" % i, [P, 8, 256], BF16) for i in range(2)]; Rwg = [sy.dreg() for _ in range(2)]
            wu = [sb(st, "wu%d" % i, [P, 8, 256], BF16) for i in range(2)]; Rwu = [sy.dreg() for _ in range(2)]
            wd = [sb(st, "wd%d" % i, [P, NJ, 256], BF16) for i in range(2)]; Rwd = [sy.dreg() for _ in range(2)]
            psG = [ps(st, "psG%d" % i, [P, 512]) for i in range(2)]; RpG = [Reg(), Reg()]
            psU = [ps(st, "psU%d" % i, [P, 512]) for i in range(2)]; RpU = [Reg(), Reg()]
            psD = [ps(st, "psD%d" % i, [P, 512]) for i in range(2)]; RpD = [Reg(), Reg()]
            psT = ps(st, "psT", [P, 8, P], BF16); RpT = Reg()
            psM = ps(st, "psM", [P, 512]); RpM = Reg()
            if first:
                win_sb = sb(st, "win_sb", [P, 8, 1568], BF16); Rwin = sy.dreg(const=True)
                load(win_sb[:], wb["win"].ap().rearrange("(kc p) n -> p kc n", p=P), Rwin, src_regs=[R_wb["win"]])
                yo = sb(st, "yo", [P, 512]); Ryo = sy.dreg()
                yb = sb(st, "yb", [P, 256], BF16); Ryb = sy.dreg()
                tT = sb(st, "tT", [P, 2, 512], BF16); RtT = sy.dreg()
                cTt = sb(st, "cTt", [P, 2, 512], BF16); RcTt = sy.dreg()
                krf = sb(st, "krf", [P, 32]); Rkrf = sy.dreg()
                krb = sb(st, "krb", [P, 32], BF16); Rkrb = Reg()
                krt = sb(st, "krt", [32, 512], BF16); Rkrt = sy.dreg()
                rtmp = sb(st, "rtmp", [P, 4, 16]); Rrt = Reg()
                glu = sb(st, "glu", [P, 4, 512]); Rglu = sy.dreg()
                sgm = sb(st, "sgm", [P, 512]); Rsgm = Reg()
            else:
                yo = sb(st, "yo", [P, D]); Ryo = sy.dreg()
            gcol0 = V_F1 if first else V_F2
            wgd, wud, wdd = wb["wg" + which], wb["wu" + which], wb["wd" + which]
            Rwsrc = [R_wb["wg" + which], R_wb["wu" + which], R_wb["wd" + which]]
            cnt = {"g": 0, "d": 0, "p": 0}

            def norm_T(src, n, boff, gc, Rsrc):
                op("act", lambda: nc.scalar.activation(out=junk[:n], in_=src[:n], func=AF.Square, accum_out=ssq[:n, 0:1]),
                   reads=[Rsrc], writes=[Rj, Rss])
                rstd_of(ssq[:, 0:1], n, ssq[:, 1:2], Rss, D)
                op("dve", lambda: nc.vector.tensor_scalar(out=xn[:n], in0=src[:n], scalar1=ssq[:n, 1:2], scalar2=None, op0=ALU.mult),
                   reads=[Rsrc, Rss], writes=[Rxn])
                for kc in range(8):
                    op("pe", lambda: nc.tensor.transpose(out=psT[:, kc, :n], in_=xn[:n, kc * P:(kc + 1) * P], identity=ident_b[:n, :n]),
                       reads=[Rxn] + CONST, writes=[RpT], signal=(kc == 7))
                for kc in range(8):
                    evac(nT[:, kc, boff:boff + n], psT[:, kc, :n], [RpT] + CONST, [RnT], scale=vec_sb[:, gc + kc:gc + kc + 1])

            for (kind, s, t0, n) in seq_tiles():
                if stage == 1 and (not first) and kind != "p":
                    continue
                gt0 = tile_gt0(kind, s, t0)
                blocks = [(b0, min(P, n - b0)) for b0 in range(0, n, P)]
                for bi, (b0, nb) in enumerate(blocks):
                    if first:
                        if kind == "p":
                            tok = t0 + b0
                            if tok == 0:
                                load(xin[bi][0:16, :], meta.ap(), Rx[bi])
                                load(xin[bi][16:nb, :], xp[s * cfg.seq: s * cfg.seq + nb - 16, :], Rx[bi])
                            else:
                                load(xin[bi][:nb, :], xp[s * cfg.seq + tok - 16: s * cfg.seq + tok - 16 + nb, :], Rx[bi])
                        elif kind == "s":
                            load(xin[bi][:nb, :], xs[b0:b0 + nb, :], Rx[bi])
                        else:
                            load(xin[bi][:nb, :], xsa[t0 + b0:t0 + b0 + nb, :], Rx[bi])
                    else:
                        load(xin[bi][:nb, :], h_d[gt0 + b0: gt0 + b0 + nb, :], Rx[bi], src_regs=[dr(("h", gt0 + b0))])
                    norm_T(xin[bi], nb, b0, gcol0, Rx[bi])
                for g in range(0, NJ, 2):
                    gs = cnt["g"] % 2; cnt["g"] += 1
                    ncol = min(2, NJ - g) * P
                    load(wg[gs][:, :, :ncol], wgd.ap().rearrange("(kc p) n -> p kc n", p=P)[:, :, g * P: g * P + ncol], Rwg[gs], src_regs=Rwsrc)
                    load(wu[gs][:, :, :ncol], wud.ap().rearrange("(kc p) n -> p kc n", p=P)[:, :, g * P: g * P + ncol], Rwu[gs], src_regs=Rwsrc)
                    for jj in range(ncol // P):
                        j = g + jj
                        pp = cnt["p"] % 2; cnt["p"] += 1
                        for kc in range(8):
                            op("pe", lambda: nc.tensor.matmul(psG[pp][:, :n], lhsT=wg[gs][:, kc, jj * P:(jj + 1) * P], rhs=nT[:, kc, :n],
                                                               start=(kc == 0), stop=(kc == 7)),
                               reads=[Rwg[gs], RnT], writes=[RpG[pp]], signal=(kc == 7))
                        for kc in range(8):
                            op("pe", lambda: nc.tensor.matmul(psU[pp][:, :n], lhsT=wu[gs][:, kc, jj * P:(jj + 1) * P], rhs=nT[:, kc, :n],
                                                               start=(kc == 0), stop=(kc == 7)),
                               reads=[Rwu[gs], RnT], writes=[RpU[pp]], signal=(kc == 7))
                        op("act", lambda: nc.scalar.activation(out=sg[pp][:, :n], in_=psG[pp][:, :n], func=AF.Silu),
                           reads=[RpG[pp]], writes=[Rsg[pp]])
                        op("dve", lambda: nc.vector.tensor_tensor(out=actT[:, j, :n], in0=sg[pp][:, :n], in1=psU[pp][:, :n], op=ALU.mult),
                           reads=[Rsg[pp], RpU[pp]], writes=[Ract])
                for q4 in range(4):
                    ds_ = cnt["d"] % 2; cnt["d"] += 1
                    load(wd[ds_][:], wdd.ap().rearrange("(j p) n -> p j n", p=P)[:, :, q4 * 256:(q4 + 1) * 256], Rwd[ds_], src_regs=Rwsrc)
                    for bi, (b0, nb) in enumerate(blocks):
                        pp = cnt["p"] % 2; cnt["p"] += 1
                        for j in range(NJ):
                            op("pe", lambda: nc.tensor.matmul(psD[pp][:nb, :256], lhsT=actT[:, j, b0:b0 + nb], rhs=wd[ds_][:, j, :],
                                                               start=(j == 0), stop=(j == NJ - 1)),
                               reads=[Rwd[ds_], Ract], writes=[RpD[pp]], signal=(j == NJ - 1))
                        op("dve", lambda: nc.vector.scalar_tensor_tensor(out=xin[bi][:nb, q4 * 256:(q4 + 1) * 256], in0=psD[pp][:nb, :256],
                                                                          scalar=0.5, in1=xin[bi][:nb, q4 * 256:(q4 + 1) * 256],
                                                                          op0=ALU.mult, op1=ALU.add),
                           reads=[RpD[pp], Rx[bi]], writes=[Rx[bi]])
                if not first:
                    for bi, (b0, nb) in enumerate(blocks):
                        op("act", lambda: nc.scalar.activation(out=junk[:nb], in_=xin[bi][:nb], func=AF.Square, accum_out=ssq[:nb, 0:1]),
                           reads=[Rx[bi]], writes=[Rj, Rss])
                        rstd_of(ssq[:, 0:1], nb, ssq[:, 1:2], Rss, D)
                        op("dve", lambda: nc.vector.scalar_tensor_tensor(out=yo[:nb], in0=xin[bi][:nb], scalar=ssq[:nb, 1:2],
                                                                          in1=bvec_sb[:nb, 512:1536], op0=ALU.mult, op1=ALU.mult),
                           reads=[Rx[bi], Rss] + CONST, writes=[Ryo])
                        if kind == "p":
                            tok = t0 + b0
                            if tok == 0:
                                store(yp[s * cfg.seq: s * cfg.seq + nb - 16, :], yo[16:nb, :], Ryo, final=True)
                            else:
                                store(yp[s * cfg.seq + tok - 16: s * cfg.seq + tok - 16 + nb, :], yo[:nb, :], Ryo, final=True)
                        else:
                            store(ys[b0:b0 + nb, :], yo[:nb, :], Ryo, final=True)
                    continue
                for bi, (b0, nb) in enumerate(blocks):
                    store(h_d[gt0 + b0: gt0 + b0 + nb, :], xin[bi][:nb, :], Rx[bi], dst_regs=[dr(("h", gt0 + b0))])
                    norm_T(xin[bi], nb, b0, V_MIX, Rx[bi])
                rblk0 = (t0 // P) if kind == "p" else NB
                rstep = 1 if kind == "p" else 0
                for bi, (b0, nb) in enumerate(blocks):
                    pp = cnt["p"] % 2; cnt["p"] += 1
                    for kc in range(8):
                        op("pe", lambda: nc.tensor.matmul(psG[pp][:nb, :512], lhsT=nT[:, kc, b0:b0 + nb], rhs=win_sb[:, kc, 0:512],
                                                           start=(kc == 0), stop=(kc == 7)), reads=[RnT, Rwin], writes=[RpG[pp]], signal=(kc == 7))
                    for kc in range(8):
                        op("pe", lambda: nc.tensor.matmul(psU[pp][:nb, :32], lhsT=nT[:, kc, b0:b0 + nb], rhs=win_sb[:, kc, 512:544],
                                                           start=(kc == 0), stop=(kc == 7)), reads=[RnT, Rwin], writes=[RpU[pp]], signal=(kc == 7))
                    for part in range(2):
                        src = psG[pp][:nb, part * 256:(part + 1) * 256]
                        op("act", lambda: nc.scalar.activation(out=junk[:nb, :256], in_=src, func=AF.Square, accum_out=ssq[:nb, 2:3]),
                           reads=[RpG[pp]], writes=[Rj, Rss])
                        rstd_of(ssq[:, 2:3], nb, ssq[:, 3:4], Rss, 256)
                        if part == 0:
                            op("dve", lambda: nc.vector.scalar_tensor_tensor(out=yb[:nb, :], in0=src, scalar=ssq[:nb, 3:4], in1=bvec_sb[:nb, 0:256],
                                                                              op0=ALU.mult, op1=ALU.mult), reads=[RpG[pp], Rss] + CONST, writes=[Ryb])
                            dstT, RdT = tT, RtT
                        else:
                            op("dve", lambda: nc.vector.scalar_tensor_tensor(out=yo[:nb, :256], in0=src, scalar=ssq[:nb, 3:4], in1=bvec_sb[:nb, 256:512],
                                                                              op0=ALU.mult, op1=ALU.mult), reads=[RpG[pp], Rss] + CONST, writes=[Ryo])
                            if kind != "a":
                                store(ckv_o[gt0 + b0: gt0 + b0 + nb, :], yo[:nb, :256], Ryo, final=True)
                            op("act", lambda: nc.scalar.activation(out=yb[:nb, :], in_=yo[:nb, :256], func=AF.Copy), reads=[Ryo], writes=[Ryb])
                            store(ckvb_d[gt0 + b0: gt0 + b0 + nb, :], yb[:nb, :], Ryb, dst_regs=[dr(("kv", gt0 + b0))])
                            dstT, RdT = cTt, RcTt
                        for rc in range(2):
                            op("pe", lambda: nc.tensor.transpose(out=psT[:, rc, :nb], in_=yb[:nb, rc * P:(rc + 1) * P], identity=ident_b[:nb, :nb]),
                               reads=[Ryb] + CONST, writes=[RpT], signal=(rc == 1))
                        evac(dstT[:, :, b0:b0 + nb], psT[:, 0:2, :nb], [RpT], [RdT])
                    cosb = rope_sb[:nb, (rblk0 + bi * rstep) * 32:(rblk0 + bi * rstep) * 32 + 16]
                    sinb = rope_sb[:nb, (rblk0 + bi * rstep) * 32 + 16:(rblk0 + bi * rstep) * 32 + 32]
                    x1 = psU[pp][:nb, 0:16]; x2 = psU[pp][:nb, 16:32]
                    rd = [RpU[pp]] + CONST
                    op("dve", lambda: nc.vector.tensor_tensor(out=rtmp[:nb, 0, :], in0=x1, in1=cosb, op=ALU.mult), reads=rd, writes=[Rrt])
                    op("dve", lambda: nc.vector.tensor_tensor(out=rtmp[:nb, 1, :], in0=x2, in1=sinb, op=ALU.mult), reads=rd, writes=[Rrt])
                    op("dve", lambda: nc.vector.tensor_tensor(out=rtmp[:nb, 2, :], in0=x1, in1=sinb, op=ALU.mult), reads=rd, writes=[Rrt])
                    op("dve", lambda: nc.vector.tensor_tensor(out=rtmp[:nb, 3, :], in0=x2, in1=cosb, op=ALU.mult), reads=rd, writes=[Rrt])
                    op("dve", lambda: nc.vector.tensor_tensor(out=krf[:nb, 0:16], in0=rtmp[:nb, 0, :], in1=rtmp[:nb, 1, :], op=ALU.subtract),
                       reads=[Rrt], writes=[Rkrf])
                    op("dve", lambda: nc.vector.tensor_tensor(out=krf[:nb, 16:32], in0=rtmp[:nb, 2, :], in1=rtmp[:nb, 3, :], op=ALU.add),
                       reads=[Rrt], writes=[Rkrf])
                    if kind != "a":
                        store(kr_o[gt0 + b0: gt0 + b0 + nb, :], krf[:nb, :], Rkrf, final=True)
                    op("act", lambda: nc.scalar.activation(out=krb[:nb, :], in_=krf[:nb, :], func=AF.Copy), reads=[Rkrf], writes=[Rkrb])
                    op("pe", lambda: nc.tensor.transpose(out=psT[:32, 2, :nb], in_=krb[:nb, :], identity=ident_b[:nb, :nb]),
                       reads=[Rkrb] + CONST, writes=[RpT])
                    evac(krt[:, b0:b0 + nb], psT[:32, 2, :nb], [RpT], [Rkrt])
                store(qcT_d.ap().rearrange("(rc p) t -> p rc t", p=P)[:, :, gt0:gt0 + n], tT[:, :, :n], RtT, dst_regs=[dr(("qc", gt0))])
                store(cT_d.ap().rearrange("(rc p) t -> p rc t", p=P)[:, :, gt0:gt0 + n], cTt[:, :, :n], RcTt, dst_regs=[dr(("cT", gt0))])
                store(krT_d[:, gt0:gt0 + n], krt[:, :n], Rkrt, dst_regs=[dr(("krT", gt0))])
                for c in range(4 if kind != "a" else 0):
                    pp = cnt["p"] % 2; cnt["p"] += 1
                    for kc in range(8):
                        op("pe", lambda: nc.tensor.matmul(psG[pp][:, :n], lhsT=win_sb[:, kc, 544 + c * P: 544 + (c + 1) * P], rhs=nT[:, kc, :n],
                                                           start=(kc == 0), stop=(kc == 7)), reads=[RnT, Rwin], writes=[RpG[pp]], signal=(kc == 7))
                    for kc in range(8):
                        op("pe", lambda: nc.tensor.matmul(psU[pp][:, :n], lhsT=win_sb[:, kc, 1056 + c * P: 1056 + (c + 1) * P], rhs=nT[:, kc, :n],
                                                           start=(kc == 0), stop=(kc == 7)), reads=[RnT, Rwin], writes=[RpU[pp]], signal=(kc == 7))
                    op("act", lambda: nc.scalar.activation(out=sgm[:, :n], in_=psU[pp][:, :n], func=AF.Sigmoid), reads=[RpU[pp]], writes=[Rsgm])
                    op("dve", lambda: nc.vector.tensor_tensor(out=glu[:, c, :n], in0=sgm[:, :n], in1=psG[pp][:, :n], op=ALU.mult),
                       reads=[Rsgm, RpG[pp]], writes=[Rglu])
                if kind == "p":
                    g0 = gcol(kind, s, t0)
                    store(glu_d.ap().rearrange("(c p) t -> p c t", p=P)[:, :, g0:g0 + n], glu[:, :, :n], Rglu, dst_regs=[dr(("glu", s))])
                elif kind == "s":
                    store(hs_o[:, :], xin[0][:n, :], Rx[0], final=True)
                    for b in range(NSM):
                        g0 = NSQ * (S + 30) + b * 34 + 30
                        store(glu_d.ap().rearrange("(c p) t -> p c t", p=P)[:, :, g0:g0 + 4], glu[:, :, b * 4:b * 4 + 4], Rglu,
                              dst_regs=[dr(("glus", b))])
            sy.barrier()

    if stage == 1:
        ffn_phase("1")

    def mixer_phase():
        with ExitStack() as st:
            wqn = sb(st, "wqn", [P, 2, 8, 64], BF16); wqr = sb(st, "wqr", [P, 2, 8, 32], BF16)
            wuk_sb = sb(st, "wuk_sb", [P, 2, 512], BF16); wukT = sb(st, "wukT", [64, 8, 256], BF16)
            wuv_sb = sb(st, "wuv_sb", [P, 2, 512], BF16); wout_sb = sb(st, "wout_sb", [P, 8, D], BF16)
            Rw = sy.dreg(const=True); RwT = Reg(const=True)
            if stage == 1:
                wq_v = wb["wuq"].ap().rearrange("(kc p) (h d) -> p kc h d", p=P, d=96)
                for kc in range(2):
                    load(wqn[:, kc], wq_v[:, kc, :, 0:64], Rw, src_regs=[R_wb["wuq"]])
                    load(wqr[:, kc], wq_v[:, kc, :, 64:96], Rw, src_regs=[R_wb["wuq"]])
                load(wuk_sb[:], wb["wuk"].ap().rearrange("(kc p) n -> p kc n", p=P), Rw, src_regs=[R_wb["wuk"]])
            load(wuv_sb[:], wb["wuv"].ap().rearrange("(kc p) n -> p kc n", p=P), Rw, src_regs=[R_wb["wuv"]])
            load(wout_sb[:], wb["wout"].ap().rearrange("(kc p) n -> p kc n", p=P), Rw, src_regs=[R_wb["wout"]])
            pst = ExitStack()
            psA_ = ps(pst, "psA_", [P, 1024]); RpA = Reg()
            psS = [ps(pst, "psS%d" % i, [P, 512]) for i in range(2)]; RpS = [Reg(), Reg()]
            psB = ps(pst, "psB", [P, 8, P], BF16); RpB = Reg(); RpB2 = [RpB, RpB]
            psO = ps(pst, "psO", [P, 512]); RpO = Reg()
            psW = ps(pst, "psW", [P, 512]); RpW = Reg()
            psX = ps(pst, "psX", [P, 512]); RpX = Reg()
            for rc in range(2 if stage == 1 else 0):
                for h in range(8):
                    op("pe", lambda: nc.tensor.transpose(out=psB[:64, h, :], in_=wuk_sb[:, rc, h * 64:(h + 1) * 64], identity=ident_b[:]),
                       reads=[Rw] + CONST, writes=[RpB], signal=(h == 7))
                evac(wukT[:, :, rc * P:(rc + 1) * P], psB[:64, :, :], [RpB], [RwT])
            MW = [Rw, RwT] + CONST
            NSEG = cfg.nseg
            SMAX = max(S, (GP // NSEG) * TPD * NPG + 4)
            cT = sb(st, "cT", [P, 2, S], BF16); RcT = sy.dreg()
            kvb = sb(st, "kvb", [P, NB, 256], BF16); Rkvb = sy.dreg()
            krT = sb(st, "krT", [32, S], BF16); RkrT = sy.dreg()
            qcT = sb(st, "qcT", [P, 2, P], BF16); RqcT = sy.dreg()
            qn = sb(st, "qn", [64, 8, P], BF16); Rqn = Reg()
            qlat = sb(st, "qlat", [P, 8, 2, P], BF16); Rql = Reg()
            qrt = sb(st, "qrt", [P, 8, 32], BF16); Rqrt = Reg()
            rt4 = sb(st, "rt4", [P, 4, 8, 16]); Rrt4 = Reg()
            qr = sb(st, "qr", [32, 8, P], BF16); Rqr = Reg()
            s_all = sb(st, "s_all", [P, SMAX]); Rs = Reg()
            p_all = sb(st, "p_all", [P, SMAX], BF16); Rp = Reg()
            sm = sb(st, "sm", [P, 8]); Rsm = Reg()
            pT = [sb(st, "pT%d" % i, [P, 4, P], BF16) for i in range(2)]; RpT_ = [Reg(), Reg()]
            on = sb(st, "on", [P, 256], BF16); Ron = Reg()
            oT = sb(st, "oT", [P, 2, P], BF16); RoT = Reg()
            an = sb(st, "an", [P, 512], BF16); Ran = Reg()
            mixT = sb(st, "mixT", [P, 8, P], BF16); Rmix = Reg()
            gpad = sb(st, "gpad", [P, 4, 30 + P]); Rgp = sy.dreg()
            yc = sb(st, "yc", [P, 4, P]); Ryc = Reg()
            ybf = sb(st, "ybf", [P, 4, P], BF16); Rybf = Reg()
            st3 = sb(st, "st3", [P, 3, P]); Rst3 = Reg()
            hin = sb(st, "hin", [P, D]); Rhin = sy.dreg()
            kvf = [sb(st, "kvf%d" % i, [P, TPD * 288]) for i in range(3)]; Rkvf = [sy.dreg() for _ in range(3)]
            krb8 = [sb(st, "krb8%d" % i, [P, TPD, 33], BF16) for i in range(2)]; Rkrb8 = [Reg(), Reg()]
            kvbS = [sb(st, "kvbS%d" % i, [P, (GP // NSEG) * TPD + 1, 256], BF16) for i in range(2)]; RkvbS = [sy.dreg() for _ in range(2)]
            krTn = sb(st, "krTn", [33, 8], BF16)
            ownf = sb(st, "ownf", [1, 4]); ownr = sb(st, "ownr", [1, 4], BF16); Rown = sy.dreg(); Rownr = sy.dreg()
            po = sb(st, "po", [32, 256]); Rpo = sy.dreg(); pml = sb(st, "pml", [32, 2]); Rpml = sy.dreg()
            cTc = [sb(st, "cTc%d" % i, [P, 2, 512], BF16) for i in range(2)]; RcTc = [Reg(), Reg()]
            krTc = [sb(st, "krTc%d" % i, [33, 512], BF16) for i in range(2)]; RkrTc = [Reg(), Reg()]
            prev = sb(st, "prev", [32, 512]); Rprev = sy.dreg()
            qls = sb(st, "qls", [P, 2, 32], BF16); qrs = sb(st, "qrs", [33, 32], BF16); Rqls = Reg()
            segm = sb(st, "segm", [32, 8]); segl = sb(st, "segl", [32, 8]); segw = sb(st, "segw", [32, 8])
            sego = sb(st, "sego", [32, NSEG, 256]); Rseg = Reg()
            cnt = {"s": 0, "t": 0, "c": 0, "k": 0}

            def q_proj(nq, ropeblk, sample):
                for h in range(8):
                    for kc in range(2):
                        op("pe", lambda: nc.tensor.matmul(psA_[:64, h * P: h * P + nq], lhsT=wqn[:, kc, h, :], rhs=qcT[:, kc, :nq],
                                                           start=(kc == 0), stop=(kc == 1)), reads=[RqcT] + MW, writes=[RpA],
                           signal=(h == 7 and kc == 1))
                evac(qn[:, :, :nq], psA_[:64, :].rearrange("p (h t) -> p h t", h=8)[:, :, :nq], [RpA], [Rqn])
                for half in range(2):
                    for hh in range(4):
                        h = half * 4 + hh
                        for rc in range(2):
                            c0 = (hh * 2 + rc) * P
                            op("pe", lambda: nc.tensor.matmul(psA_[:, c0:c0 + nq], lhsT=wukT[:, h, rc * P:(rc + 1) * P], rhs=qn[:, h, :nq],
                                                               start=True, stop=True), reads=[Rqn] + MW, writes=[RpA],
                               signal=(hh == 3 and rc == 1))
                    evac(qlat[:, half * 4:half * 4 + 4, :, :nq], psA_[:, :].rearrange("p (h r t) -> p h r t", h=4, r=2)[:, :, :, :nq], [RpA], [Rql])
                for kc in range(2):
                    op("pe", lambda: nc.tensor.matmul(psX[:nq, :256], lhsT=qcT[:, kc, :nq], rhs=wqr[:, kc].rearrange("p h d -> p (h d)"),
                                                       start=(kc == 0), stop=(kc == 1)), reads=[RqcT] + MW, writes=[RpX], signal=(kc == 1))
                xv = psX[:nq, :256].rearrange("p (h d) -> p h d", h=8)
                x1 = xv[:, :, 0:16]; x2 = xv[:, :, 16:32]
                cosb = rope_sb[:nq, ropeblk * 32: ropeblk * 32 + 16].unsqueeze(1).to_broadcast([nq, 8, 16])
                sinb = rope_sb[:nq, ropeblk * 32 + 16: ropeblk * 32 + 32].unsqueeze(1).to_broadcast([nq, 8, 16])
                rd = [RpX] + CONST
                op("dve", lambda: nc.vector.tensor_tensor(out=rt4[:nq, 0], in0=x1, in1=cosb, op=ALU.mult), reads=rd, writes=[Rrt4])
                op("dve", lambda: nc.vector.tensor_tensor(out=rt4[:nq, 1], in0=x2, in1=sinb, op=ALU.mult), reads=rd, writes=[Rrt4])
                op("dve", lambda: nc.vector.tensor_tensor(out=rt4[:nq, 2], in0=x1, in1=sinb, op=ALU.mult), reads=rd, writes=[Rrt4])
                op("dve", lambda: nc.vector.tensor_tensor(out=rt4[:nq, 3], in0=x2, in1=cosb, op=ALU.mult), reads=rd, writes=[Rrt4])
                op("dve", lambda: nc.vector.tensor_tensor(out=qrt[:nq, :, 0:16], in0=rt4[:nq, 0], in1=rt4[:nq, 1], op=ALU.subtract),
                   reads=[Rrt4], writes=[Rqrt])
                op("dve", lambda: nc.vector.tensor_tensor(out=qrt[:nq, :, 16:32], in0=rt4[:nq, 2], in1=rt4[:nq, 3], op=ALU.add),
                   reads=[Rrt4], writes=[Rqrt])
                for h in range(8):
                    op("pe", lambda: nc.tensor.transpose(out=psB[:32, h, :nq], in_=qrt[:nq, h, :], identity=ident_b[:nq, :nq]),
                       reads=[Rqrt] + CONST, writes=[RpB], signal=(h == 7))
                evac(qr[:, :, :nq], psB[:32, :, :nq], [RpB], [Rqr])

            def attn_rows(nr, ql, qrr, chunks, vblocks, rdq):
                for ch in chunks:
                    score_chunk(nr, ql, qrr, ch, rdq)
                softmax_pv(nr, chunks[-1][0] + chunks[-1][1], vblocks)

            def score_chunk(nr, ql, qrr, ch, rdq):
                if True:
                    (k0, nk, cfn, kap, regs, mask) = ch
                    si = cnt["s"] % 2; cnt["s"] += 1
                    rdd = rdq + regs + CONST
                    op("pe", lambda: nc.tensor.matmul(psS[si][:nr, :nk], lhsT=ql(0), rhs=cfn(0), start=True, stop=False), reads=rdd, writes=[RpS[si]], signal=False)
                    op("pe", lambda: nc.tensor.matmul(psS[si][:nr, :nk], lhsT=ql(1), rhs=cfn(1), start=False, stop=False), reads=rdd, writes=[RpS[si]], signal=False)
                    if mask is not None:
                        map_, c0, ncol = mask
                        op("pe", lambda: nc.tensor.matmul(psS[si][:nr, c0:c0 + ncol], lhsT=ident_b[:nr, :nr], rhs=map_, start=False, stop=False),
                           reads=rdd, writes=[RpS[si]], signal=False)
                    op("pe", lambda: nc.tensor.matmul(psS[si][:nr, :nk], lhsT=qrr, rhs=kap, start=False, stop=True), reads=rdd, writes=[RpS[si]])
                    evac(s_all[:nr, k0:k0 + nk], psS[si][:nr, :nk], [RpS[si]], [Rs])
            def softmax_pv(nr, nkeys, vblocks):
                op("dve", lambda: nc.vector.tensor_reduce(out=sm[:nr, 0:1], in_=s_all[:nr, :nkeys], axis=AX.X, op=ALU.max), reads=[Rs], writes=[Rsm])
                op("dve", lambda: nc.vector.tensor_scalar(out=sm[:nr, 1:2], in0=sm[:nr, 0:1], scalar1=-SCALE, scalar2=None, op0=ALU.mult),
                   reads=[Rsm], writes=[Rsm])
                op("act", lambda: nc.scalar.activation(out=p_all[:nr, :nkeys], in_=s_all[:nr, :nkeys], func=AF.Exp, bias=sm[:nr, 1:2], scale=SCALE,
                                                        accum_out=sm[:nr, 3:4]), reads=[Rs, Rsm], writes=[Rp, Rsm])
                op("dve", lambda: nc.vector.reciprocal(out=sm[:nr, 2:3], in_=sm[:nr, 3:4]), reads=[Rsm], writes=[Rsm])
                nvb = len(vblocks)
                for g0 in range(0, nvb, 4):
                    grp = vblocks[g0:g0 + 4]
                    ti = cnt["t"] % 2; cnt["t"] += 1
                    for j, (k0, nkb, kvap, regs) in enumerate(grp):
                        op("pe", lambda: nc.tensor.transpose(out=psB[:nkb, ti * 4 + j, :nr], in_=p_all[:nr, k0:k0 + nkb], identity=ident_b[:nr, :nr]),
                           reads=[Rp] + CONST, writes=[RpB2[ti]], signal=(j == len(grp) - 1))
                    if len(grp) == 4 and all(g[1] == grp[0][1] for g in grp):
                        nk_ = grp[0][1]
                        evac(pT[ti][:nk_, :, :nr], psB[:nk_, ti * 4:ti * 4 + 4, :nr], [RpB2[ti]], [RpT_[ti]])
                    else:
                        for j, (k0, nkb, kvap, regs) in enumerate(grp):
                            evac(pT[ti][:nkb, j, :nr], psB[:nkb, ti * 4 + j, :nr], [RpB2[ti]], [RpT_[ti]])
                    for j, (k0, nkb, kvap, regs) in enumerate(grp):
                        last = (g0 + j == nvb - 1)
                        op("pe", lambda: nc.tensor.matmul(psO[:nr, :256], lhsT=pT[ti][:nkb, j, :nr], rhs=kvap, start=(g0 + j == 0), stop=last),
                           reads=[RpT_[ti]] + regs, writes=[RpO], signal=(last or j == len(grp) - 1))

            def o_to_T(nr, src=None, Rsrc=None):
                if src is None:
                    src = psO[:nr, :256]; Rsrc = RpO
                op("act", lambda: nc.scalar.activation(out=on[:nr, :], in_=src, func=AF.Copy, scale=sm[:nr, 2:3]),
                   reads=[Rsrc, Rsm], writes=[Ron])
                for rc in range(2):
                    op("pe", lambda: nc.tensor.transpose(out=psB[:, rc, :nr], in_=on[:nr, rc * P:(rc + 1) * P], identity=ident_b[:nr, :nr]),
                       reads=[Ron] + CONST, writes=[RpB], signal=(rc == 1))
                evac(oT[:, :, :nr], psB[:, 0:2, :nr], [RpB], [RoT])

            def mix_attn(nq):
                op("act", lambda: nc.scalar.activation(out=hin[:nq, :512], in_=psW[:nq, :512], func=AF.Square, accum_out=sm[:nq, 4:5]),
                   reads=[RpW], writes=[Rhin, Rsm])
                rstd_of(sm[:, 4:5], nq, sm[:, 5:6], Rsm, 512)
                op("dve", lambda: nc.vector.tensor_scalar(out=an[:nq, :], in0=psW[:nq, :512], scalar1=sm[:nq, 5:6], scalar2=None, op0=ALU.mult),
                   reads=[RpW, Rsm], writes=[Ran])
                for c in range(4):
                    op("pe", lambda: nc.tensor.transpose(out=psB[:, c, :nq], in_=an[:nq, c * P:(c + 1) * P], identity=ident_b[:nq, :nq]),
                       reads=[Ran] + CONST, writes=[RpB], signal=(c == 3))
                for c in range(4):
                    evac(mixT[:, c, :nq], psB[:, c, :nq], [RpB] + CONST, [Rmix], scale=vec_sb[:, V_AG + c:V_AG + c + 1])

            def mix_conv(nq, gp_c0):
                for c in range(4):
                    eng = "dve"; E = nc.vector
                    op(eng, lambda: E.tensor_scalar(out=yc[:, c, :nq], in0=gpad[:, c, gp_c0:gp_c0 + nq], scalar1=convw_sb[:, c * 31:c * 31 + 1],
                                                    scalar2=vec_sb[:, V_CB + c:V_CB + c + 1], op0=ALU.mult, op1=ALU.add),
                       reads=[Rgp] + CONST, writes=[Ryc])
                    for k in range(1, 31):
                        op(eng, lambda: E.scalar_tensor_tensor(out=yc[:, c, :nq], in0=gpad[:, c, gp_c0 + k:gp_c0 + k + nq],
                                                               scalar=convw_sb[:, c * 31 + k:c * 31 + k + 1], in1=yc[:, c, :nq],
                                                               op0=ALU.mult, op1=ALU.add), reads=[Rgp, Ryc] + CONST, writes=[Ryc])
                op("act", lambda: nc.scalar.activation(out=ybf[:, :, :nq], in_=yc[:, :, :nq], func=AF.Copy), reads=[Ryc], writes=[Rybf])
                for c in range(4):
                    op("pe", lambda: nc.tensor.matmul(psX[:, :nq], lhsT=ones_b[:], rhs=ybf[:, c, :nq], start=(c == 0), stop=(c == 3)),
                       reads=[Rybf] + CONST, writes=[RpX], signal=(c == 3))
                op("dve", lambda: nc.vector.tensor_scalar(out=st3[:, 0, :nq], in0=psX[:, :nq], scalar1=1.0 / 512, scalar2=None, op0=ALU.mult),
                   reads=[RpX], writes=[Rst3])
                op("act", lambda: nc.scalar.activation(out=ybf[:, :, :nq], in_=yc[:, :, :nq], func=AF.Square), reads=[Ryc, RpX], writes=[Rybf])
                for c in range(4):
                    op("pe", lambda: nc.tensor.matmul(psX[:, :nq], lhsT=ones_b[:], rhs=ybf[:, c, :nq], start=(c == 0), stop=(c == 3)),
                       reads=[Rybf] + CONST, writes=[RpX], signal=(c == 3))
                op("dve", lambda: nc.vector.tensor_tensor(out=st3[:, 1, :nq], in0=st3[:, 0, :nq], in1=st3[:, 0, :nq], op=ALU.mult), reads=[Rst3], writes=[Rst3])
                op("dve", lambda: nc.vector.scalar_tensor_tensor(out=st3[:, 1, :nq], in0=psX[:, :nq], scalar=1.0 / 512, in1=st3[:, 1, :nq],
                                                                  op0=ALU.mult, op1=ALU.subtract), reads=[RpX, Rst3], writes=[Rst3])
                op("dve", lambda: nc.vector.tensor_scalar(out=st3[:, 1, :nq], in0=st3[:, 1, :nq], scalar1=EPS, scalar2=None, op0=ALU.add),
                   reads=[Rst3], writes=[Rst3])
                op("act", lambda: nc.scalar.activation(out=st3[:, 1, :nq], in_=st3[:, 1, :nq], func=AF.Sqrt), reads=[Rst3], writes=[Rst3])
                op("dve", lambda: nc.vector.reciprocal(out=st3[:, 1, :nq], in_=st3[:, 1, :nq]), reads=[Rst3], writes=[Rst3])
                for c in range(4):
                    op("dve", lambda: nc.vector.tensor_tensor(out=yc[:, c, :nq], in0=yc[:, c, :nq], in1=st3[:, 0, :nq], op=ALU.subtract),
                       reads=[Ryc, Rst3], writes=[Ryc])
                    op("dve", lambda: nc.vector.tensor_tensor(out=yc[:, c, :nq], in0=yc[:, c, :nq], in1=st3[:, 1, :nq], op=ALU.mult),
                       reads=[Ryc, Rst3], writes=[Ryc])
                    op("act", lambda: nc.scalar.activation(out=yc[:, c, :nq], in_=yc[:, c, :nq], func=AF.Silu,
                                                            bias=vec_sb[:, V_LB + c:V_LB + c + 1], scale=vec_sb[:, V_LG + c:V_LG + c + 1]),
                       reads=[Ryc] + CONST, writes=[Ryc])
                op("act", lambda: nc.scalar.activation(out=ybf[:, :, :nq], in_=yc[:, :, :nq], func=AF.Square), reads=[Ryc, RpX], writes=[Rybf])
                for c in range(4):
                    op("pe", lambda: nc.tensor.matmul(psX[:, :nq], lhsT=ones_b[:], rhs=ybf[:, c, :nq], start=(c == 0), stop=(c == 3)),
                       reads=[Rybf] + CONST, writes=[RpX], signal=(c == 3))
                op("dve", lambda: nc.vector.tensor_scalar(out=st3[:, 2, :nq], in0=psX[:, :nq], scalar1=1.0 / 512, scalar2=EPS, op0=ALU.mult, op1=ALU.add),
                   reads=[RpX], writes=[Rst3])
                op("act", lambda: nc.scalar.activation(out=st3[:, 2, :nq], in_=st3[:, 2, :nq], func=AF.Sqrt), reads=[Rst3], writes=[Rst3])
                op("dve", lambda: nc.vector.reciprocal(out=st3[:, 2, :nq], in_=st3[:, 2, :nq]), reads=[Rst3], writes=[Rst3])
                for c in range(4):
                    op("dve", lambda: nc.vector.scalar_tensor_tensor(out=mixT[:, 4 + c, :nq], in0=yc[:, c, :nq], scalar=vec_sb[:, V_CG + c:V_CG + c + 1],
                                                                      in1=st3[:, 2, :nq], op0=ALU.mult, op1=ALU.mult),
                       reads=[Ryc, Rst3] + CONST, writes=[Rmix])

            def mix_proj(nq, gt):
                load(hin[:nq, :], h_d[gt:gt + nq, :], Rhin, src_regs=[dr(("h", gt))])
                for half in range(2):
                    for kc in range(8):
                        op("pe", lambda: nc.tensor.matmul(psX[:nq, :512], lhsT=mixT[:, kc, :nq], rhs=wout_sb[:, kc, half * 512:(half + 1) * 512],
                                                           start=(kc == 0), stop=(kc == 7)), reads=[Rmix] + MW, writes=[RpX], signal=(kc == 7))
                    op("dve", lambda: nc.vector.tensor_tensor(out=hin[:nq, half * 512:(half + 1) * 512], in0=hin[:nq, half * 512:(half + 1) * 512],
                                                               in1=psX[:nq, :512], op=ALU.add), reads=[Rhin, RpX], writes=[Rhin])
                store(h_d[gt:gt + nq, :], hin[:nq, :], Rhin, dst_regs=[dr(("h", gt))])

            if stage == 2:
                Rh0 = sy.dreg()
                op("sp", lambda: nc.sync.dma_start(out=h_d[NSQ * S:NSQ * S + cfg.nstok, :], in_=hs_in[:, :]),
                   writes=[dr(("h", NSQ * S + t)) for t in range(0, cfg.nstok, 4)], dma=Rh0)
                sego2 = sb(st, "sego2", [32, NCR, 256]); segml = sb(st, "segml", [32, NCR, 2]); Rs2 = sy.dreg()
                segm2 = sb(st, "segm2", [32, NCR]); segl2 = sb(st, "segl2", [32, NCR]); segw2 = sb(st, "segw2", [32, NCR]); Rw2 = Reg()
                osum = sb(st, "osum", [32, 256]); Rosum = Reg()
                cmf2 = sb(st, "cmf2", [P, 16]); Rcm2 = sy.dreg()
                for b in range(NSM):
                    gt = NSQ * S + b * 4
                    r0 = b * NCR * 32
                    load(sego2[:, :, :], pin_o[r0:r0 + NCR * 32, :].rearrange("(c r) f -> r c f", r=32), Rs2)
                    load(segml[:, :, :], pin_ml[r0:r0 + NCR * 32, :].rearrange("(c r) t -> r c t", r=32), Rs2)
                    op("dve", lambda: nc.vector.tensor_copy(out=segm2[:, :], in_=segml[:, :, 0]), reads=[Rs2], writes=[Rw2])
                    op("dve", lambda: nc.vector.tensor_copy(out=segl2[:, :], in_=segml[:, :, 1]), reads=[Rs2], writes=[Rw2])
                    op("dve", lambda: nc.vector.tensor_reduce(out=sm[:32, 0:1], in_=segm2[:, :], axis=AX.X, op=ALU.max), reads=[Rw2], writes=[Rsm])
                    op("dve", lambda: nc.vector.tensor_scalar(out=sm[:32, 1:2], in0=sm[:32, 0:1], scalar1=-SCALE, scalar2=None, op0=ALU.mult),
                       reads=[Rsm], writes=[Rsm])
                    op("act", lambda: nc.scalar.activation(out=segw2[:, :], in_=segm2[:, :], func=AF.Exp, bias=sm[:32, 1:2], scale=SCALE),
                       reads=[Rw2, Rsm], writes=[Rw2])
                    op("dve", lambda: nc.vector.tensor_tensor(out=segl2[:, :], in0=segl2[:, :], in1=segw2[:, :], op=ALU.mult), reads=[Rw2], writes=[Rw2])
                    op("dve", lambda: nc.vector.tensor_reduce(out=sm[:32, 3:4], in_=segl2[:, :], axis=AX.X, op=ALU.add), reads=[Rw2], writes=[Rsm])
                    op("dve", lambda: nc.vector.reciprocal(out=sm[:32, 2:3], in_=sm[:32, 3:4]), reads=[Rsm], writes=[Rsm])
                    op("dve", lambda: nc.vector.tensor_scalar(out=osum[:, :], in0=sego2[:, 0, :], scalar1=segw2[:, 0:1], scalar2=None, op0=ALU.mult),
                       reads=[Rs2, Rw2], writes=[Rosum])
                    for c_ in range(1, NCR):
                        op("dve", lambda: nc.vector.scalar_tensor_tensor(out=osum[:, :], in0=sego2[:, c_, :], scalar=segw2[:, c_:c_ + 1],
                                                                          in1=osum[:, :], op0=ALU.mult, op1=ALU.add), reads=[Rs2, Rw2, Rosum], writes=[Rosum])
                    o_to_T(32, src=osum[:32, :], Rsrc=Rosum)
                    for h in range(8):
                        for rc in range(2):
                            op("pe", lambda: nc.tensor.matmul(psW[:4, h * 64:(h + 1) * 64], lhsT=oT[:, rc, h * 4:(h + 1) * 4], rhs=wuv_sb[:, rc, h * 64:(h + 1) * 64],
                                                               start=(rc == 0), stop=(rc == 1)), reads=[RoT] + MW, writes=[RpW], signal=(h == 7 and rc == 1))
                    mix_attn(4)
                    load(cmf2[:, :], cm_in[b * P:(b + 1) * P, :], Rcm2)
                    op("dve", lambda: nc.vector.tensor_copy(out=mixT[:, 4:8, :4], in_=cmf2[:, :].rearrange("p (c q) -> p c q", c=4)), reads=[Rcm2], writes=[Rmix])
                    mix_proj(4, gt)
                sy.barrier()
                pst.close()
                return
            for s in range(NSQ):
                c0 = s * (S + 30)
                op("sp", lambda: nc.sync.dma_start(out=glu_d.ap().rearrange("(c p) t -> p c t", p=P)[:, :, c0:c0 + 30],
                                                   in_=zero_sb[:, 0:30].unsqueeze(1).to_broadcast([P, 4, 30])),
                   reads=[R_z], writes=[dr(("glu", s))], dma=R_m)
            gview = glu_d.ap().rearrange("(c p) t -> p c t", p=P)
            def mix_out(nq, gt, a_reads, gp_c0):
                mix_attn(nq); mix_conv(nq, gp_c0); mix_proj(nq, gt)

            for s in range(NSQ):
                g0s = s * S
                load(cT[:], cT_d.ap().rearrange("(rc p) t -> p rc t", p=P)[:, :, g0s:g0s + S], RcT,
                     src_regs=[dr(("cT", g0s + t)) for t in range(0, S, 512)])
                load(krT[:], krT_d[:, g0s:g0s + S], RkrT, src_regs=[dr(("krT", g0s + t)) for t in range(0, S, 512)])
                nfull = S // P
                kvregs = [dr(("kv", g0s + t)) for t in range(0, S, P)]
                if nfull:
                    load(kvb[:, :nfull, :], ckvb_d[g0s:g0s + nfull * P, :].rearrange("(b p) r -> p b r", p=P), Rkvb, src_regs=kvregs)
                if S % P:
                    load(kvb[:S % P, nfull, :], ckvb_d[g0s + nfull * P:g0s + S, :], Rkvb, src_regs=kvregs)
                with nc.allow_non_contiguous_dma(reason="conv state transpose"):
                    for c in range(4):
                        cc = s * (S + 30) + 30 + S - 30
                        op("sp", lambda: nc.sync.dma_start(out=convp[s * 30:(s + 1) * 30, c * P:(c + 1) * P],
                                                           in_=glu_d[c * P:(c + 1) * P, cc:cc + 30].rearrange("c t -> t c")),
                           reads=[dr(("glu", s))], writes=[], dma=R_m)
                        out_events.append((R_m.dsem, sy.dcnt[id(R_m.dsem)]))
                for i in range(NB):
                    nq = min(P, S - i * P); gt = g0s + i * P
                    load(qcT[:, :, :nq], qcT_d.ap().rearrange("(rc p) t -> p rc t", p=P)[:, :, gt:gt + nq], RqcT,
                         src_regs=[dr(("qc", g0s + (i * P // 512) * 512))])
                    q_proj(nq, i, False)
                    gc0 = s * (S + 30) + i * P
                    load(gpad[:, :, :30 + nq], gview[:, :, gc0:gc0 + 30 + nq], Rgp, src_regs=[dr(("glu", s))])
                    nkeys = i * P + nq
                    for h in range(8):
                        chunks = []
                        for k0 in range(0, nkeys, 512):
                            nk = min(512, nkeys - k0)
                            mask = None
                            if k0 + nk == nkeys:
                                mask = (maskc[:nq, :nq], i * P - k0, nq)
                            chunks.append((k0, nk, (lambda rc, k0=k0, nk=nk: cT[:, rc, k0:k0 + nk]), krT[:, k0:k0 + nk], [RcT, RkrT], mask))
                        vbl = [(b * P, min(P, nkeys - b * P), kvb[:min(P, nkeys - b * P), b, :], [Rkvb]) for b in range(i + 1)]
                        attn_rows(nq, (lambda rc, h=h: qlat[:, h, rc, :nq]), qr[:, h, :nq], chunks, vbl, [Rql, Rqr])
                        o_to_T(nq)
                        for rc in range(2):
                            op("pe", lambda: nc.tensor.matmul(psW[:nq, h * 64:(h + 1) * 64], lhsT=oT[:, rc, :nq], rhs=wuv_sb[:, rc, h * 64:(h + 1) * 64],
                                                               start=(rc == 0), stop=(rc == 1)), reads=[RoT] + MW, writes=[RpW], signal=(rc == 1))
                    mix_out(nq, gt, [], 0)
            cmf = sb(st, "cmf", [P, 16]); Rcmf = sy.dreg()
            for b in range(NSM):
                load(prev[:30, :], stconv[b * 30:(b + 1) * 30, :], Rprev)
                for c in range(4):
                    op("pe", lambda: nc.tensor.matmul(psX[:, c * 32:c * 32 + 30], lhsT=prev[:30, c * P:(c + 1) * P], rhs=ident_f[:30, :30],
                                                       start=True, stop=True), reads=[Rprev] + CONST, writes=[RpX], signal=(c == 3))
                gs0 = NSQ * (S + 30) + b * 34
                load(gpad[:, :, 30:34], gview[:, :, gs0 + 30:gs0 + 34], Rgp, src_regs=[dr(("glus", b))])
                op("dve", lambda: nc.vector.tensor_copy(out=gpad[:, :, 0:30], in_=psX[:, 0:128].rearrange("p (c t) -> p c t", c=4)[:, :, 0:30]),
                   reads=[RpX], writes=[Rgp])
                op("sp", lambda: nc.sync.dma_start(out=convs[b * 30:b * 30 + 26, :], in_=stconv[b * 30 + 4:(b + 1) * 30, :]), writes=[], dma=R_m)
                out_events.append((R_m.dsem, sy.dcnt[id(R_m.dsem)]))
                with nc.allow_non_contiguous_dma(reason="conv state transpose"):
                    for c in range(4):
                        op("sp", lambda: nc.sync.dma_start(out=convs[b * 30 + 26:(b + 1) * 30, c * P:(c + 1) * P],
                                                           in_=glu_d[c * P:(c + 1) * P, gs0 + 30:gs0 + 34].rearrange("c t -> t c")),
                           reads=[dr(("glus", b))], writes=[], dma=R_m)
                        out_events.append((R_m.dsem, sy.dcnt[id(R_m.dsem)]))
                mix_conv(4, 0)
                op("dve", lambda: nc.vector.tensor_copy(out=cmf[:, :].rearrange("p (c q) -> p c q", c=4), in_=mixT[:, 4:8, :4]), reads=[Rmix], writes=[Rcmf])
                store(cm_o[b * P:(b + 1) * P, :], cmf[:, :], Rcmf, final=True)
            sy.barrier(); pst.close(); pst = ExitStack()
            psSS = ps(pst, "psSS", [P, 1024]); psS = [psSS[:, 0:512], psSS[:, 512:1024]]; RpS = [Reg(), Reg()]
            psA_ = psSS; RpA = Reg(subs=RpS)
            psO = ps(pst, "psO2", [P, 512]); RpO = Reg()
            psX = psO; RpX = RpO
            psKV = [ps(pst, "psKV%d" % i, [P, 8, P], BF16) for i in range(2)]; RpKV = [Reg(), Reg()]
            psKR = [ps(pst, "psKR%d" % i, [P, 8, P], BF16) for i in range(2)]; RpKR = [Reg(), Reg()]
            psB = ps(pst, "psP", [P, 8, P], BF16); RpB = Reg(); RpB2 = [RpB, RpB]
            op("dve", lambda: nc.vector.memset(qrs[32:33, :], 1.0), writes=[Rqls])
            for i_ in range(3):
                op("dve", lambda: nc.vector.memset(kvf[i_][:], 0.0), writes=[Rkvf[i_]])
            DPS = GP // NSEG
            bc_reg = nc.gpsimd.to_reg(cfg.npool_sh * GP - 1)
            for b in range(NSALL):
                gt = NSQ * S + cfg.nstok + b * 4
                load(qcT[:, :, :4], qcT_d.ap().rearrange("(rc p) t -> p rc t", p=P)[:, :, gt:gt + 4], RqcT,
                     src_regs=[dr(("qc", NSQ * S + cfg.nstok + (b * 4 // 512) * 512))])
                q_proj(4, NB, True)
                for rc in range(2):
                    evac(qls[:, rc, :].rearrange("p (h q) -> p h q", h=8), qlat[:, :, rc, :4], [Rql], [Rqls])
                evac(qrs[0:32, :].rearrange("p (h q) -> p h q", h=8), qr[:, :, :4], [Rqr], [Rqls])
                ql_s = (lambda rc: qls[:, rc, :]); qr_s = qrs[:, :]
                for sg_ in range(NSEG):
                    kvs = kvbS[sg_ % 2]; Rkvs = RkvbS[sg_ % 2]
                    vbl = []; k0 = 0; pend = None
                    for dd in range(DPS):
                        kk = sg_ * DPS + dd
                        ci = cnt["c"] % 3; cnt["c"] += 1
                        col = kk * NSALL + b
                        op("pool", lambda: nc.gpsimd.indirect_dma_start(out=kvf[ci][:NPG, :], out_offset=None, in_=poolx[:, :],
                                                                        in_offset=bass.IndirectOffsetOnAxis(ap=idxk[:NPG, col:col + 1], axis=0),
                                                                        bounds_check=bc_reg, oob_is_err=False),
                           reads=[R_idx], writes=[Rkvf[ci]], dma=Rkvf[ci])
                        kvv = kvf[ci][:NPG, :].rearrange("p (t f) -> p t f", f=288)
                        kr8 = krb8[dd % 2]; Rkr8 = Rkrb8[dd % 2]
                        op("dve", lambda: nc.vector.tensor_copy(out=kvs[:NPG, dd * TPD:(dd + 1) * TPD, :], in_=kvv[:, :, 0:256]),
                           reads=[Rkvf[ci]], writes=[Rkvs])
                        op("act", lambda: nc.scalar.activation(out=kr8[:NPG, :, 0:32], in_=kvv[:, :, 256:288], func=AF.Copy),
                           reads=[Rkvf[ci]], writes=[Rkr8])
                        op("dve", lambda: nc.vector.tensor_copy(out=kr8[:NPG, :, 32:33], in_=biasb[:NPG, b:b + 1].unsqueeze(1).to_broadcast([NPG, TPD, 1])),
                           reads=[R_idx], writes=[Rkr8])
                        for hf in range(TPD // 4):
                            c2 = cnt["k"] % 2; cnt["k"] += 1
                            for j in range(4):
                                blk = dd * TPD + hf * 4 + j
                                for rc in range(2):
                                    op("pe", lambda: nc.tensor.transpose(out=psKV[c2][:, rc * 4 + j, :NPG], in_=kvs[:NPG, blk, rc * P:(rc + 1) * P],
                                                                         identity=ident_b[:NPG, :NPG]),
                                       reads=[Rkvs] + CONST, writes=[RpKV[c2]], signal=(j == 3 and rc == 1))
                            evac(cTc[c2][:, :, :4 * NPG].rearrange("p r (j t) -> p r j t", t=NPG),
                                 psKV[c2][:, :, :NPG].rearrange("p (r j) t -> p r j t", r=2), [RpKV[c2]], [RcTc[c2]])
                            for j in range(4):
                                op("pe", lambda: nc.tensor.transpose(out=psKR[c2][:33, j, :NPG], in_=kr8[:NPG, hf * 4 + j, :], identity=ident_b[:NPG, :NPG]),
                                   reads=[Rkr8] + CONST, writes=[RpKR[c2]], signal=(j == 3))
                            evac(krTc[c2][:, :4 * NPG].rearrange("p (j t) -> p j t", t=NPG), psKR[c2][:33, 0:4, :NPG], [RpKR[c2]], [RkrTc[c2]])
                            if pend is not None:
                                score_chunk(32, ql_s, qr_s, pend, [Rqls])
                            pend = (k0, 4 * NPG, (lambda rc, c2=c2: cTc[c2][:, rc, :4 * NPG]), krTc[c2][:, :4 * NPG], [RcTc[c2], RkrTc[c2]], None)
                            for j in range(4):
                                vbl.append((k0 + j * NPG, NPG, kvs[:NPG, dd * TPD + hf * 4 + j, :], [Rkvs]))
                            k0 += 4 * NPG
                    score_chunk(32, ql_s, qr_s, pend, [Rqls])
                    if sg_ == NSEG - 1:
                        load(cT[:, :, :4], cT_d.ap().rearrange("(rc p) t -> p rc t", p=P)[:, :, gt:gt + 4], RcT,
                             src_regs=[dr(("cT", NSQ * S + cfg.nstok + (b * 4 // 512) * 512))])
                        load(krTn[0:32, :4], krT_d[:, gt:gt + 4], RkrT, src_regs=[dr(("krT", NSQ * S + cfg.nstok + (b * 4 // 512) * 512))])
                        load(ownf[0:1, :], ownb[b:b + 1, :], Rown)
                        op("dve", lambda: nc.vector.tensor_copy(out=ownr[0:1, :], in_=ownf[0:1, :]), reads=[Rown], writes=[Rownr])
                        op("sp", lambda: nc.sync.dma_start(out=krTn[32:33, :4], in_=ownr[0:1, :]), reads=[Rownr], writes=[RkrT], dma=RkrT)
                        load(kvs[:4, DPS * TPD, :], ckvb_d[gt:gt + 4, :], Rkvs, src_regs=[dr(("kv", NSQ * S + cfg.nstok + (b * 4 // P) * P))])
                        score_chunk(32, ql_s, qr_s, (k0, 4, (lambda rc: cT[:, rc, :4]), krTn[:, :4], [RcT, RkrT], (masks[:32, :4], 0, 4)), [Rqls])
                        vbl.append((k0, 4, kvs[:4, DPS * TPD, :], [Rkvs]))
                        k0 += 4
                    softmax_pv(32, k0, vbl)
                    op("dve", lambda: nc.vector.tensor_copy(out=segm[:32, sg_:sg_ + 1], in_=sm[:32, 0:1]), reads=[Rsm], writes=[Rseg])
                    op("dve", lambda: nc.vector.tensor_copy(out=segl[:32, sg_:sg_ + 1], in_=sm[:32, 3:4]), reads=[Rsm], writes=[Rseg])
                    op("act", lambda: nc.scalar.activation(out=sego[:32, sg_, :], in_=psO[:32, :256], func=AF.Copy), reads=[RpO], writes=[Rseg])
                op("dve", lambda: nc.vector.tensor_reduce(out=pml[:32, 0:1], in_=segm[:32, :NSEG], axis=AX.X, op=ALU.max), reads=[Rseg], writes=[Rpml])
                op("dve", lambda: nc.vector.tensor_scalar(out=sm[:32, 1:2], in0=pml[:32, 0:1], scalar1=-SCALE, scalar2=None, op0=ALU.mult), reads=[Rpml], writes=[Rsm])
                op("act", lambda: nc.scalar.activation(out=segw[:32, :NSEG], in_=segm[:32, :NSEG], func=AF.Exp, bias=sm[:32, 1:2], scale=SCALE),
                   reads=[Rseg, Rsm], writes=[Rseg])
                op("dve", lambda: nc.vector.tensor_tensor(out=segl[:32, :NSEG], in0=segl[:32, :NSEG], in1=segw[:32, :NSEG], op=ALU.mult), reads=[Rseg], writes=[Rseg])
                op("dve", lambda: nc.vector.tensor_reduce(out=pml[:32, 1:2], in_=segl[:32, :NSEG], axis=AX.X, op=ALU.add), reads=[Rseg], writes=[Rpml])
                op("dve", lambda: nc.vector.tensor_scalar(out=po[:32, :], in0=sego[:32, 0, :], scalar1=segw[:32, 0:1], scalar2=None, op0=ALU.mult),
                   reads=[Rseg], writes=[Rpo])
                for sg_ in range(1, NSEG):
                    op("dve", lambda: nc.vector.scalar_tensor_tensor(out=po[:32, :], in0=sego[:32, sg_, :], scalar=segw[:32, sg_:sg_ + 1],
                                                                      in1=po[:32, :], op0=ALU.mult, op1=ALU.add), reads=[Rseg, Rpo], writes=[Rpo])
                store(part_o[b * 32:(b + 1) * 32, :], po[:32, :], Rpo, final=True)
                store(part_ml[b * 32:(b + 1) * 32, :], pml[:32, 0:2], Rpml, final=True)
            sy.barrier()
            pst.close()

    mixer_phase()
    ffn_phase("2")
    sy._wait("sp", out_events)
    sy.barrier()
    return nc, es


_BUILD_CACHE = {}


def _common_consts(cfg, inp, c):
    f32 = np.float32
    m = {}
    vecs = np.zeros((P, 64), f32)

    def fm(v, n):
        return np.asarray(v, f32).reshape(n, P).T
    vecs[:, 0:8] = fm(inp["ffn1_norm"][0], 8); vecs[:, 8:16] = fm(inp["mix_norm"][0], 8); vecs[:, 16:24] = fm(inp["ffn2_norm"][0], 8)
    vecs[:, 24:28] = fm(inp["attn_grp_norm"][0], 4); vecs[:, 28:32] = fm(inp["conv_grp_norm"][0], 4)
    vecs[:, 32:36] = fm(inp["conv_b"][0], 4); vecs[:, 36:40] = fm(inp["conv_ln_g"][0], 4); vecs[:, 40:44] = fm(inp["conv_ln_b"][0], 4)
    vecs[:, 45] = np.float32(c * cfg.npool_sh)
    m["vecs"] = vecs
    m["bvec"] = np.concatenate([np.asarray(inp["q_norm"][0], f32), np.asarray(inp["kv_norm"][0], f32),
                                np.asarray(inp["final_norm"], f32)]).reshape(1, 1536)
    m["identf"] = np.eye(P, dtype=f32)
    return m


def _host_layout1(cfg, inp, core):
    D, S, NB = cfg.D, cfg.S, cfg.NB
    f32 = np.float32
    c = core
    m = _common_consts(cfg, inp, c)
    m["xp"] = np.ascontiguousarray(inp["x_prompt"][c * cfg.nseq:(c + 1) * cfg.nseq]).reshape(cfg.nseq * cfg.seq, D)
    m["xs"] = np.ascontiguousarray(inp["x_sample"][c * cfg.nsamp:(c + 1) * cfg.nsamp]).reshape(cfg.nstok, D)
    m["xsa"] = np.ascontiguousarray(inp["x_sample"]).reshape(cfg.nsall * 4, D)
    pk = inp["cache_kv_latent"][0][c * cfg.npool_sh:(c + 1) * cfg.npool_sh]
    pr = inp["cache_k_rope"][0][c * cfg.npool_sh:(c + 1) * cfg.npool_sh]
    m["poolx"] = np.concatenate([pk, pr], axis=-1).reshape(cfg.npool_sh * cfg.gp, cfg.tpd * 288)
    m["stconv"] = np.ascontiguousarray(inp["state_conv"][0, c * cfg.nsamp:(c + 1) * cfg.nsamp]).reshape(cfg.nsamp * 30, 512)
    m["ptabT"] = np.ascontiguousarray(inp["page_table"].T).astype(np.int32)
    own = np.full((cfg.nsall, 4), -30000.0, f32); own[c * cfg.nsamp:(c + 1) * cfg.nsamp] = 0.0
    m["ownb"] = own
    m["meta"] = np.ascontiguousarray(inp["meta_tokens"]).astype(f32)
    for f in ("1", "2"):
        m["wg" + f] = np.ascontiguousarray(inp["ffn%s_w_gate" % f][0]); m["wu" + f] = np.ascontiguousarray(inp["ffn%s_w_up" % f][0])
        m["wd" + f] = np.ascontiguousarray(inp["ffn%s_w_down" % f][0])
    m["win"] = np.ascontiguousarray(inp["w_in"][0]); m["wuq"] = np.ascontiguousarray(inp["w_uq"][0])
    m["wuk"] = np.ascontiguousarray(inp["w_uk"][0]).reshape(256, 512); m["wuv"] = np.ascontiguousarray(inp["w_uv"][0]).reshape(256, 512)
    m["wout"] = np.ascontiguousarray(inp["w_out"][0])
    cw = np.asarray(inp["conv_w"][0], f32)
    m["convw"] = np.ascontiguousarray(cw.T.reshape(4, P, 31).transpose(1, 0, 2)).reshape(P, 4 * 31)
    qi = np.arange(P)[:, None]; ki = np.arange(P)[None, :]
    m["maskc"] = np.where(ki <= qi, 0.0, -30000.0).astype(f32)
    r = np.arange(32)[:, None] % 4; k4 = np.arange(4)[None, :]
    m["masks"] = np.where(k4 <= r, 0.0, -30000.0).astype(f32)
    half = 16
    inv = (np.float32(10000.0) ** (-np.arange(half, dtype=f32) / np.float32(half))).astype(f32)
    tab = np.zeros((P, NB + 1, 32), f32)
    pos = (np.arange(NB * P).reshape(NB, P).T).astype(f32)
    ang = pos[:, :, None] * inv[None, None, :]
    tab[:, :NB, :16] = np.cos(ang); tab[:, :NB, 16:] = np.sin(ang)
    sp = (cfg.past + (np.arange(P) % 4)).astype(f32)
    angs = sp[:, None] * inv[None, :]
    tab[:, NB, :16] = np.cos(angs); tab[:, NB, 16:] = np.sin(angs)
    m["ropet"] = tab.reshape(P, (NB + 1) * 32)
    return m


def _host_layout2(cfg, inp, core, rs1):
    c = core
    m = _common_consts(cfg, inp, c)
    m["wg2"] = np.ascontiguousarray(inp["ffn2_w_gate"][0]); m["wu2"] = np.ascontiguousarray(inp["ffn2_w_up"][0])
    m["wd2"] = np.ascontiguousarray(inp["ffn2_w_down"][0])
    m["wuv"] = np.ascontiguousarray(inp["w_uv"][0]).reshape(256, 512); m["wout"] = np.ascontiguousarray(inp["w_out"][0])
    b0 = c * cfg.nsamp
    po = np.stack([np.asarray(r["part_o"]).reshape(cfg.nsall, 32, 256)[b0:b0 + cfg.nsamp] for r in rs1], axis=1)
    pm = np.stack([np.asarray(r["part_ml"]).reshape(cfg.nsall, 32, 2)[b0:b0 + cfg.nsamp] for r in rs1], axis=1)
    m["pin_o"] = np.ascontiguousarray(po).reshape(cfg.nsamp * cfg.ncores * 32, 256)
    m["pin_ml"] = np.ascontiguousarray(pm).reshape(cfg.nsamp * cfg.ncores * 32, 2)
    m["hs_in"] = np.ascontiguousarray(rs1[c]["hs_o"]); m["cm_in"] = np.ascontiguousarray(rs1[c]["cm_o"])
    return m


def kernel(**inp):
    cfg = Cfg()
    inp = {k: np.asarray(v) for k, v in inp.items()}
    if "nc1" not in _BUILD_CACHE:
        _BUILD_CACHE["nc1"] = build(cfg, 1)
        _BUILD_CACHE["nc2"] = build(cfg, 2)
    nc1, _e1 = _BUILD_CACHE["nc1"]; nc2, _e2 = _BUILD_CACHE["nc2"]
    cores = list(range(cfg.ncores))
    res1 = run_bass_kernel_spmd(nc1, [_host_layout1(cfg, inp, c) for c in cores], core_ids=cores).results
    res2 = run_bass_kernel_spmd(nc2, [_host_layout2(cfg, inp, c, res1) for c in cores], core_ids=cores).results
    return assemble(cfg, res1, res2)


def assemble(cfg, rs, rs2):
    S = cfg.S; NSQ = cfg.nseq
    yp = np.concatenate([r["yp"].reshape(NSQ, cfg.seq, cfg.D) for r in rs], 0)
    ys = np.concatenate([r["ys"].reshape(cfg.nsamp, 4, cfg.D) for r in rs2], 0)
    ckv = [np.asarray(r["ckv_o"]) for r in rs]; kr = [np.asarray(r["kr_o"]) for r in rs]
    ckp = np.concatenate([a[:NSQ * S].reshape(NSQ, S, 256) for a in ckv], 0)[None]
    krp = np.concatenate([a[:NSQ * S].reshape(NSQ, S, 32) for a in kr], 0)[None]
    cks = np.concatenate([a[NSQ * S:].reshape(cfg.nsamp, 4, 256) for a in ckv], 0)[None]
    krs = np.concatenate([a[NSQ * S:].reshape(cfg.nsamp, 4, 32) for a in kr], 0)[None]
    cvp = np.concatenate([r["convp"].reshape(NSQ, 30, 512) for r in rs], 0)[None]
    cvs = np.concatenate([r["convs"].reshape(cfg.nsamp, 30, 512) for r in rs], 0)[None]
    f = lambda a: np.ascontiguousarray(a, dtype=np.float32)
    return (f(yp), f(ys), f(ckp), f(krp), f(cvp), f(cks), f(krs), f(cvs))
```

```python
import numpy as np
from contextlib import ExitStack
import concourse.bass as bass
import concourse.mybir as mybir
from concourse.bass_utils import run_bass_kernel_spmd

F32 = mybir.dt.float32
BF16 = mybir.dt.bfloat16
I32 = mybir.dt.int32
AF = mybir.ActivationFunctionType
ALU = mybir.AluOpType
AX = mybir.AxisListType
P = 128
EPS = 1e-6
CONV_MUL_ENG = "pool"


class Cfg:
    def __init__(self, ncores=8, nseq=4, seq=2048, nsamp=16, npg=128, npool_sh=2560, dff=2816, gp=16):
        self.gp = gp; self.pt = 128 // gp; self.nseg = 4
        self.ncores = ncores; self.nseq = nseq; self.seq = seq; self.S = seq + 16
        self.nsamp = nsamp; self.npg = npg; self.npool_sh = npool_sh; self.dff = dff
        self.D = 1024; self.past = npg * 128
        self.NB = -(-self.S // P)
        self.nstok = nsamp * 4
        self.nsall = ncores * nsamp
        self.T = nseq * self.S + self.nstok + self.nsall * 4
        self.tpd = 128 // gp
        self.nj = dff // P


class Reg:
    __slots__ = ("w", "r", "dsem", "const", "subs")

    def __init__(self, dsem=None, const=False, subs=None):
        self.w = None; self.r = {}; self.dsem = dsem; self.const = const; self.subs = subs


class Sync:
    def __init__(self, nc, es):
        self.nc = nc; self.es = es
        self.E = {"pe": nc.tensor, "act": nc.scalar, "dve": nc.vector, "pool": nc.gpsimd, "sp": nc.sync}
        self.sem = {}; self.cnt = {}; self.known = {e: {} for e in self.E}
        for e in ("pe", "act", "dve", "pool"):
            self.sem[e] = es.enter_context(nc.semaphore("sem_" + e)); self.cnt[e] = 0
        self.dcnt = {}; self.dsems = []; self.nd = 0

    def dreg(self, const=False):
        s = self.es.enter_context(self.nc.semaphore("dsem%d" % self.nd)); self.nd += 1
        self.dcnt[id(s)] = 0; self.dsems.append(s)
        return Reg(dsem=s, const=const)

    def _wait(self, eng, evs):
        need = {}
        for (s, v) in evs:
            k = id(s)
            if k not in need or need[k][1] < v:
                need[k] = (s, v)
        kn = self.known[eng]
        for k, (s, v) in need.items():
            if kn.get(k, 0) < v:
                self.E[eng].wait_ge(s, v); kn[k] = v

    @staticmethod
    def _flat(regs):
        out = []
        for r in regs:
            if r.subs is not None: out.extend(r.subs)
            else: out.append(r)
        return out

    def op(self, eng, fn, reads=(), writes=(), signal=True, dma=None):
        reads = self._flat(reads); writes = self._flat(writes)
        evs = []
        for r in reads:
            if r.w is not None: evs.append(r.w)
        for w in writes:
            if w.w is not None: evs.append(w.w)
            evs.extend(w.r.values())
        if eng == "pe":
            evs = [e for e in evs if e[0] is not self.sem["pe"]]
        self._wait(eng, evs)
        inst = fn()
        if not signal:
            return inst
        if dma is not None:
            s = dma.dsem; self.dcnt[id(s)] += 16; v = self.dcnt[id(s)]; inst.then_inc(s, 16)
        else:
            s = self.sem[eng]; self.cnt[eng] += 1; v = self.cnt[eng]; inst.then_inc(s, 1)
        ev = (s, v)
        for w in writes:
            w.w = ev; w.r = {}
        for r in reads:
            if not r.const:
                r.r[id(s)] = ev
        return inst

    def barrier(self):
        evs = [(self.sem[e], self.cnt[e]) for e in self.sem if self.cnt[e] > 0]
        evs += [(s, self.dcnt[id(s)]) for s in self.dsems if self.dcnt[id(s)] > 0]
        for e in self.E:
            self._wait(e, evs)


def build(cfg, stage=1):
    nc = bass.Bass("TRN2", target_bir_lowering=False)
    es = ExitStack()
    sy = Sync(nc, es)
    D, S, NB, T, DFF, NJ = cfg.D, cfg.S, cfg.NB, cfg.T, cfg.dff, cfg.nj
    NSQ, NSM, NPG = cfg.nseq, cfg.nsamp, cfg.npg
    SCALE = 96.0 ** -0.5

    def din(name, shape, dt=F32):
        return nc.dram_tensor(name, list(shape), dt, kind="ExternalInput")

    def dout(name, shape, dt=F32):
        return nc.dram_tensor(name, list(shape), dt, kind="ExternalOutput")

    def dscr(name, shape, dt):
        return nc.dram_tensor(name, list(shape), dt, kind="Internal")

    NSALL = cfg.nsall; GP = cfg.gp; TPD = cfg.tpd; TOUT = NSQ * S + cfg.nstok
    NCR = cfg.ncores
    if stage == 2:
        wsrc = {}
        for k_, shp in (("wg2", [D, DFF]), ("wu2", [D, DFF]), ("wd2", [DFF, D]), ("wuv", [256, 512]), ("wout", [D, D])):
            wsrc[k_] = din(k_, shp)
        vecs = din("vecs", [P, 64]); bvec = din("bvec", [1, 1536]); identf = din("identf", [P, P])
        pin_o = din("pin_o", [NSM * NCR * 32, 256]); pin_ml = din("pin_ml", [NSM * NCR * 32, 2])
        hs_in = din("hs_in", [cfg.nstok, D]); cm_in = din("cm_in", [NSM * P, 16])
        ys = dout("ys", [cfg.nstok, D])
        xp = xs = xsa = poolx = stconv = ptabT = ownb = meta = convw = maskc_d = masks_d = ropet = None
        yp = ckv_o = kr_o = convp = convs = part_o = part_ml = hs_o = cm_o = None
    else:
        xp = din("xp", [NSQ * cfg.seq, D]); xs = din("xs", [cfg.nstok, D]); xsa = din("xsa", [NSALL * 4, D])
        poolx = din("poolx", [cfg.npool_sh * GP, TPD * 288])
        stconv = din("stconv", [NSM * 30, 512]); ptabT = din("ptabT", [NPG, NSALL], I32)
        ownb = din("ownb", [NSALL, 4])
        meta = din("meta", [16, D])
        wsrc = {}
        for f in ("1", "2"):
            wsrc["wg" + f] = din("wg" + f, [D, DFF]); wsrc["wu" + f] = din("wu" + f, [D, DFF])
            wsrc["wd" + f] = din("wd" + f, [DFF, D])
        wsrc["win"] = din("win", [D, 1568]); wsrc["wuq"] = din("wuq", [256, 768])
        wsrc["wuk"] = din("wuk", [256, 512]); wsrc["wuv"] = din("wuv", [256, 512]); wsrc["wout"] = din("wout", [D, D])
        vecs = din("vecs", [P, 64])
        convw = din("convw", [P, 4 * 31])
        bvec = din("bvec", [1, 1536])
        identf = din("identf", [P, P]); maskc_d = din("maskc", [P, P]); masks_d = din("masks", [32, 4])
        ropet = din("ropet", [P, (NB + 1) * 32])

        yp = dout("yp", [NSQ * cfg.seq, D]); ys = None
        ckv_o = dout("ckv_o", [TOUT, 256]); kr_o = dout("kr_o", [TOUT, 32])
        convp = dout("convp", [NSQ * 30, 512]); convs = dout("convs", [NSM * 30, 512])
        part_o = dout("part_o", [NSALL * 32, 256]); part_ml = dout("part_ml", [NSALL * 32, 2])
        hs_o = dout("hs_o", [cfg.nstok, D]); cm_o = dout("cm_o", [NSM * P, 16])

    wb = {k: dscr(k + "_b", v.shape, BF16) for k, v in wsrc.items()}
    h_d = dscr("h_d", [T, D], F32)
    qcT_d = dscr("qcT_d", [256, T], BF16); cT_d = dscr("cT_d", [256, T], BF16)
    ckvb_d = dscr("ckvb_d", [T, 256], BF16); krT_d = dscr("krT_d", [32, T], BF16)
    GW = NSQ * (S + 30) + NSM * 34
    glu_d = dscr("glu_d", [512, GW], F32)
    NCOLI = GP * NSALL

    uid = [0]

    def sb(st, name, shape, dt=F32):
        uid[0] += 1
        return st.enter_context(nc.sbuf_tensor("s%d_%s" % (uid[0], name), list(shape), dt))

    def ps(st, name, shape, dt=F32):
        uid[0] += 1
        return st.enter_context(nc.psum_tensor("p%d_%s" % (uid[0], name), list(shape), dt))

    op = sy.op
    dram_regs = {}

    def dr(key):
        if key not in dram_regs: dram_regs[key] = Reg()
        return dram_regs[key]

    out_events = []

    def load(dst_ap, src_ap, reg, src_regs=(), eng="sp"):
        return op(eng, lambda: sy.E[eng].dma_start(out=dst_ap, in_=src_ap), reads=src_regs, writes=[reg], dma=reg)

    def store(dst_ap, src_ap, reg, dst_regs=(), final=False, eng="sp"):
        i = op(eng, lambda: sy.E[eng].dma_start(out=dst_ap, in_=src_ap), reads=[reg], writes=dst_regs, dma=reg)
        if final: out_events.append((reg.dsem, sy.dcnt[id(reg.dsem)]))
        return i

    cs = es
    ident_f = sb(cs, "ident_f", [P, P]); ident_b = sb(cs, "ident_b", [P, P], BF16)
    ones_b = sb(cs, "ones_b", [P, P], BF16); maskc = sb(cs, "maskc_sb", [P, P], BF16); masks = sb(cs, "masks_sb", [32, 4], BF16)
    mtmp = sb(cs, "mtmp", [P, P]); mtmp2 = sb(cs, "mtmp2", [32, 4])
    vec_sb = sb(cs, "vec_sb", [P, 64]); convw_sb = sb(cs, "convw_sb", [P, 4 * 31]); bvec_sb = sb(cs, "bvec_sb", [P, 1536])
    rope_sb = sb(cs, "rope_sb", [P, (NB + 1) * 32])
    idxk = sb(cs, "idxk", [P, NCOLI], I32); ptT = sb(cs, "ptT", [P, NSALL], I32)
    fA = sb(cs, "fA", [P, NSALL]); fV = sb(cs, "fV", [P, NSALL]); fN = sb(cs, "fN", [P, NSALL]); fT = sb(cs, "fT", [P, NSALL])
    biasb = sb(cs, "biasb", [P, NSALL], BF16); R_idxf = Reg()
    zero_sb = sb(cs, "zero_sb", [P, 32])
    R_c = sy.dreg(const=True); R_c2 = Reg(const=True); R_idx = sy.dreg(); R_m = sy.dreg(); R_z = Reg()
    load(ident_f[:], identf.ap(), R_c); load(vec_sb[:], vecs.ap(), R_c)
    load(bvec_sb[:], bvec.ap().partition_broadcast(P), R_c)
    if stage == 1:
        load(convw_sb[:], convw.ap(), R_c); load(rope_sb[:], ropet.ap(), R_c)
        load(mtmp[:], maskc_d.ap(), R_m); load(mtmp2[:], masks_d.ap(), R_m)
        load(ptT[:NPG, :], ptabT.ap(), R_idx)
    op("dve", lambda: nc.vector.tensor_copy(out=ident_b[:], in_=ident_f[:]), reads=[R_c], writes=[R_c2])
    op("dve", lambda: nc.vector.memset(ones_b[:], 1.0), writes=[R_c2])
    op("dve", lambda: nc.vector.memset(zero_sb[:], 0.0), writes=[R_z])
    if stage == 1:
        op("dve", lambda: nc.vector.tensor_copy(out=maskc[:], in_=mtmp[:]), reads=[R_m], writes=[R_c2])
        op("dve", lambda: nc.vector.tensor_copy(out=masks[:], in_=mtmp2[:]), reads=[R_m], writes=[R_c2])
    if stage == 1:
        NPL = float(cfg.npool_sh); BIG = float(2 ** 30)
        def dv(fn, rd=(), wr=()):
            op("dve", fn, reads=[R_idxf] + list(rd), writes=[R_idxf] + list(wr))
        op("dve", lambda: nc.vector.tensor_copy(out=fA[:NPG], in_=ptT[:NPG]), reads=[R_idx], writes=[R_idxf])
        dv(lambda: nc.vector.tensor_scalar(out=fA[:NPG], in0=fA[:NPG], scalar1=vec_sb[:NPG, 45:46], scalar2=None, op0=ALU.subtract), rd=[R_c])
        dv(lambda: nc.vector.tensor_scalar(out=fV[:NPG], in0=fA[:NPG], scalar1=1.0, scalar2=0.0, op0=ALU.add, op1=ALU.max))
        dv(lambda: nc.vector.tensor_scalar(out=fV[:NPG], in0=fV[:NPG], scalar1=1.0, scalar2=None, op0=ALU.min))
        dv(lambda: nc.vector.tensor_scalar(out=fT[:NPG], in0=fA[:NPG], scalar1=-1.0, scalar2=NPL, op0=ALU.mult, op1=ALU.add))
        dv(lambda: nc.vector.tensor_scalar(out=fT[:NPG], in0=fT[:NPG], scalar1=0.0, scalar2=1.0, op0=ALU.max, op1=ALU.min))
        dv(lambda: nc.vector.tensor_tensor(out=fV[:NPG], in0=fV[:NPG], in1=fT[:NPG], op=ALU.mult))
        dv(lambda: nc.vector.tensor_scalar(out=fN[:NPG], in0=fV[:NPG], scalar1=-BIG, scalar2=BIG, op0=ALU.mult, op1=ALU.add))
        dv(lambda: nc.vector.tensor_scalar(out=fT[:NPG], in0=fV[:NPG], scalar1=-1.0, scalar2=30000.0, op0=ALU.add, op1=ALU.mult))
        dv(lambda: nc.vector.tensor_copy(out=biasb[:NPG], in_=fT[:NPG]), wr=[R_idx])
        dv(lambda: nc.vector.tensor_scalar(out=fA[:NPG], in0=fA[:NPG], scalar1=float(GP), scalar2=None, op0=ALU.mult))
        for k in range(GP):
            dv(lambda: nc.vector.tensor_scalar(out=fT[:NPG], in0=fA[:NPG], scalar1=float(k), scalar2=None, op0=ALU.add))
            dv(lambda: nc.vector.tensor_tensor(out=fT[:NPG], in0=fT[:NPG], in1=fV[:NPG], op=ALU.mult))
            dv(lambda: nc.vector.tensor_tensor(out=fT[:NPG], in0=fT[:NPG], in1=fN[:NPG], op=ALU.add))
            dv(lambda: nc.vector.tensor_copy(out=idxk[:NPG, k * NSALL:(k + 1) * NSALL], in_=fT[:NPG]), wr=[R_idx])
    CONST = [R_c, R_c2]
    V_F1, V_MIX, V_F2, V_AG, V_CG, V_CB, V_LG, V_LB = 0, 8, 16, 24, 28, 32, 36, 40
    alt = [0]

    def evac(out_ap, in_ap, reads, writes, scale=None):
        alt[0] ^= 1
        if alt[0]:
            if scale is None:
                return op("act", lambda: nc.scalar.activation(out=out_ap, in_=in_ap, func=AF.Copy), reads=reads, writes=writes)
            return op("act", lambda: nc.scalar.activation(out=out_ap, in_=in_ap, func=AF.Copy, scale=scale), reads=reads, writes=writes)
        if scale is None:
            return op("dve", lambda: nc.vector.tensor_copy(out=out_ap, in_=in_ap), reads=reads, writes=writes)
        return op("dve", lambda: nc.vector.tensor_scalar(out=out_ap, in0=in_ap, scalar1=scale, scalar2=None, op0=ALU.mult),
                  reads=reads, writes=writes)

    def rstd_of(ss_ap, n, out_ap, reg, nfeat):
        op("dve", lambda: nc.vector.tensor_scalar(out=out_ap[:n], in0=ss_ap[:n], scalar1=1.0 / nfeat, scalar2=EPS,
                                                   op0=ALU.mult, op1=ALU.add), reads=[reg], writes=[reg])
        op("act", lambda: nc.scalar.activation(out=out_ap[:n], in_=out_ap[:n], func=AF.Sqrt), reads=[reg], writes=[reg])
        op("dve", lambda: nc.vector.reciprocal(out=out_ap[:n], in_=out_ap[:n]), reads=[reg], writes=[reg])

    R_wb = {k: Reg() for k in wsrc}
    with ExitStack() as st:
        MAXC = max(DFF, 1568)
        stf = [sb(st, "stf%d" % i, [P, MAXC]) for i in range(2)]; stb = [sb(st, "stb%d" % i, [P, MAXC], BF16) for i in range(2)]
        Rf = [sy.dreg() for _ in range(2)]; Rb = [sy.dreg() for _ in range(2)]
        k = 0
        for name, w in wsrc.items():
            R, Cn = w.shape
            for rc in range(R // P):
                s = k % 2; k += 1
                load(stf[s][:, :Cn], w[rc * P:(rc + 1) * P, :], Rf[s])
                e = ("dve", "act", "pool")[k % 3]
                if e == "act":
                    op("act", lambda: nc.scalar.activation(out=stb[s][:, :Cn], in_=stf[s][:, :Cn], func=AF.Copy), reads=[Rf[s]], writes=[Rb[s]])
                elif e == "dve":
                    op("dve", lambda: nc.vector.tensor_copy(out=stb[s][:, :Cn], in_=stf[s][:, :Cn]), reads=[Rf[s]], writes=[Rb[s]])
                else:
                    op("pool", lambda: nc.gpsimd.tensor_copy(out=stb[s][:, :Cn], in_=stf[s][:, :Cn]), reads=[Rf[s]], writes=[Rb[s]])
                store(wb[name][rc * P:(rc + 1) * P, :], stb[s][:, :Cn], Rb[s], dst_regs=[R_wb[name]])
        sy.barrier()

    def seq_tiles():
        tiles = []
        if stage == 2:
            return [("s", 0, 0, cfg.nstok)]
        for s in range(NSQ):
            for t0 in range(0, S, 512):
                n = min(512, S - t0)
                tiles.append(("p", s, t0, n))
        tiles.append(("s", 0, 0, cfg.nstok))
        for t0 in range(0, NSALL * 4, 512):
            tiles.append(("a", 0, t0, min(512, NSALL * 4 - t0)))
        return tiles

    def tile_gt0(kind, s, t0):
        if kind == "p": return s * S + t0
        if kind == "s": return NSQ * S
        return NSQ * S + cfg.nstok + t0

    def gcol(kind, s, t0):
        return s * (S + 30) + 30 + t0

    def ffn_phase(which):
        first = which == "1"
        with ExitStack() as st:
            xin = [sb(st, "xin%d" % i, [P, D]) for i in range(4)]; Rx = [sy.dreg() for _ in range(4)]
            junk = sb(st, "junk", [P, D]); Rj = Reg()
            ssq = sb(st, "ssq", [P, 8]); Rss = Reg()
            xn = sb(st, "xn", [P, D], BF16); Rxn = Reg()
            nT = sb(st, "nT", [P, 8, 512], BF16); RnT = Reg()
            actT = sb(st, "actT", [P, NJ, 512], BF16); Ract = Reg()
            sg = [sb(st, "sg%d" % i, [P, 512]) for i in range(2)]; Rsg = [Reg(), Reg()]
            wg = [sb(st, "wg%d" % i, [P, 8, 256], BF16) for i in range(4)]; Rwg = [sy.dreg() for _ in range(4)]
            wu = [sb(st, "wu%d" % i, [P, 8, 256], BF16) for i in range(4)]; Rwu = [sy.dreg() for _ in range(4)]
            wd = [sb(st, "wd%d" % i, [P, NJ, 256], BF16) for i in range(3)]; Rwd = [sy.dreg() for _ in range(3)]
            psG = [ps(st, "psG%d" % i, [P, 512]) for i in range(2)]; RpG = [Reg(), Reg()]
            psU = [ps(st, "psU%d" % i, [P, 512]) for i in range(2)]; RpU = [Reg(), Reg()]
            psD = [ps(st, "psD%d" % i, [P, 512]) for i in range(2)]; RpD = [Reg(), Reg()]
            psT = ps(st, "psT", [P, 8, P], BF16); RpT = Reg()
            psM = ps(st, "psM", [P, 512]); RpM = Reg()
            if first:
                win_sb = sb(st, "win_sb", [P, 8, 1568], BF16); Rwin = sy.dreg(const=True)
                load(win_sb[:], wb["win"].ap().rearrange("(kc p) n -> p kc n", p=P), Rwin, src_regs=[R_wb["win"]])
                yo = sb(st, "yo", [P, 512]); Ryo = sy.dreg()
                yb = sb(st, "yb", [P, 256], BF16); Ryb = sy.dreg()
                tT = sb(st, "tT", [P, 2, 512], BF16); RtT = sy.dreg()
                cTt = sb(st, "cTt", [P, 2, 512], BF16); RcTt = sy.dreg()
                krf = sb(st, "krf", [P, 32]); Rkrf = sy.dreg()
                krb = sb(st, "krb", [P, 32], BF16); Rkrb = Reg()
                krt = sb(st, "krt", [32, 512], BF16); Rkrt = sy.dreg()
                rtmp = sb(st, "rtmp", [P, 4, 16]); Rrt = Reg()
                glu = sb(st, "glu", [P, 4, 512]); Rglu = sy.dreg()
                sgm = sb(st, "sgm", [P, 512]); Rsgm = Reg()
            else:
                yo = sb(st, "yo", [P, D]); Ryo = sy.dreg()
            gcol0 = V_F1 if first else V_F2
            wgd, wud, wdd = wb["wg" + which], wb["wu" + which], wb["wd" + which]
            Rwsrc = [R_wb["wg" + which], R_wb["wu" + which], R_wb["wd" + which]]
            cnt = {"g": 0, "d": 0, "p": 0}

            def norm_T(src, n, boff, gc, Rsrc):
                op("act", lambda: nc.scalar.activation(out=junk[:n], in_=src[:n], func=AF.Square, accum_out=ssq[:n, 0:1]),
                   reads=[Rsrc], writes=[Rj, Rss])
                rstd_of(ssq[:, 0:1], n, ssq[:, 1:2], Rss, D)
                op("dve", lambda: nc.vector.tensor_scalar(out=xn[:n], in0=src[:n], scalar1=ssq[:n, 1:2], scalar2=None, op0=ALU.mult),
                   reads=[Rsrc, Rss], writes=[Rxn])
                for kc in range(8):
                    op("pe", lambda: nc.tensor.transpose(out=psT[:, kc, :n], in_=xn[:n, kc * P:(kc + 1) * P], identity=ident_b[:n, :n]),
                       reads=[Rxn] + CONST, writes=[RpT], signal=(kc == 7))
                for kc in range(8):
                    evac(nT[:, kc, boff:boff + n], psT[:, kc, :n], [RpT] + CONST, [RnT], scale=vec_sb[:, gc + kc:gc + kc + 1])

            for (kind, s, t0, n) in seq_tiles():
                if stage == 1 and (not first) and kind != "p":
                    continue
                gt0 = tile_gt0(kind, s, t0)
                blocks = [(b0, min(P, n - b0)) for b0 in range(0, n, P)]
                for bi, (b0, nb) in enumerate(blocks):
                    if first:
                        if kind == "p":
                            tok = t0 + b0
                            if tok == 0:
                                load(xin[bi][0:16, :], meta.ap(), Rx[bi])
                                load(xin[bi][16:nb, :], xp[s * cfg.seq: s * cfg.seq + nb - 16, :], Rx[bi])
                            else:
                                load(xin[bi][:nb, :], xp[s * cfg.seq + tok - 16: s * cfg.seq + tok - 16 + nb, :], Rx[bi])
                        elif kind == "s":
                            load(xin[bi][:nb, :], xs[b0:b0 + nb, :], Rx[bi])
                        else:
                            load(xin[bi][:nb, :], xsa[t0 + b0:t0 + b0 + nb, :], Rx[bi])
                    else:
                        load(xin[bi][:nb, :], h_d[gt0 + b0: gt0 + b0 + nb, :], Rx[bi], src_regs=[dr(("h", gt0 + b0))])
                    norm_T(xin[bi], nb, b0, gcol0, Rx[bi])
                for g in range(0, NJ, 2):
                    gs = cnt["g"] % 4; cnt["g"] += 1
                    ncol = min(2, NJ - g) * P
                    load(wg[gs][:, :, :ncol], wgd.ap().rearrange("(kc p) n -> p kc n", p=P)[:, :, g * P: g * P + ncol], Rwg[gs], src_regs=Rwsrc)
                    load(wu[gs][:, :, :ncol], wud.ap().rearrange("(kc p) n -> p kc n", p=P)[:, :, g * P: g * P + ncol], Rwu[gs], src_regs=Rwsrc)
                    for jj in range(ncol // P):
                        j = g + jj
                        pp = cnt["p"] % 2; cnt["p"] += 1
                        for kc in range(8):
                            op("pe", lambda: nc.tensor.matmul(psG[pp][:, :n], lhsT=wg[gs][:, kc, jj * P:(jj + 1) * P], rhs=nT[:, kc, :n],
                                                               start=(kc == 0), stop=(kc == 7)),
                               reads=[Rwg[gs], RnT], writes=[RpG[pp]], signal=(kc == 7))
                        for kc in range(8):
                            op("pe", lambda: nc.tensor.matmul(psU[pp][:, :n], lhsT=wu[gs][:, kc, jj * P:(jj + 1) * P], rhs=nT[:, kc, :n],
                                                               start=(kc == 0), stop=(kc == 7)),
                               reads=[Rwu[gs], RnT], writes=[RpU[pp]], signal=(kc == 7))
                        op("act", lambda: nc.scalar.activation(out=sg[pp][:, :n], in_=psG[pp][:, :n], func=AF.Silu),
                           reads=[RpG[pp]], writes=[Rsg[pp]])
                        op("dve", lambda: nc.vector.tensor_tensor(out=actT[:, j, :n], in0=sg[pp][:, :n], in1=psU[pp][:, :n], op=ALU.mult),
                           reads=[Rsg[pp], RpU[pp]], writes=[Ract])
                for q4 in range(4):
                    ds_ = cnt["d"] % 3; cnt["d"] += 1
                    load(wd[ds_][:], wdd.ap().rearrange("(j p) n -> p j n", p=P)[:, :, q4 * 256:(q4 + 1) * 256], Rwd[ds_], src_regs=Rwsrc)
                    for bi, (b0, nb) in enumerate(blocks):
                        pp = cnt["p"] % 2; cnt["p"] += 1
                        for j in range(NJ):
                            op("pe", lambda: nc.tensor.matmul(psD[pp][:nb, :256], lhsT=actT[:, j, b0:b0 + nb], rhs=wd[ds_][:, j, :],
                                                               start=(j == 0), stop=(j == NJ - 1)),
                               reads=[Rwd[ds_], Ract], writes=[RpD[pp]], signal=(j == NJ - 1))
                        op("dve", lambda: nc.vector.scalar_tensor_tensor(out=xin[bi][:nb, q4 * 256:(q4 + 1) * 256], in0=psD[pp][:nb, :256],
                                                                          scalar=0.5, in1=xin[bi][:nb, q4 * 256:(q4 + 1) * 256],
                                                                          op0=ALU.mult, op1=ALU.add),
                           reads=[RpD[pp], Rx[bi]], writes=[Rx[bi]])
                if not first:
                    for bi, (b0, nb) in enumerate(blocks):
                        op("act", lambda: nc.scalar.activation(out=junk[:nb], in_=xin[bi][:nb], func=AF.Square, accum_out=ssq[:nb, 0:1]),
                           reads=[Rx[bi]], writes=[Rj, Rss])
                        rstd_of(ssq[:, 0:1], nb, ssq[:, 1:2], Rss, D)
                        op("dve", lambda: nc.vector.scalar_tensor_tensor(out=yo[:nb], in0=xin[bi][:nb], scalar=ssq[:nb, 1:2],
                                                                          in1=bvec_sb[:nb, 512:1536], op0=ALU.mult, op1=ALU.mult),
                           reads=[Rx[bi], Rss] + CONST, writes=[Ryo])
                        if kind == "p":
                            tok = t0 + b0
                            if tok == 0:
                                store(yp[s * cfg.seq: s * cfg.seq + nb - 16, :], yo[16:nb, :], Ryo, final=True)
                            else:
                                store(yp[s * cfg.seq + tok - 16: s * cfg.seq + tok - 16 + nb, :], yo[:nb, :], Ryo, final=True)
                        else:
                            store(ys[b0:b0 + nb, :], yo[:nb, :], Ryo, final=True)
                    continue
                for bi, (b0, nb) in enumerate(blocks):
                    store(h_d[gt0 + b0: gt0 + b0 + nb, :], xin[bi][:nb, :], Rx[bi], dst_regs=[dr(("h", gt0 + b0))])
                    norm_T(xin[bi], nb, b0, V_MIX, Rx[bi])
                rblk0 = (t0 // P) if kind == "p" else NB
                rstep = 1 if kind == "p" else 0
                for bi, (b0, nb) in enumerate(blocks):
                    pp = cnt["p"] % 2; cnt["p"] += 1
                    for kc in range(8):
                        op("pe", lambda: nc.tensor.matmul(psG[pp][:nb, :512], lhsT=nT[:, kc, b0:b0 + nb], rhs=win_sb[:, kc, 0:512],
                                                           start=(kc == 0), stop=(kc == 7)), reads=[RnT, Rwin], writes=[RpG[pp]], signal=(kc == 7))
                    for kc in range(8):
                        op("pe", lambda: nc.tensor.matmul(psU[pp][:nb, :32], lhsT=nT[:, kc, b0:b0 + nb], rhs=win_sb[:, kc, 512:544],
                                                           start=(kc == 0), stop=(kc == 7)), reads=[RnT, Rwin], writes=[RpU[pp]], signal=(kc == 7))
                    for part in range(2):
                        src = psG[pp][:nb, part * 256:(part + 1) * 256]
                        op("act", lambda: nc.scalar.activation(out=junk[:nb, :256], in_=src, func=AF.Square, accum_out=ssq[:nb, 2:3]),
                           reads=[RpG[pp]], writes=[Rj, Rss])
                        rstd_of(ssq[:, 2:3], nb, ssq[:, 3:4], Rss, 256)
                        if part == 0:
                            op("dve", lambda: nc.vector.scalar_tensor_tensor(out=yb[:nb, :], in0=src, scalar=ssq[:nb, 3:4], in1=bvec_sb[:nb, 0:256],
                                                                              op0=ALU.mult, op1=ALU.mult), reads=[RpG[pp], Rss] + CONST, writes=[Ryb])
                            dstT, RdT = tT, RtT
                        else:
                            op("dve", lambda: nc.vector.scalar_tensor_tensor(out=yo[:nb, :256], in0=src, scalar=ssq[:nb, 3:4], in1=bvec_sb[:nb, 256:512],
                                                                              op0=ALU.mult, op1=ALU.mult), reads=[RpG[pp], Rss] + CONST, writes=[Ryo])
                            if kind != "a":
                                store(ckv_o[gt0 + b0: gt0 + b0 + nb, :], yo[:nb, :256], Ryo, final=True)
                            op("act", lambda: nc.scalar.activation(out=yb[:nb, :], in_=yo[:nb, :256], func=AF.Copy), reads=[Ryo], writes=[Ryb])
                            store(ckvb_d[gt0 + b0: gt0 + b0 + nb, :], yb[:nb, :], Ryb, dst_regs=[dr(("kv", gt0 + b0))])
                            dstT, RdT = cTt, RcTt
                        for rc in range(2):
                            op("pe", lambda: nc.tensor.transpose(out=psT[:, rc, :nb], in_=yb[:nb, rc * P:(rc + 1) * P], identity=ident_b[:nb, :nb]),
                               reads=[Ryb] + CONST, writes=[RpT], signal=(rc == 1))
                        evac(dstT[:, :, b0:b0 + nb], psT[:, 0:2, :nb], [RpT], [RdT])
                    cosb = rope_sb[:nb, (rblk0 + bi * rstep) * 32:(rblk0 + bi * rstep) * 32 + 16]
                    sinb = rope_sb[:nb, (rblk0 + bi * rstep) * 32 + 16:(rblk0 + bi * rstep) * 32 + 32]
                    x1 = psU[pp][:nb, 0:16]; x2 = psU[pp][:nb, 16:32]
                    rd = [RpU[pp]] + CONST
                    op("dve", lambda: nc.vector.tensor_tensor(out=rtmp[:nb, 0, :], in0=x1, in1=cosb, op=ALU.mult), reads=rd, writes=[Rrt])
                    op("dve", lambda: nc.vector.tensor_tensor(out=rtmp[:nb, 1, :], in0=x2, in1=sinb, op=ALU.mult), reads=rd, writes=[Rrt])
                    op("dve", lambda: nc.vector.tensor_tensor(out=rtmp[:nb, 2, :], in0=x1, in1=sinb, op=ALU.mult), reads=rd, writes=[Rrt])
                    op("dve", lambda: nc.vector.tensor_tensor(out=rtmp[:nb, 3, :], in0=x2, in1=cosb, op=ALU.mult), reads=rd, writes=[Rrt])
                    op("dve", lambda: nc.vector.tensor_tensor(out=krf[:nb, 0:16], in0=rtmp[:nb, 0, :], in1=rtmp[:nb, 1, :], op=ALU.subtract),
                       reads=[Rrt], writes=[Rkrf])
                    op("dve", lambda: nc.vector.tensor_tensor(out=krf[:nb, 16:32], in0=rtmp[:nb, 2, :], in1=rtmp[:nb, 3, :], op=ALU.add),
                       reads=[Rrt], writes=[Rkrf])
                    if kind != "a":
                        store(kr_o[gt0 + b0: gt0 + b0 + nb, :], krf[:nb, :], Rkrf, final=True)
                    op("act", lambda: nc.scalar.activation(out=krb[:nb, :], in_=krf[:nb, :], func=AF.Copy), reads=[Rkrf], writes=[Rkrb])
                    op("pe", lambda: nc.tensor.transpose(out=psT[:32, 2, :nb], in_=krb[:nb, :], identity=ident_b[:nb, :nb]),
                       reads=[Rkrb] + CONST, writes=[RpT])
                    evac(krt[:, b0:b0 + nb], psT[:32, 2, :nb], [RpT], [Rkrt])
                store(qcT_d.ap().rearrange("(rc p) t -> p rc t", p=P)[:, :, gt0:gt0 + n], tT[:, :, :n], RtT, dst_regs=[dr(("qc", gt0))])
                store(cT_d.ap().rearrange("(rc p) t -> p rc t", p=P)[:, :, gt0:gt0 + n], cTt[:, :, :n], RcTt, dst_regs=[dr(("cT", gt0))])
                store(krT_d[:, gt0:gt0 + n], krt[:, :n], Rkrt, dst_regs=[dr(("krT", gt0))])
                for c in range(4 if kind != "a" else 0):
                    pp = cnt["p"] % 2; cnt["p"] += 1
                    for kc in range(8):
                        op("pe", lambda: nc.tensor.matmul(psG[pp][:, :n], lhsT=win_sb[:, kc, 544 + c * P: 544 + (c + 1) * P], rhs=nT[:, kc, :n],
                                                           start=(kc == 0), stop=(kc == 7)), reads=[RnT, Rwin], writes=[RpG[pp]], signal=(kc == 7))
                    for kc in range(8):
                        op("pe", lambda: nc.tensor.matmul(psU[pp][:, :n], lhsT=win_sb[:, kc, 1056 + c * P: 1056 + (c + 1) * P], rhs=nT[:, kc, :n],
                                                           start=(kc == 0), stop=(kc == 7)), reads=[RnT, Rwin], writes=[RpU[pp]], signal=(kc == 7))
                    op("act", lambda: nc.scalar.activation(out=sgm[:, :n], in_=psU[pp][:, :n], func=AF.Sigmoid), reads=[RpU[pp]], writes=[Rsgm])
                    op("dve", lambda: nc.vector.tensor_tensor(out=glu[:, c, :n], in0=sgm[:, :n], in1=psG[pp][:, :n], op=ALU.mult),
                       reads=[Rsgm, RpG[pp]], writes=[Rglu])
                if kind == "p":
                    g0 = gcol(kind, s, t0)
                    store(glu_d.ap().rearrange("(c p) t -> p c t", p=P)[:, :, g0:g0 + n], glu[:, :, :n], Rglu, dst_regs=[dr(("glu", s))])
                elif kind == "s":
                    store(hs_o[:, :], xin[0][:n, :], Rx[0], final=True)
                    for b in range(NSM):
                        g0 = NSQ * (S + 30) + b * 34 + 30
                        store(glu_d.ap().rearrange("(c p) t -> p c t", p=P)[:, :, g0:g0 + 4], glu[:, :, b * 4:b * 4 + 4], Rglu,
                              dst_regs=[dr(("glus", b))])
            sy.barrier()

    if stage == 1:
        ffn_phase("1")

    def mixer_phase():
        with ExitStack() as st:
            wqn = sb(st, "wqn", [P, 2, 8, 64], BF16); wqr = sb(st, "wqr", [P, 2, 8, 32], BF16)
            wuk_sb = sb(st, "wuk_sb", [P, 2, 512], BF16); wukT = sb(st, "wukT", [64, 8, 256], BF16)
            wuv_sb = sb(st, "wuv_sb", [P, 2, 512], BF16); wout_sb = sb(st, "wout_sb", [P, 8, D], BF16)
            Rw = sy.dreg(const=True); RwT = Reg(const=True)
            if stage == 1:
                wq_v = wb["wuq"].ap().rearrange("(kc p) (h d) -> p kc h d", p=P, d=96)
                for kc in range(2):
                    load(wqn[:, kc], wq_v[:, kc, :, 0:64], Rw, src_regs=[R_wb["wuq"]])
                    load(wqr[:, kc], wq_v[:, kc, :, 64:96], Rw, src_regs=[R_wb["wuq"]])
                load(wuk_sb[:], wb["wuk"].ap().rearrange("(kc p) n -> p kc n", p=P), Rw, src_regs=[R_wb["wuk"]])
            load(wuv_sb[:], wb["wuv"].ap().rearrange("(kc p) n -> p kc n", p=P), Rw, src_regs=[R_wb["wuv"]])
            load(wout_sb[:], wb["wout"].ap().rearrange("(kc p) n -> p kc n", p=P), Rw, src_regs=[R_wb["wout"]])
            pst = ExitStack()
            psA_ = ps(pst, "psA_", [P, 1024]); RpA = Reg()
            psS = [ps(pst, "psS%d" % i, [P, 512]) for i in range(2)]; RpS = [Reg(), Reg()]
            psB = ps(pst, "psB", [P, 8, P], BF16); RpB = Reg(); RpB2 = [RpB, RpB]
            psPT = [psB, psB]; RpPT = [RpB, RpB]; PTOFF = [0, 4]
            psO = ps(pst, "psO", [P, 512]); RpO = Reg()
            psW = ps(pst, "psW", [P, 512]); RpW = Reg()
            psX = ps(pst, "psX", [P, 512]); RpX = Reg()
            for rc in range(2 if stage == 1 else 0):
                for h in range(8):
                    op("pe", lambda: nc.tensor.transpose(out=psB[:64, h, :], in_=wuk_sb[:, rc, h * 64:(h + 1) * 64], identity=ident_b[:]),
                       reads=[Rw] + CONST, writes=[RpB], signal=(h == 7))
                evac(wukT[:, :, rc * P:(rc + 1) * P], psB[:64, :, :], [RpB], [RwT])
            MW = [Rw, RwT] + CONST
            NSEG = cfg.nseg
            SMAX = max(S, (GP // NSEG) * TPD * NPG + 4) if stage == 1 else 64
            cT = sb(st, "cT", [P, 2, S], BF16); RcT = sy.dreg()
            kvb = sb(st, "kvb", [P, NB, 256], BF16); Rkvb = sy.dreg()
            krT = sb(st, "krT", [32, S], BF16); RkrT = sy.dreg()
            qcT = sb(st, "qcT", [P, 2, P], BF16); RqcT = sy.dreg()
            qn = sb(st, "qn", [64, 8, P], BF16); Rqn = Reg()
            qlat = sb(st, "qlat", [P, 8, 2, P], BF16); Rql = Reg()
            qrt = sb(st, "qrt", [P, 8, 32], BF16); Rqrt = Reg()
            rt4 = sb(st, "rt4", [P, 4, 8, 16]); Rrt4 = Reg()
            qr = sb(st, "qr", [32, 8, P], BF16); Rqr = Reg()
            s_all = sb(st, "s_all", [P, SMAX]); Rs = Reg()
            p_all = sb(st, "p_all", [P, SMAX], BF16); Rp = Reg()
            sm = sb(st, "sm", [P, 8]); Rsm = Reg()
            pT = [sb(st, "pT%d" % i, [P, 4, P], BF16) for i in range(2)]; RpT_ = [Reg(), Reg()]
            on = sb(st, "on", [P, 256], BF16); Ron = Reg()
            oT = sb(st, "oT", [P, 2, P], BF16); RoT = Reg()
            an = sb(st, "an", [P, 512], BF16); Ran = Reg()
            mixT = sb(st, "mixT", [P, 8, P], BF16); Rmix = Reg()
            gpad = sb(st, "gpad", [P, 4, 30 + P]); Rgp = sy.dreg()
            yc = sb(st, "yc", [P, 4, P]); Ryc = Reg()
            ctmp = [sb(st, "ctmp%d" % i, [P, 4, P]) for i in range(2)]; Rctmp = [Reg(), Reg()]
            ybf = sb(st, "ybf", [P, 4, P], BF16); Rybf = Reg()
            st3 = sb(st, "st3", [P, 3, P]); Rst3 = Reg()
            hin = sb(st, "hin", [P, D]); Rhin = sy.dreg()
            kvf = [sb(st, "kvf%d" % i, [P, TPD * 288 if stage == 1 else 8]) for i in range(3)]; Rkvf = [sy.dreg() for _ in range(3)]
            krb8 = [sb(st, "krb8%d" % i, [P, TPD, 33], BF16) for i in range(2)]; Rkrb8 = [Reg(), Reg()]
            kvbS = [sb(st, "kvbS%d" % i, [P, (GP // NSEG) * TPD + 1 if stage == 1 else 1, 256], BF16) for i in range(2)]; RkvbS = [sy.dreg() for _ in range(2)]
            krTn = sb(st, "krTn", [33, 8], BF16)
            ownf = sb(st, "ownf", [1, 4]); ownr = sb(st, "ownr", [1, 4], BF16); Rown = sy.dreg(); Rownr = sy.dreg()
            po = sb(st, "po", [32, 256]); Rpo = sy.dreg(); pml = sb(st, "pml", [32, 2]); Rpml = sy.dreg()
            cTc = [sb(st, "cTc%d" % i, [P, 2, 512], BF16) for i in range(2)]; RcTc = [Reg(), Reg()]
            krTc = [sb(st, "krTc%d" % i, [33, 512], BF16) for i in range(2)]; RkrTc = [Reg(), Reg()]
            prev = sb(st, "prev", [32, 512]); Rprev = sy.dreg()
            qls = sb(st, "qls", [P, 2, 32], BF16); qrs = sb(st, "qrs", [33, 32], BF16); Rqls = Reg()
            segm = sb(st, "segm", [32, 8]); segl = sb(st, "segl", [32, 8]); segw = sb(st, "segw", [32, 8])
            sego = sb(st, "sego", [32, NSEG, 256]); Rseg = Reg()
            cnt = {"s": 0, "t": 0, "c": 0, "k": 0}

            def q_proj(nq, ropeblk, sample):
                for h in range(8):
                    for kc in range(2):
                        op("pe", lambda: nc.tensor.matmul(psA_[:64, h * P: h * P + nq], lhsT=wqn[:, kc, h, :], rhs=qcT[:, kc, :nq],
                                                           start=(kc == 0), stop=(kc == 1)), reads=[RqcT] + MW, writes=[RpA],
                           signal=(h == 7 and kc == 1))
                evac(qn[:, :, :nq], psA_[:64, :].rearrange("p (h t) -> p h t", h=8)[:, :, :nq], [RpA], [Rqn])
                for half in range(2):
                    for hh in range(4):
                        h = half * 4 + hh
                        for rc in range(2):
                            c0 = (hh * 2 + rc) * P
                            op("pe", lambda: nc.tensor.matmul(psA_[:, c0:c0 + nq], lhsT=wukT[:, h, rc * P:(rc + 1) * P], rhs=qn[:, h, :nq],
                                                               start=True, stop=True), reads=[Rqn] + MW, writes=[RpA],
                               signal=(hh == 3 and rc == 1))
                    evac(qlat[:, half * 4:half * 4 + 4, :, :nq], psA_[:, :].rearrange("p (h r t) -> p h r t", h=4, r=2)[:, :, :, :nq], [RpA], [Rql])
                for kc in range(2):
                    op("pe", lambda: nc.tensor.matmul(psX[:nq, :256], lhsT=qcT[:, kc, :nq], rhs=wqr[:, kc].rearrange("p h d -> p (h d)"),
                                                       start=(kc == 0), stop=(kc == 1)), reads=[RqcT] + MW, writes=[RpX], signal=(kc == 1))
                xv = psX[:nq, :256].rearrange("p (h d) -> p h d", h=8)
                x1 = xv[:, :, 0:16]; x2 = xv[:, :, 16:32]
                cosb = rope_sb[:nq, ropeblk * 32: ropeblk * 32 + 16].unsqueeze(1).to_broadcast([nq, 8, 16])
                sinb = rope_sb[:nq, ropeblk * 32 + 16: ropeblk * 32 + 32].unsqueeze(1).to_broadcast([nq, 8, 16])
                rd = [RpX] + CONST
                op("dve", lambda: nc.vector.tensor_tensor(out=rt4[:nq, 0], in0=x1, in1=cosb, op=ALU.mult), reads=rd, writes=[Rrt4])
                op("dve", lambda: nc.vector.tensor_tensor(out=rt4[:nq, 1], in0=x2, in1=sinb, op=ALU.mult), reads=rd, writes=[Rrt4])
                op("dve", lambda: nc.vector.tensor_tensor(out=rt4[:nq, 2], in0=x1, in1=sinb, op=ALU.mult), reads=rd, writes=[Rrt4])
                op("dve", lambda: nc.vector.tensor_tensor(out=rt4[:nq, 3], in0=x2, in1=cosb, op=ALU.mult), reads=rd, writes=[Rrt4])
                op("dve", lambda: nc.vector.tensor_tensor(out=qrt[:nq, :, 0:16], in0=rt4[:nq, 0], in1=rt4[:nq, 1], op=ALU.subtract),
                   reads=[Rrt4], writes=[Rqrt])
                op("dve", lambda: nc.vector.tensor_tensor(out=qrt[:nq, :, 16:32], in0=rt4[:nq, 2], in1=rt4[:nq, 3], op=ALU.add),
                   reads=[Rrt4], writes=[Rqrt])
                for h in range(8):
                    op("pe", lambda: nc.tensor.transpose(out=psB[:32, h, :nq], in_=qrt[:nq, h, :], identity=ident_b[:nq, :nq]),
                       reads=[Rqrt] + CONST, writes=[RpB], signal=(h == 7))
                evac(qr[:, :, :nq], psB[:32, :, :nq], [RpB], [Rqr])

            def attn_rows(nr, ql, qrr, chunks, vblocks, rdq):
                for ch in chunks:
                    score_chunk(nr, ql, qrr, ch, rdq)
                softmax_pv(nr, chunks[-1][0] + chunks[-1][1], vblocks)

            def score_chunk(nr, ql, qrr, ch, rdq):
                if True:
                    (k0, nk, cfn, kap, regs, mask) = ch
                    si = cnt["s"] % 2; cnt["s"] += 1
                    rdd = rdq + regs + CONST
                    op("pe", lambda: nc.tensor.matmul(psS[si][:nr, :nk], lhsT=ql(0), rhs=cfn(0), start=True, stop=False), reads=rdd, writes=[RpS[si]], signal=False)
                    op("pe", lambda: nc.tensor.matmul(psS[si][:nr, :nk], lhsT=ql(1), rhs=cfn(1), start=False, stop=False), reads=rdd, writes=[RpS[si]], signal=False)
                    if mask is not None:
                        map_, c0, ncol = mask
                        op("pe", lambda: nc.tensor.matmul(psS[si][:nr, c0:c0 + ncol], lhsT=ident_b[:nr, :nr], rhs=map_, start=False, stop=False),
                           reads=rdd, writes=[RpS[si]], signal=False)
                    op("pe", lambda: nc.tensor.matmul(psS[si][:nr, :nk], lhsT=qrr, rhs=kap, start=False, stop=True), reads=rdd, writes=[RpS[si]])
                    evac(s_all[:nr, k0:k0 + nk], psS[si][:nr, :nk], [RpS[si]], [Rs])
            def softmax_pv(nr, nkeys, vblocks):
                op("dve", lambda: nc.vector.tensor_reduce(out=sm[:nr, 0:1], in_=s_all[:nr, :nkeys], axis=AX.X, op=ALU.max), reads=[Rs], writes=[Rsm])
                op("dve", lambda: nc.vector.tensor_scalar(out=sm[:nr, 1:2], in0=sm[:nr, 0:1], scalar1=-SCALE, scalar2=None, op0=ALU.mult),
                   reads=[Rsm], writes=[Rsm])
                op("act", lambda: nc.scalar.activation(out=p_all[:nr, :nkeys], in_=s_all[:nr, :nkeys], func=AF.Exp, bias=sm[:nr, 1:2], scale=SCALE,
                                                        accum_out=sm[:nr, 3:4]), reads=[Rs, Rsm], writes=[Rp, Rsm])
                op("dve", lambda: nc.vector.reciprocal(out=sm[:nr, 2:3], in_=sm[:nr, 3:4]), reads=[Rsm], writes=[Rsm])
                nvb = len(vblocks)
                for g0 in range(0, nvb, 4):
                    grp = vblocks[g0:g0 + 4]
                    ti = cnt["t"] % 2; cnt["t"] += 1
                    for j, (k0, nkb, kvap, regs) in enumerate(grp):
                        op("pe", lambda: nc.tensor.transpose(out=psPT[ti][:nkb, PTOFF[ti] + j, :nr], in_=p_all[:nr, k0:k0 + nkb], identity=ident_b[:nr, :nr]),
                           reads=[Rp] + CONST, writes=[RpPT[ti]], signal=(j == len(grp) - 1))
                    if len(grp) == 4 and all(g[1] == grp[0][1] for g in grp):
                        nk_ = grp[0][1]
                        evac(pT[ti][:nk_, :, :nr], psPT[ti][:nk_, PTOFF[ti]:PTOFF[ti] + 4, :nr], [RpPT[ti]], [RpT_[ti]])
                    else:
                        for j, (k0, nkb, kvap, regs) in enumerate(grp):
                            evac(pT[ti][:nkb, j, :nr], psPT[ti][:nkb, PTOFF[ti] + j, :nr], [RpPT[ti]], [RpT_[ti]])
                    for j, (k0, nkb, kvap, regs) in enumerate(grp):
                        last = (g0 + j == nvb - 1)
                        op("pe", lambda: nc.tensor.matmul(psO[:nr, :256], lhsT=pT[ti][:nkb, j, :nr], rhs=kvap, start=(g0 + j == 0), stop=last),
                           reads=[RpT_[ti]] + regs, writes=[RpO], signal=(last or j == len(grp) - 1))

            def o_to_T(nr, src=None, Rsrc=None):
                if src is None:
                    src = psO[:nr, :256]; Rsrc = RpO
                op("act", lambda: nc.scalar.activation(out=on[:nr, :], in_=src, func=AF.Copy, scale=sm[:nr, 2:3]),
                   reads=[Rsrc, Rsm], writes=[Ron])
                for rc in range(2):
                    op("pe", lambda: nc.tensor.transpose(out=psB[:, rc, :nr], in_=on[:nr, rc * P:(rc + 1) * P], identity=ident_b[:nr, :nr]),
                       reads=[Ron] + CONST, writes=[RpB], signal=(rc == 1))
                evac(oT[:, :, :nr], psB[:, 0:2, :nr], [RpB], [RoT])

            def mix_attn(nq):
                op("act", lambda: nc.scalar.activation(out=hin[:nq, :512], in_=psW[:nq, :512], func=AF.Square, accum_out=sm[:nq, 4:5]),
                   reads=[RpW], writes=[Rhin, Rsm])
                rstd_of(sm[:, 4:5], nq, sm[:, 5:6], Rsm, 512)
                op("dve", lambda: nc.vector.tensor_scalar(out=an[:nq, :], in0=psW[:nq, :512], scalar1=sm[:nq, 5:6], scalar2=None, op0=ALU.mult),
                   reads=[RpW, Rsm], writes=[Ran])
                for c in range(4):
                    op("pe", lambda: nc.tensor.transpose(out=psB[:, c, :nq], in_=an[:nq, c * P:(c + 1) * P], identity=ident_b[:nq, :nq]),
                       reads=[Ran] + CONST, writes=[RpB], signal=(c == 3))
                for c in range(4):
                    evac(mixT[:, c, :nq], psB[:, c, :nq], [RpB] + CONST, [Rmix], scale=vec_sb[:, V_AG + c:V_AG + c + 1])

            def mix_conv(nq, gp_c0):
                wv = convw_sb[:, :].rearrange("p (c k) -> p c k", k=31)
                CM = CONV_MUL_ENG
                EM = nc.gpsimd if CM == "pool" else nc.vector
                op("dve", lambda: nc.vector.tensor_tensor(out=yc[:, :, :nq], in0=gpad[:, :, gp_c0:gp_c0 + nq],
                                                           in1=wv[:, :, 0:1].to_broadcast([P, 4, nq]), op=ALU.mult),
                   reads=[Rgp] + CONST, writes=[Ryc])
                op("dve", lambda: nc.vector.tensor_tensor(out=yc[:, :, :nq], in0=yc[:, :, :nq],
                                                           in1=vec_sb[:, V_CB:V_CB + 4].unsqueeze(2).to_broadcast([P, 4, nq]), op=ALU.add),
                   reads=[Ryc] + CONST, writes=[Ryc])
                for k in range(1, 31):
                    tb = ctmp[k % 2]; Rtb = Rctmp[k % 2]
                    op(CM, lambda: EM.tensor_tensor(out=tb[:, :, :nq], in0=gpad[:, :, gp_c0 + k:gp_c0 + k + nq],
                                                    in1=wv[:, :, k:k + 1].to_broadcast([P, 4, nq]), op=ALU.mult),
                       reads=[Rgp] + CONST, writes=[Rtb])
                    op("dve", lambda: nc.vector.tensor_tensor(out=yc[:, :, :nq], in0=yc[:, :, :nq], in1=tb[:, :, :nq], op=ALU.add),
                       reads=[Ryc, Rtb], writes=[Ryc])
                op("act", lambda: nc.scalar.activation(out=ybf[:, :, :nq], in_=yc[:, :, :nq], func=AF.Copy), reads=[Ryc], writes=[Rybf])
                for c in range(4):
                    op("pe", lambda: nc.tensor.matmul(psX[:, :nq], lhsT=ones_b[:], rhs=ybf[:, c, :nq], start=(c == 0), stop=(c == 3)),
                       reads=[Rybf] + CONST, writes=[RpX], signal=(c == 3))
                op("dve", lambda: nc.vector.tensor_scalar(out=st3[:, 0, :nq], in0=psX[:, :nq], scalar1=1.0 / 512, scalar2=None, op0=ALU.mult),
                   reads=[RpX], writes=[Rst3])
                op("act", lambda: nc.scalar.activation(out=ybf[:, :, :nq], in_=yc[:, :, :nq], func=AF.Square), reads=[Ryc, RpX], writes=[Rybf])
                for c in range(4):
                    op("pe", lambda: nc.tensor.matmul(psX[:, :nq], lhsT=ones_b[:], rhs=ybf[:, c, :nq], start=(c == 0), stop=(c == 3)),
                       reads=[Rybf] + CONST, writes=[RpX], signal=(c == 3))
                op("dve", lambda: nc.vector.tensor_tensor(out=st3[:, 1, :nq], in0=st3[:, 0, :nq], in1=st3[:, 0, :nq], op=ALU.mult), reads=[Rst3], writes=[Rst3])
                op("dve", lambda: nc.vector.scalar_tensor_tensor(out=st3[:, 1, :nq], in0=psX[:, :nq], scalar=1.0 / 512, in1=st3[:, 1, :nq],
                                                                  op0=ALU.mult, op1=ALU.subtract), reads=[RpX, Rst3], writes=[Rst3])
                op("dve", lambda: nc.vector.tensor_scalar(out=st3[:, 1, :nq], in0=st3[:, 1, :nq], scalar1=EPS, scalar2=None, op0=ALU.add),
                   reads=[Rst3], writes=[Rst3])
                op("act", lambda: nc.scalar.activation(out=st3[:, 1, :nq], in_=st3[:, 1, :nq], func=AF.Sqrt), reads=[Rst3], writes=[Rst3])
                op("dve", lambda: nc.vector.reciprocal(out=st3[:, 1, :nq], in_=st3[:, 1, :nq]), reads=[Rst3], writes=[Rst3])
                for c in range(4):
                    op("dve", lambda: nc.vector.tensor_tensor(out=yc[:, c, :nq], in0=yc[:, c, :nq], in1=st3[:, 0, :nq], op=ALU.subtract),
                       reads=[Ryc, Rst3], writes=[Ryc])
                    op("dve", lambda: nc.vector.tensor_tensor(out=yc[:, c, :nq], in0=yc[:, c, :nq], in1=st3[:, 1, :nq], op=ALU.mult),
                       reads=[Ryc, Rst3], writes=[Ryc])
                    op("act", lambda: nc.scalar.activation(out=yc[:, c, :nq], in_=yc[:, c, :nq], func=AF.Silu,
                                                            bias=vec_sb[:, V_LB + c:V_LB + c + 1], scale=vec_sb[:, V_LG + c:V_LG + c + 1]),
                       reads=[Ryc] + CONST, writes=[Ryc])
                op("act", lambda: nc.scalar.activation(out=ybf[:, :, :nq], in_=yc[:, :, :nq], func=AF.Square), reads=[Ryc, RpX], writes=[Rybf])
                for c in range(4):
                    op("pe", lambda: nc.tensor.matmul(psX[:, :nq], lhsT=ones_b[:], rhs=ybf[:, c, :nq], start=(c == 0), stop=(c == 3)),
                       reads=[Rybf] + CONST, writes=[RpX], signal=(c == 3))
                op("dve", lambda: nc.vector.tensor_scalar(out=st3[:, 2, :nq], in0=psX[:, :nq], scalar1=1.0 / 512, scalar2=EPS, op0=ALU.mult, op1=ALU.add),
                   reads=[RpX], writes=[Rst3])
                op("act", lambda: nc.scalar.activation(out=st3[:, 2, :nq], in_=st3[:, 2, :nq], func=AF.Sqrt), reads=[Rst3], writes=[Rst3])
                op("dve", lambda: nc.vector.reciprocal(out=st3[:, 2, :nq], in_=st3[:, 2, :nq]), reads=[Rst3], writes=[Rst3])
                for c in range(4):
                    op("dve", lambda: nc.vector.scalar_tensor_tensor(out=mixT[:, 4 + c, :nq], in0=yc[:, c, :nq], scalar=vec_sb[:, V_CG + c:V_CG + c + 1],
                                                                      in1=st3[:, 2, :nq], op0=ALU.mult, op1=ALU.mult),
                       reads=[Ryc, Rst3] + CONST, writes=[Rmix])

            def mix_proj(nq, gt):
                load(hin[:nq, :], h_d[gt:gt + nq, :], Rhin, src_regs=[dr(("h", gt))])
                for half in range(2):
                    for kc in range(8):
                        op("pe", lambda: nc.tensor.matmul(psX[:nq, :512], lhsT=mixT[:, kc, :nq], rhs=wout_sb[:, kc, half * 512:(half + 1) * 512],
                                                           start=(kc == 0), stop=(kc == 7)), reads=[Rmix] + MW, writes=[RpX], signal=(kc == 7))
                    op("dve", lambda: nc.vector.tensor_tensor(out=hin[:nq, half * 512:(half + 1) * 512], in0=hin[:nq, half * 512:(half + 1) * 512],
                                                               in1=psX[:nq, :512], op=ALU.add), reads=[Rhin, RpX], writes=[Rhin])
                store(h_d[gt:gt + nq, :], hin[:nq, :], Rhin, dst_regs=[dr(("h", gt))])

            if stage == 2:
                Rh0 = sy.dreg()
                op("sp", lambda: nc.sync.dma_start(out=h_d[NSQ * S:NSQ * S + cfg.nstok, :], in_=hs_in[:, :]),
                   writes=[dr(("h", NSQ * S + t)) for t in range(0, cfg.nstok, 4)], dma=Rh0)
                sego2 = sb(st, "sego2", [32, NCR, 256]); segml = sb(st, "segml", [32, NCR, 2]); Rs2 = sy.dreg()
                segm2 = sb(st, "segm2", [32, NCR]); segl2 = sb(st, "segl2", [32, NCR]); segw2 = sb(st, "segw2", [32, NCR]); Rw2 = Reg()
                osum = sb(st, "osum", [32, 256]); Rosum = Reg()
                cmf2 = sb(st, "cmf2", [P, 16]); Rcm2 = sy.dreg()
                for b in range(NSM):
                    gt = NSQ * S + b * 4
                    r0 = b * NCR * 32
                    load(sego2[:, :, :], pin_o[r0:r0 + NCR * 32, :].rearrange("(c r) f -> r c f", r=32), Rs2)
                    load(segml[:, :, :], pin_ml[r0:r0 + NCR * 32, :].rearrange("(c r) t -> r c t", r=32), Rs2)
                    op("dve", lambda: nc.vector.tensor_copy(out=segm2[:, :], in_=segml[:, :, 0]), reads=[Rs2], writes=[Rw2])
                    op("dve", lambda: nc.vector.tensor_copy(out=segl2[:, :], in_=segml[:, :, 1]), reads=[Rs2], writes=[Rw2])
                    op("dve", lambda: nc.vector.tensor_reduce(out=sm[:32, 0:1], in_=segm2[:, :], axis=AX.X, op=ALU.max), reads=[Rw2], writes=[Rsm])
                    op("dve", lambda: nc.vector.tensor_scalar(out=sm[:32, 1:2], in0=sm[:32, 0:1], scalar1=-SCALE, scalar2=None, op0=ALU.mult),
                       reads=[Rsm], writes=[Rsm])
                    op("act", lambda: nc.scalar.activation(out=segw2[:, :], in_=segm2[:, :], func=AF.Exp, bias=sm[:32, 1:2], scale=SCALE),
                       reads=[Rw2, Rsm], writes=[Rw2])
                    op("dve", lambda: nc.vector.tensor_tensor(out=segl2[:, :], in0=segl2[:, :], in1=segw2[:, :], op=ALU.mult), reads=[Rw2], writes=[Rw2])
                    op("dve", lambda: nc.vector.tensor_reduce(out=sm[:32, 3:4], in_=segl2[:, :], axis=AX.X, op=ALU.add), reads=[Rw2], writes=[Rsm])
                    op("dve", lambda: nc.vector.reciprocal(out=sm[:32, 2:3], in_=sm[:32, 3:4]), reads=[Rsm], writes=[Rsm])
                    op("dve", lambda: nc.vector.tensor_scalar(out=osum[:, :], in0=sego2[:, 0, :], scalar1=segw2[:, 0:1], scalar2=None, op0=ALU.mult),
                       reads=[Rs2, Rw2], writes=[Rosum])
                    for c_ in range(1, NCR):
                        op("dve", lambda: nc.vector.scalar_tensor_tensor(out=osum[:, :], in0=sego2[:, c_, :], scalar=segw2[:, c_:c_ + 1],
                                                                          in1=osum[:, :], op0=ALU.mult, op1=ALU.add), reads=[Rs2, Rw2, Rosum], writes=[Rosum])
                    o_to_T(32, src=osum[:32, :], Rsrc=Rosum)
                    for h in range(8):
                        for rc in range(2):
                            op("pe", lambda: nc.tensor.matmul(psW[:4, h * 64:(h + 1) * 64], lhsT=oT[:, rc, h * 4:(h + 1) * 4], rhs=wuv_sb[:, rc, h * 64:(h + 1) * 64],
                                                               start=(rc == 0), stop=(rc == 1)), reads=[RoT] + MW, writes=[RpW], signal=(h == 7 and rc == 1))
                    mix_attn(4)
                    load(cmf2[:, :], cm_in[b * P:(b + 1) * P, :], Rcm2)
                    op("dve", lambda: nc.vector.tensor_copy(out=mixT[:, 4:8, :4], in_=cmf2[:, :].rearrange("p (c q) -> p c q", c=4)), reads=[Rcm2], writes=[Rmix])
                    mix_proj(4, gt)
                sy.barrier()
                pst.close()
                return
            for s in range(NSQ):
                c0 = s * (S + 30)
                op("sp", lambda: nc.sync.dma_start(out=glu_d.ap().rearrange("(c p) t -> p c t", p=P)[:, :, c0:c0 + 30],
                                                   in_=zero_sb[:, 0:30].unsqueeze(1).to_broadcast([P, 4, 30])),
                   reads=[R_z], writes=[dr(("glu", s))], dma=R_m)
            gview = glu_d.ap().rearrange("(c p) t -> p c t", p=P)
            def mix_out(nq, gt, a_reads, gp_c0):
                mix_attn(nq); mix_conv(nq, gp_c0); mix_proj(nq, gt)

            for s in range(NSQ):
                g0s = s * S
                load(cT[:], cT_d.ap().rearrange("(rc p) t -> p rc t", p=P)[:, :, g0s:g0s + S], RcT,
                     src_regs=[dr(("cT", g0s + t)) for t in range(0, S, 512)])
                load(krT[:], krT_d[:, g0s:g0s + S], RkrT, src_regs=[dr(("krT", g0s + t)) for t in range(0, S, 512)])
                nfull = S // P
                kvregs = [dr(("kv", g0s + t)) for t in range(0, S, P)]
                if nfull:
                    load(kvb[:, :nfull, :], ckvb_d[g0s:g0s + nfull * P, :].rearrange("(b p) r -> p b r", p=P), Rkvb, src_regs=kvregs)
                if S % P:
                    load(kvb[:S % P, nfull, :], ckvb_d[g0s + nfull * P:g0s + S, :], Rkvb, src_regs=kvregs)
                with nc.allow_non_contiguous_dma(reason="conv state transpose"):
                    for c in range(4):
                        cc = s * (S + 30) + 30 + S - 30
                        op("sp", lambda: nc.sync.dma_start(out=convp[s * 30:(s + 1) * 30, c * P:(c + 1) * P],
                                                           in_=glu_d[c * P:(c + 1) * P, cc:cc + 30].rearrange("c t -> t c")),
                           reads=[dr(("glu", s))], writes=[], dma=R_m)
                        out_events.append((R_m.dsem, sy.dcnt[id(R_m.dsem)]))
                for i in range(NB):
                    nq = min(P, S - i * P); gt = g0s + i * P
                    load(qcT[:, :, :nq], qcT_d.ap().rearrange("(rc p) t -> p rc t", p=P)[:, :, gt:gt + nq], RqcT,
                         src_regs=[dr(("qc", g0s + (i * P // 512) * 512))])
                    q_proj(nq, i, False)
                    gc0 = s * (S + 30) + i * P
                    load(gpad[:, :, :30 + nq], gview[:, :, gc0:gc0 + 30 + nq], Rgp, src_regs=[dr(("glu", s))])
                    nkeys = i * P + nq
                    for h in range(8):
                        chunks = []
                        for k0 in range(0, nkeys, 512):
                            nk = min(512, nkeys - k0)
                            mask = None
                            if k0 + nk == nkeys:
                                mask = (maskc[:nq, :nq], i * P - k0, nq)
                            chunks.append((k0, nk, (lambda rc, k0=k0, nk=nk: cT[:, rc, k0:k0 + nk]), krT[:, k0:k0 + nk], [RcT, RkrT], mask))
                        vbl = [(b * P, min(P, nkeys - b * P), kvb[:min(P, nkeys - b * P), b, :], [Rkvb]) for b in range(i + 1)]
                        attn_rows(nq, (lambda rc, h=h: qlat[:, h, rc, :nq]), qr[:, h, :nq], chunks, vbl, [Rql, Rqr])
                        o_to_T(nq)
                        for rc in range(2):
                            op("pe", lambda: nc.tensor.matmul(psW[:nq, h * 64:(h + 1) * 64], lhsT=oT[:, rc, :nq], rhs=wuv_sb[:, rc, h * 64:(h + 1) * 64],
                                                               start=(rc == 0), stop=(rc == 1)), reads=[RoT] + MW, writes=[RpW], signal=(rc == 1))
                    mix_out(nq, gt, [], 0)
            cmf = sb(st, "cmf", [P, 16]); Rcmf = sy.dreg()
            for b in range(NSM):
                load(prev[:30, :], stconv[b * 30:(b + 1) * 30, :], Rprev)
                for c in range(4):
                    op("pe", lambda: nc.tensor.matmul(psX[:, c * 32:c * 32 + 30], lhsT=prev[:30, c * P:(c + 1) * P], rhs=ident_f[:30, :30],
                                                       start=True, stop=True), reads=[Rprev] + CONST, writes=[RpX], signal=(c == 3))
                gs0 = NSQ * (S + 30) + b * 34
                load(gpad[:, :, 30:34], gview[:, :, gs0 + 30:gs0 + 34], Rgp, src_regs=[dr(("glus", b))])
                op("dve", lambda: nc.vector.tensor_copy(out=gpad[:, :, 0:30], in_=psX[:, 0:128].rearrange("p (c t) -> p c t", c=4)[:, :, 0:30]),
                   reads=[RpX], writes=[Rgp])
                op("sp", lambda: nc.sync.dma_start(out=convs[b * 30:b * 30 + 26, :], in_=stconv[b * 30 + 4:(b + 1) * 30, :]), writes=[], dma=R_m)
                out_events.append((R_m.dsem, sy.dcnt[id(R_m.dsem)]))
                with nc.allow_non_contiguous_dma(reason="conv state transpose"):
                    for c in range(4):
                        op("sp", lambda: nc.sync.dma_start(out=convs[b * 30 + 26:(b + 1) * 30, c * P:(c + 1) * P],
                                                           in_=glu_d[c * P:(c + 1) * P, gs0 + 30:gs0 + 34].rearrange("c t -> t c")),
                           reads=[dr(("glus", b))], writes=[], dma=R_m)
                        out_events.append((R_m.dsem, sy.dcnt[id(R_m.dsem)]))
                mix_conv(4, 0)
                op("dve", lambda: nc.vector.tensor_copy(out=cmf[:, :].rearrange("p (c q) -> p c q", c=4), in_=mixT[:, 4:8, :4]), reads=[Rmix], writes=[Rcmf])
                store(cm_o[b * P:(b + 1) * P, :], cmf[:, :], Rcmf, final=True)
            sy.barrier(); pst.close(); pst = ExitStack()
            psSS = ps(pst, "psSS", [P, 1024]); psS = [psSS[:, 0:512], psSS[:, 512:1024]]; RpS = [Reg(), Reg()]
            psA_ = psSS; RpA = Reg(subs=RpS)
            psO = ps(pst, "psO2", [P, 512]); RpO = Reg()
            psX = psO; RpX = RpO
            psKV = [ps(pst, "psKV%d" % i, [P, 8, P], BF16) for i in range(2)]; RpKV = [Reg(), Reg()]
            psKR = [ps(pst, "psKR%d" % i, [P, 8, P], BF16) for i in range(2)]; RpKR = [Reg(), Reg()]
            psB = ps(pst, "psP", [P, 8, P], BF16); RpB = Reg(); RpB2 = [RpB, RpB]
            psPT = psKV; RpPT = RpKV; PTOFF = [0, 0]
            op("dve", lambda: nc.vector.memset(qrs[32:33, :], 1.0), writes=[Rqls])
            for i_ in range(3):
                op("dve", lambda: nc.vector.memset(kvf[i_][:], 0.0), writes=[Rkvf[i_]])
            DPS = GP // NSEG
            bc_reg = nc.gpsimd.to_reg(cfg.npool_sh * GP - 1)
            for b in range(NSALL):
                gt = NSQ * S + cfg.nstok + b * 4
                load(qcT[:, :, :4], qcT_d.ap().rearrange("(rc p) t -> p rc t", p=P)[:, :, gt:gt + 4], RqcT,
                     src_regs=[dr(("qc", NSQ * S + cfg.nstok + (b * 4 // 512) * 512))])
                q_proj(4, NB, True)
                for rc in range(2):
                    evac(qls[:, rc, :].rearrange("p (h q) -> p h q", h=8), qlat[:, :, rc, :4], [Rql], [Rqls])
                evac(qrs[0:32, :].rearrange("p (h q) -> p h q", h=8), qr[:, :, :4], [Rqr], [Rqls])
                ql_s = (lambda rc: qls[:, rc, :]); qr_s = qrs[:, :]
                for sg_ in range(NSEG):
                    kvs = kvbS[sg_ % 2]; Rkvs = RkvbS[sg_ % 2]
                    vbl = []; k0 = 0; pend = None
                    for dd in range(DPS):
                        kk = sg_ * DPS + dd
                        ci = cnt["c"] % 3; cnt["c"] += 1
                        col = kk * NSALL + b
                        op("pool", lambda: nc.gpsimd.indirect_dma_start(out=kvf[ci][:NPG, :], out_offset=None, in_=poolx[:, :],
                                                                        in_offset=bass.IndirectOffsetOnAxis(ap=idxk[:NPG, col:col + 1], axis=0),
                                                                        bounds_check=bc_reg, oob_is_err=False),
                           reads=[R_idx], writes=[Rkvf[ci]], dma=Rkvf[ci])
                        kvv = kvf[ci][:NPG, :].rearrange("p (t f) -> p t f", f=288)
                        kr8 = krb8[dd % 2]; Rkr8 = Rkrb8[dd % 2]
                        op("dve", lambda: nc.vector.tensor_copy(out=kvs[:NPG, dd * TPD:(dd + 1) * TPD, :], in_=kvv[:, :, 0:256]),
                           reads=[Rkvf[ci]], writes=[Rkvs])
                        op("act", lambda: nc.scalar.activation(out=kr8[:NPG, :, 0:32], in_=kvv[:, :, 256:288], func=AF.Copy),
                           reads=[Rkvf[ci]], writes=[Rkr8])
                        op("dve", lambda: nc.vector.tensor_copy(out=kr8[:NPG, :, 32:33], in_=biasb[:NPG, b:b + 1].unsqueeze(1).to_broadcast([NPG, TPD, 1])),
                           reads=[R_idx], writes=[Rkr8])
                        for hf in range(TPD // 4):
                            c2 = cnt["k"] % 2; cnt["k"] += 1
                            for j in range(4):
                                blk = dd * TPD + hf * 4 + j
                                for rc in range(2):
                                    op("pe", lambda: nc.tensor.transpose(out=psKV[c2][:, rc * 4 + j, :NPG], in_=kvs[:NPG, blk, rc * P:(rc + 1) * P],
                                                                         identity=ident_b[:NPG, :NPG]),
                                       reads=[Rkvs] + CONST, writes=[RpKV[c2]], signal=(j == 3 and rc == 1))
                            evac(cTc[c2][:, :, :4 * NPG].rearrange("p r (j t) -> p r j t", t=NPG),
                                 psKV[c2][:, :, :NPG].rearrange("p (r j) t -> p r j t", r=2), [RpKV[c2]], [RcTc[c2]])
                            for j in range(4):
                                op("pe", lambda: nc.tensor.transpose(out=psKR[c2][:33, j, :NPG], in_=kr8[:NPG, hf * 4 + j, :], identity=ident_b[:NPG, :NPG]),
                                   reads=[Rkr8] + CONST, writes=[RpKR[c2]], signal=(j == 3))
                            evac(krTc[c2][:, :4 * NPG].rearrange("p (j t) -> p j t", t=NPG), psKR[c2][:33, 0:4, :NPG], [RpKR[c2]], [RkrTc[c2]])
                            if pend is not None:
                                score_chunk(32, ql_s, qr_s, pend, [Rqls])
                            pend = (k0, 4 * NPG, (lambda rc, c2=c2: cTc[c2][:, rc, :4 * NPG]), krTc[c2][:, :4 * NPG], [RcTc[c2], RkrTc[c2]], None)
                            for j in range(4):
                                vbl.append((k0 + j * NPG, NPG, kvs[:NPG, dd * TPD + hf * 4 + j, :], [Rkvs]))
                            k0 += 4 * NPG
                    score_chunk(32, ql_s, qr_s, pend, [Rqls])
                    if sg_ == NSEG - 1:
                        load(cT[:, :, :4], cT_d.ap().rearrange("(rc p) t -> p rc t", p=P)[:, :, gt:gt + 4], RcT,
                             src_regs=[dr(("cT", NSQ * S + cfg.nstok + (b * 4 // 512) * 512))])
                        load(krTn[0:32, :4], krT_d[:, gt:gt + 4], RkrT, src_regs=[dr(("krT", NSQ * S + cfg.nstok + (b * 4 // 512) * 512))])
                        load(ownf[0:1, :], ownb[b:b + 1, :], Rown)
                        op("dve", lambda: nc.vector.tensor_copy(out=ownr[0:1, :], in_=ownf[0:1, :]), reads=[Rown], writes=[Rownr])
                        op("sp", lambda: nc.sync.dma_start(out=krTn[32:33, :4], in_=ownr[0:1, :]), reads=[Rownr], writes=[RkrT], dma=RkrT)
                        load(kvs[:4, DPS * TPD, :], ckvb_d[gt:gt + 4, :], Rkvs, src_regs=[dr(("kv", NSQ * S + cfg.nstok + (b * 4 // P) * P))])
                        score_chunk(32, ql_s, qr_s, (k0, 4, (lambda rc: cT[:, rc, :4]), krTn[:, :4], [RcT, RkrT], (masks[:32, :4], 0, 4)), [Rqls])
                        vbl.append((k0, 4, kvs[:4, DPS * TPD, :], [Rkvs]))
                        k0 += 4
                    softmax_pv(32, k0, vbl)
                    op("dve", lambda: nc.vector.tensor_copy(out=segm[:32, sg_:sg_ + 1], in_=sm[:32, 0:1]), reads=[Rsm], writes=[Rseg])
                    op("dve", lambda: nc.vector.tensor_copy(out=segl[:32, sg_:sg_ + 1], in_=sm[:32, 3:4]), reads=[Rsm], writes=[Rseg])
                    op("act", lambda: nc.scalar.activation(out=sego[:32, sg_, :], in_=psO[:32, :256], func=AF.Copy), reads=[RpO], writes=[Rseg])
                op("dve", lambda: nc.vector.tensor_reduce(out=pml[:32, 0:1], in_=segm[:32, :NSEG], axis=AX.X, op=ALU.max), reads=[Rseg], writes=[Rpml])
                op("dve", lambda: nc.vector.tensor_scalar(out=sm[:32, 1:2], in0=pml[:32, 0:1], scalar1=-SCALE, scalar2=None, op0=ALU.mult), reads=[Rpml], writes=[Rsm])
                op("act", lambda: nc.scalar.activation(out=segw[:32, :NSEG], in_=segm[:32, :NSEG], func=AF.Exp, bias=sm[:32, 1:2], scale=SCALE),
                   reads=[Rseg, Rsm], writes=[Rseg])
                op("dve", lambda: nc.vector.tensor_tensor(out=segl[:32, :NSEG], in0=segl[:32, :NSEG], in1=segw[:32, :NSEG], op=ALU.mult), reads=[Rseg], writes=[Rseg])
                op("dve", lambda: nc.vector.tensor_reduce(out=pml[:32, 1:2], in_=segl[:32, :NSEG], axis=AX.X, op=ALU.add), reads=[Rseg], writes=[Rpml])
                op("dve", lambda: nc.vector.tensor_scalar(out=po[:32, :], in0=sego[:32, 0, :], scalar1=segw[:32, 0:1], scalar2=None, op0=ALU.mult),
                   reads=[Rseg], writes=[Rpo])
                for sg_ in range(1, NSEG):
                    op("dve", lambda: nc.vector.scalar_tensor_tensor(out=po[:32, :], in0=sego[:32, sg_, :], scalar=segw[:32, sg_:sg_ + 1],
                                                                      in1=po[:32, :], op0=ALU.mult, op1=ALU.add), reads=[Rseg, Rpo], writes=[Rpo])
                store(part_o[b * 32:(b + 1) * 32, :], po[:32, :], Rpo, final=True)
                store(part_ml[b * 32:(b + 1) * 32, :], pml[:32, 0:2], Rpml, final=True)
            sy.barrier()
            pst.close()

    mixer_phase()
    ffn_phase("2")
    sy._wait("sp", out_events)
    sy.barrier()
    return nc, es


_BUILD_CACHE = {}


def _common_consts(cfg, inp, c):
    f32 = np.float32
    m = {}
    vecs = np.zeros((P, 64), f32)

    def fm(v, n):
        return np.asarray(v, f32).reshape(n, P).T
    vecs[:, 0:8] = fm(inp["ffn1_norm"][0], 8); vecs[:, 8:16] = fm(inp["mix_norm"][0], 8); vecs[:, 16:24] = fm(inp["ffn2_norm"][0], 8)
    vecs[:, 24:28] = fm(inp["attn_grp_norm"][0], 4); vecs[:, 28:32] = fm(inp["conv_grp_norm"][0], 4)
    vecs[:, 32:36] = fm(inp["conv_b"][0], 4); vecs[:, 36:40] = fm(inp["conv_ln_g"][0], 4); vecs[:, 40:44] = fm(inp["conv_ln_b"][0], 4)
    vecs[:, 45] = np.float32(c * cfg.npool_sh)
    m["vecs"] = vecs
    m["bvec"] = np.concatenate([np.asarray(inp["q_norm"][0], f32), np.asarray(inp["kv_norm"][0], f32),
                                np.asarray(inp["final_norm"], f32)]).reshape(1, 1536)
    m["identf"] = np.eye(P, dtype=f32)
    return m


def _host_layout1(cfg, inp, core):
    D, S, NB = cfg.D, cfg.S, cfg.NB
    f32 = np.float32
    c = core
    m = _common_consts(cfg, inp, c)
    m["xp"] = np.ascontiguousarray(inp["x_prompt"][c * cfg.nseq:(c + 1) * cfg.nseq]).reshape(cfg.nseq * cfg.seq, D)
    m["xs"] = np.ascontiguousarray(inp["x_sample"][c * cfg.nsamp:(c + 1) * cfg.nsamp]).reshape(cfg.nstok, D)
    m["xsa"] = np.ascontiguousarray(inp["x_sample"]).reshape(cfg.nsall * 4, D)
    pk = inp["cache_kv_latent"][0][c * cfg.npool_sh:(c + 1) * cfg.npool_sh]
    pr = inp["cache_k_rope"][0][c * cfg.npool_sh:(c + 1) * cfg.npool_sh]
    m["poolx"] = np.concatenate([pk, pr], axis=-1).reshape(cfg.npool_sh * cfg.gp, cfg.tpd * 288)
    m["stconv"] = np.ascontiguousarray(inp["state_conv"][0, c * cfg.nsamp:(c + 1) * cfg.nsamp]).reshape(cfg.nsamp * 30, 512)
    m["ptabT"] = np.ascontiguousarray(inp["page_table"].T).astype(np.int32)
    own = np.full((cfg.nsall, 4), -30000.0, f32); own[c * cfg.nsamp:(c + 1) * cfg.nsamp] = 0.0
    m["ownb"] = own
    m["meta"] = np.ascontiguousarray(inp["meta_tokens"]).astype(f32)
    for f in ("1", "2"):
        m["wg" + f] = np.ascontiguousarray(inp["ffn%s_w_gate" % f][0]); m["wu" + f] = np.ascontiguousarray(inp["ffn%s_w_up" % f][0])
        m["wd" + f] = np.ascontiguousarray(inp["ffn%s_w_down" % f][0])
    m["win"] = np.ascontiguousarray(inp["w_in"][0]); m["wuq"] = np.ascontiguousarray(inp["w_uq"][0])
    m["wuk"] = np.ascontiguousarray(inp["w_uk"][0]).reshape(256, 512); m["wuv"] = np.ascontiguousarray(inp["w_uv"][0]).reshape(256, 512)
    m["wout"] = np.ascontiguousarray(inp["w_out"][0])
    cw = np.asarray(inp["conv_w"][0], f32)
    m["convw"] = np.ascontiguousarray(cw.T.reshape(4, P, 31).transpose(1, 0, 2)).reshape(P, 4 * 31)
    qi = np.arange(P)[:, None]; ki = np.arange(P)[None, :]
    m["maskc"] = np.where(ki <= qi, 0.0, -30000.0).astype(f32)
    r = np.arange(32)[:, None] % 4; k4 = np.arange(4)[None, :]
    m["masks"] = np.where(k4 <= r, 0.0, -30000.0).astype(f32)
    half = 16
    inv = (np.float32(10000.0) ** (-np.arange(half, dtype=f32) / np.float32(half))).astype(f32)
    tab = np.zeros((P, NB + 1, 32), f32)
    pos = (np.arange(NB * P).reshape(NB, P).T).astype(f32)
    ang = pos[:, :, None] * inv[None, None, :]
    tab[:, :NB, :16] = np.cos(ang); tab[:, :NB, 16:] = np.sin(ang)
    sp = (cfg.past + (np.arange(P) % 4)).astype(f32)
    angs = sp[:, None] * inv[None, :]
    tab[:, NB, :16] = np.cos(angs); tab[:, NB, 16:] = np.sin(angs)
    m["ropet"] = tab.reshape(P, (NB + 1) * 32)
    return m


def _host_layout2(cfg, inp, core, rs1):
    c = core
    m = _common_consts(cfg, inp, c)
    m["wg2"] = np.ascontiguousarray(inp["ffn2_w_gate"][0]); m["wu2"] = np.ascontiguousarray(inp["ffn2_w_up"][0])
    m["wd2"] = np.ascontiguousarray(inp["ffn2_w_down"][0])
    m["wuv"] = np.ascontiguousarray(inp["w_uv"][0]).reshape(256, 512); m["wout"] = np.ascontiguousarray(inp["w_out"][0])
    b0 = c * cfg.nsamp
    po = np.stack([np.asarray(r["part_o"]).reshape(cfg.nsall, 32, 256)[b0:b0 + cfg.nsamp] for r in rs1], axis=1)
    pm = np.stack([np.asarray(r["part_ml"]).reshape(cfg.nsall, 32, 2)[b0:b0 + cfg.nsamp] for r in rs1], axis=1)
    m["pin_o"] = np.ascontiguousarray(po).reshape(cfg.nsamp * cfg.ncores * 32, 256)
    m["pin_ml"] = np.ascontiguousarray(pm).reshape(cfg.nsamp * cfg.ncores * 32, 2)
    m["hs_in"] = np.ascontiguousarray(rs1[c]["hs_o"]); m["cm_in"] = np.ascontiguousarray(rs1[c]["cm_o"])
    return m


def kernel(**inp):
    cfg = Cfg()
    inp = {k: np.asarray(v) for k, v in inp.items()}
    if "nc1" not in _BUILD_CACHE:
        _BUILD_CACHE["nc1"] = build(cfg, 1)
        _BUILD_CACHE["nc2"] = build(cfg, 2)
    nc1, _e1 = _BUILD_CACHE["nc1"]; nc2, _e2 = _BUILD_CACHE["nc2"]
    cores = list(range(cfg.ncores))
    res1 = run_bass_kernel_spmd(nc1, [_host_layout1(cfg, inp, c) for c in cores], core_ids=cores).results
    res2 = run_bass_kernel_spmd(nc2, [_host_layout2(cfg, inp, c, res1) for c in cores], core_ids=cores).results
    return assemble(cfg, res1, res2)


def assemble(cfg, rs, rs2):
    S = cfg.S; NSQ = cfg.nseq
    yp = np.concatenate([r["yp"].reshape(NSQ, cfg.seq, cfg.D) for r in rs], 0)
    ys = np.concatenate([r["ys"].reshape(cfg.nsamp, 4, cfg.D) for r in rs2], 0)
    ckv = [np.asarray(r["ckv_o"]) for r in rs]; kr = [np.asarray(r["kr_o"]) for r in rs]
    ckp = np.concatenate([a[:NSQ * S].reshape(NSQ, S, 256) for a in ckv], 0)[None]
    krp = np.concatenate([a[:NSQ * S].reshape(NSQ, S, 32) for a in kr], 0)[None]
    cks = np.concatenate([a[NSQ * S:].reshape(cfg.nsamp, 4, 256) for a in ckv], 0)[None]
    krs = np.concatenate([a[NSQ * S:].reshape(cfg.nsamp, 4, 32) for a in kr], 0)[None]
    cvp = np.concatenate([r["convp"].reshape(NSQ, 30, 512) for r in rs], 0)[None]
    cvs = np.concatenate([r["convs"].reshape(cfg.nsamp, 30, 512) for r in rs], 0)[None]
    f = lambda a: np.ascontiguousarray(a, dtype=np.float32)
    return (f(yp), f(ys), f(ckp), f(krp), f(cvp), f(cks), f(krs), f(cvs))
```
